# Optimizing a Trainium2 kernel written in Bass

```python
import math
import jax, jax.numpy as jnp
from jax import lax
import numpy as np

D_MODEL = 1024
BATCH = 4
SEQ = 8192
DEPTH = 2

MLSTM_HEADS = 4
MLSTM_HEAD_DIM = 256
MLSTM_WIDTH = MLSTM_HEADS * MLSTM_HEAD_DIM
MLSTM_CONV = 4
MLSTM_CHUNK = 64
MOBA_HEADS = 8
MOBA_HEAD_DIM = 128
MOBA_WIDTH = MOBA_HEADS * MOBA_HEAD_DIM
MOBA_BLOCK = 256
MOBA_TOPK = 3
MOBA_Q_CHUNK = 32
NUM_BUCKETS = 32
REL_MAX_DISTANCE = 128
D_FF = 2816
FFN_CONV = 3
ALPHA = (2 * DEPTH) ** 0.25
BETA = (8 * DEPTH) ** -0.25
LN_EPS = 1e-5
SPLIT_SIZES = (MLSTM_WIDTH, MLSTM_WIDTH, MLSTM_WIDTH, MLSTM_WIDTH,
               MLSTM_HEADS, MLSTM_HEADS,
               MOBA_WIDTH, MOBA_WIDTH, MOBA_WIDTH,
               D_MODEL, D_MODEL)
N_IN = 4 * MLSTM_WIDTH + 2 * MLSTM_HEADS + 3 * MOBA_WIDTH + 2 * D_MODEL
MLSTM_V_START = 2 * MLSTM_WIDTH
MLSTM_F_START = 4 * MLSTM_WIDTH + MLSTM_HEADS
MOBA_V_START = 4 * MLSTM_WIDTH + 2 * MLSTM_HEADS + 2 * MOBA_WIDTH

kernel_name = "hybrid_mlstm_moba_convffn_deepnorm"


def _split_cols(t, sizes):
    idx, acc = [], 0
    for s in sizes[:-1]:
        acc += s
        idx.append(acc)
    return jnp.split(t, idx, axis=-1)


def layer_norm(x, g, b):
    xf = x.astype(jnp.float32)
    mu = xf.mean(-1, keepdims=True)
    var = jnp.square(xf - mu).mean(-1, keepdims=True)
    return ((xf - mu) * lax.rsqrt(var + LN_EPS) * g + b).astype(x.dtype)


def causal_dwconv(x, w):
    k, c = w.shape
    return lax.conv_general_dilated(
        x, w[:, None, :], window_strides=(1,), padding=((k - 1, 0),),
        dimension_numbers=("NWC", "WIO", "NWC"), feature_group_count=c)


def to_heads(t, h, d):
    b, s, _ = t.shape
    return t.reshape(b, s, h, d).transpose(0, 2, 1, 3)


def mlstm_chunkwise(q, k, v, i_pre, f_pre):
    B, H, S, Dh = q.shape
    L = MLSTM_CHUNK
    nc = S // L

    def chunks(t):
        return jnp.moveaxis(t.reshape(B, H, nc, L, *t.shape[3:]), 2, 0)

    qf = q.astype(jnp.float32)
    kf = k.astype(jnp.float32) * (Dh ** -0.5)
    vf = v.astype(jnp.float32)
    log_f = jax.nn.log_sigmoid(f_pre.astype(jnp.float32))
    log_i = i_pre.astype(jnp.float32)
    b_cum = jnp.cumsum(chunks(log_f), axis=-1)
    causal = jnp.tril(jnp.ones((L, L), dtype=bool))

    def step(carry, xs):
        C, n, m = carry
        q_c, k_c, v_c, b_c, li_c = xs
        d_intra = jnp.where(causal, b_c[..., :, None] - b_c[..., None, :] + li_c[..., None, :], -jnp.inf)
        inter = b_c + m[..., None]
        m_q = jnp.maximum(inter, d_intra.max(-1))
        w_inter = jnp.exp(inter - m_q)
        s = jnp.einsum("bhld,bhsd->bhls", q_c, k_c) * jnp.exp(d_intra - m_q[..., None])
        num = w_inter[..., None] * jnp.einsum("bhld,bhde->bhle", q_c, C) + jnp.einsum("bhls,bhse->bhle", s, v_c)
        den = w_inter * jnp.einsum("bhld,bhd->bhl", q_c, n) + s.sum(-1)
        h = num / jnp.maximum(jnp.abs(den), jnp.exp(-m_q))[..., None]
        b_last = b_c[..., -1]
        d_state = b_last[..., None] - b_c + li_c
        m_new = jnp.maximum(b_last + m, d_state.max(-1))
        w_prev = jnp.exp(b_last + m - m_new)
        w_k = jnp.exp(d_state - m_new[..., None])
        C = w_prev[..., None, None] * C + jnp.einsum("bhl,bhld,bhle->bhde", w_k, k_c, v_c)
        n = w_prev[..., None] * n + jnp.einsum("bhl,bhld->bhd", w_k, k_c)
        return (C, n, m_new), h

    init = (jnp.zeros((B, H, Dh, Dh), jnp.float32), jnp.zeros((B, H, Dh), jnp.float32),
            jnp.zeros((B, H), jnp.float32))
    _, h = lax.scan(step, init, (chunks(qf), chunks(kf), chunks(vf), b_cum, chunks(log_i)))
    return jnp.moveaxis(h, 0, 2).reshape(B, H, S, Dh)


def head_norm(h, gain):
    mu = h.mean(-1, keepdims=True)
    var = jnp.square(h - mu).mean(-1, keepdims=True)
    hn = (h - mu) * lax.rsqrt(var + LN_EPS)
    B, H, S, Dh = h.shape
    return hn.transpose(0, 2, 1, 3).reshape(B, S, H * Dh) * gain.astype(jnp.float32)


def t5_bucket(rel):
    n = jnp.maximum(rel, 0)
    max_exact = NUM_BUCKETS // 2
    nf = jnp.maximum(n, max_exact).astype(jnp.float32)
    large = max_exact + (jnp.log(nf / max_exact) / math.log(REL_MAX_DISTANCE / max_exact)
                         * (NUM_BUCKETS - max_exact)).astype(jnp.int32)
    large = jnp.minimum(large, NUM_BUCKETS - 1)
    return jnp.where(n < max_exact, n, large)


def moba_attention(q, k, v, rel_bias):
    B, H, S, Dh = q.shape
    nb = -(-S // MOBA_BLOCK)
    s_pad = nb * MOBA_BLOCK
    pad = ((0, 0), (0, 0), (0, s_pad - S), (0, 0))
    k_blocks = jnp.pad(k, pad).reshape(B, H, nb, MOBA_BLOCK, Dh)
    v_blocks = jnp.pad(v, pad).reshape(B, H, nb, MOBA_BLOCK, Dh)
    k_mean = k_blocks.astype(jnp.float32).mean(axis=3)
    topk = min(MOBA_TOPK, nb)
    scale = Dh ** -0.5
    qc = MOBA_Q_CHUNK
    blk_pos = jnp.arange(MOBA_BLOCK, dtype=jnp.int32)
    b_idx = jnp.arange(B)[:, None, None, None]
    h_idx = jnp.arange(H)[None, :, None, None]
    h_idx5 = jnp.arange(H)[None, :, None, None, None]

    def chunk(c):
        start = c * qc
        q_c = lax.dynamic_slice_in_dim(q, start, qc, axis=2)
        q_pos = start + jnp.arange(qc, dtype=jnp.int32)
        own = start // MOBA_BLOCK
        gate = jnp.einsum("bhqd,bhnd->bhqn", q_c.astype(jnp.float32), k_mean)
        gate = jnp.where(jnp.arange(nb) < own, gate, -jnp.inf)
        _, sel = lax.top_k(gate, topk)
        valid = jnp.arange(topk) < own
        k_sel = k_blocks[b_idx, h_idx, sel]
        v_sel = v_blocks[b_idx, h_idx, sel]
        k_pos = sel[..., None] * MOBA_BLOCK + blk_pos
        bias_sel = rel_bias[t5_bucket(q_pos[:, None, None] - k_pos), h_idx5].astype(jnp.float32)
        logit_sel = jnp.einsum("bhqd,bhqtkd->bhqtk", q_c, k_sel).astype(jnp.float32) * scale + bias_sel
        logit_sel = jnp.where(valid[:, None], logit_sel, -jnp.inf)
        k_own = lax.dynamic_index_in_dim(k_blocks, own, axis=2, keepdims=False)
        v_own = lax.dynamic_index_in_dim(v_blocks, own, axis=2, keepdims=False)
        rel_own = q_pos[:, None] - (own * MOBA_BLOCK + blk_pos)[None, :]
        bias_own = jnp.moveaxis(rel_bias[t5_bucket(rel_own)], -1, 0).astype(jnp.float32)
        logit_own = jnp.einsum("bhqd,bhkd->bhqk", q_c, k_own).astype(jnp.float32) * scale + bias_own
        logit_own = jnp.where(rel_own >= 0, logit_own, -jnp.inf)
        logits = jnp.concatenate([logit_sel.reshape(B, H, qc, topk * MOBA_BLOCK), logit_own], axis=-1)
        p = jax.nn.softmax(logits, axis=-1)
        p_sel = p[..., :topk * MOBA_BLOCK].reshape(B, H, qc, topk, MOBA_BLOCK).astype(v.dtype)
        p_own = p[..., topk * MOBA_BLOCK:].astype(v.dtype)
        return (jnp.einsum("bhqtk,bhqtkd->bhqd", p_sel, v_sel)
                + jnp.einsum("bhqk,bhkd->bhqd", p_own, v_own))

    outs = lax.map(chunk, jnp.arange(S // qc))
    return jnp.moveaxis(outs, 0, 2).reshape(B, H, S, Dh)


def hybrid_mixer(x, w_in, b_in, conv_qk, mlstm_norm, rel_bias, w_branch_a, w_branch_b, w_out):
    B, S, _ = x.shape
    proj = jnp.einsum("bsd,dn->bsn", x, w_in) + b_in
    q_m, k_m, v_m, o_m, i_m, f_m, q_b, k_b, v_b, g_a, g_b = _split_cols(proj, SPLIT_SIZES)
    qk = jax.nn.silu(causal_dwconv(jnp.concatenate([q_m, k_m], axis=-1), conv_qk))
    q_m, k_m = jnp.split(qk, 2, axis=-1)
    h_m = mlstm_chunkwise(to_heads(q_m, MLSTM_HEADS, MLSTM_HEAD_DIM),
                         to_heads(k_m, MLSTM_HEADS, MLSTM_HEAD_DIM),
                         to_heads(v_m, MLSTM_HEADS, MLSTM_HEAD_DIM),
                         i_m.transpose(0, 2, 1), f_m.transpose(0, 2, 1))
    y_a = (jax.nn.sigmoid(o_m.astype(jnp.float32)) * head_norm(h_m, mlstm_norm)).astype(x.dtype)
    h_b = moba_attention(to_heads(q_b, MOBA_HEADS, MOBA_HEAD_DIM),
                         to_heads(k_b, MOBA_HEADS, MOBA_HEAD_DIM),
                         to_heads(v_b, MOBA_HEADS, MOBA_HEAD_DIM), rel_bias)
    y_b = h_b.transpose(0, 2, 1, 3).reshape(B, S, MOBA_WIDTH)
    merged = (jax.nn.sigmoid(g_a) * jnp.einsum("bsw,wd->bsd", y_a, w_branch_a)
              + jax.nn.sigmoid(g_b) * jnp.einsum("bsw,wd->bsd", y_b, w_branch_b))
    return jnp.einsum("bsd,de->bse", merged, w_out)


def conv_gated_mlp(x, w_up, conv_ffn, w_down):
    u = causal_dwconv(jnp.einsum("bsd,df->bsf", x, w_up), conv_ffn)
    g, up = jnp.split(u, 2, axis=-1)
    return jnp.einsum("bsf,fd->bsd", jax.nn.silu(g) * up, w_down)


def setup_inputs(seed: int = 0) -> dict:
    key = jax.random.key(seed)
    ks = jax.random.split(key, 16)
    nrm = jax.random.normal
    x = nrm(ks[0], (BATCH, SEQ, D_MODEL), jnp.float32)
    col_scale = jnp.ones((N_IN,), jnp.float32)
    col_scale = col_scale.at[MLSTM_V_START:MLSTM_V_START + MLSTM_WIDTH].set(BETA)
    col_scale = col_scale.at[MOBA_V_START:MOBA_V_START + MOBA_WIDTH].set(BETA)
    w_in = nrm(ks[1], (DEPTH, D_MODEL, N_IN), jnp.float32) * (D_MODEL ** -0.5) * col_scale
    b_in = 0.02 * nrm(ks[2], (DEPTH, N_IN), jnp.float32)
    b_in = b_in.at[:, MLSTM_F_START:MLSTM_F_START + MLSTM_HEADS].add(
        jnp.linspace(3.0, 6.0, MLSTM_HEADS, dtype=jnp.float32))
    conv_qk = nrm(ks[3], (DEPTH, MLSTM_CONV, 2 * MLSTM_WIDTH), jnp.float32) * (MLSTM_CONV ** -0.5)
    mlstm_norm = 1.0 + 0.02 * nrm(ks[4], (DEPTH, MLSTM_WIDTH), jnp.float32)
    rel_bias = 0.1 * nrm(ks[5], (NUM_BUCKETS, MOBA_HEADS), jnp.float32)
    w_branch_a = nrm(ks[6], (DEPTH, MLSTM_WIDTH, D_MODEL), jnp.float32) * (MLSTM_WIDTH ** -0.5)
    w_branch_b = nrm(ks[7], (DEPTH, MOBA_WIDTH, D_MODEL), jnp.float32) * (MOBA_WIDTH ** -0.5)
    w_out = nrm(ks[8], (DEPTH, D_MODEL, D_MODEL), jnp.float32) * (D_MODEL ** -0.5) * BETA
    ln1_g = 1.0 + 0.02 * nrm(ks[9], (DEPTH, D_MODEL), jnp.float32)
    ln1_b = 0.02 * nrm(ks[10], (DEPTH, D_MODEL), jnp.float32)
    w_up = nrm(ks[11], (DEPTH, D_MODEL, 2 * D_FF), jnp.float32) * (D_MODEL ** -0.5)
    conv_ffn = nrm(ks[12], (DEPTH, FFN_CONV, 2 * D_FF), jnp.float32) * (FFN_CONV ** -0.5)
    w_down = nrm(ks[13], (DEPTH, D_FF, D_MODEL), jnp.float32) * (D_FF ** -0.5) * BETA
    ln2_g = 1.0 + 0.02 * nrm(ks[14], (DEPTH, D_MODEL), jnp.float32)
    ln2_b = 0.02 * nrm(ks[15], (DEPTH, D_MODEL), jnp.float32)
    return {"x": x, "w_in": w_in, "b_in": b_in, "conv_qk": conv_qk, "mlstm_norm": mlstm_norm,
            "rel_bias": rel_bias, "w_branch_a": w_branch_a, "w_branch_b": w_branch_b,
            "w_out": w_out, "ln1_g": ln1_g, "ln1_b": ln1_b, "w_up": w_up, "conv_ffn": conv_ffn,
            "w_down": w_down, "ln2_g": ln2_g, "ln2_b": ln2_b}


def reference(x, w_in, b_in, conv_qk, mlstm_norm, rel_bias, w_branch_a, w_branch_b, w_out,
              ln1_g, ln1_b, w_up, conv_ffn, w_down, ln2_g, ln2_b):
    for l in range(DEPTH):
        mix = hybrid_mixer(x, w_in[l], b_in[l], conv_qk[l], mlstm_norm[l], rel_bias,
                           w_branch_a[l], w_branch_b[l], w_out[l])
        x = layer_norm(ALPHA * x + mix, ln1_g[l], ln1_b[l])
        ffn = conv_gated_mlp(x, w_up[l], conv_ffn[l], w_down[l])
        x = layer_norm(ALPHA * x + ffn, ln2_g[l], ln2_b[l])
    return x
```

```python
import math
from contextlib import ExitStack

import numpy as np
import ml_dtypes
import concourse.bass as bass
import concourse.mybir as mybir
from concourse.bass_utils import run_bass_kernel_spmd

F32 = mybir.dt.float32
BF16 = mybir.dt.bfloat16
AF = mybir.ActivationFunctionType
ALU = mybir.AluOpType
AX = mybir.AxisListType

D = 1024
NIN = 9224
DFF = 2816
NL = 2
ALPHA = (2 * NL) ** 0.25
EPS = 1e-5
NBK = 32
BIG = 30000.0
SCALE_B = 128 ** -0.5

ENGS = ["pe", "act", "dve", "pool", "sp"]
NDMA = 24
SAME_ENGINE_SYNC = True
SBUF_BASE = 20 * 1024
SBUF_LIMIT = 222 * 1024


class Buf:
    __slots__ = ("t", "name", "w", "r")

    def __init__(self, t, name):
        self.t = t
        self.name = name
        self.w = None
        self.r = {}

    def __getitem__(self, idx):
        return self.t[idx]


class Sched:
    def __init__(self, nc):
        self.nc = nc
        self.prog = {e: [] for e in ENGS}
        self.seq = {e: 0 for e in ENGS}
        self.pending = {e: False for e in ENGS}
        self.waited = {e: {} for e in ENGS}
        self.dma_gen = [0] * NDMA
        self.dma_rr = 0
        self.sems = {}
        self.dsems = []
        self.n_inst = 0

    def _deps(self, reads, writes):
        deps = {}
        for b in list(reads) + list(writes):
            if b.w is not None:
                k, v = b.w
                if deps.get(k, 0) < v:
                    deps[k] = v
        for b in writes:
            for k, v in b.r.items():
                if deps.get(k, 0) < v:
                    deps[k] = v
        return deps

    def _emit_waits(self, eng, deps):
        wd = self.waited[eng]
        for k, v in deps.items():
            if k == eng and (eng == "pe" or not SAME_ENGINE_SYNC):
                continue
            if wd.get(k, 0) >= v:
                continue
            wd[k] = v
            self.prog[eng].append(("wait", k, v))

    def _mark(self, reads, writes, ev):
        k, v = ev
        for b in writes:
            b.w = ev
            b.r = {}
        for b in reads:
            if b.r.get(k, 0) < v:
                b.r[k] = v

    def op(self, eng, fn, reads=(), writes=(), signal=True):
        deps = self._deps(reads, writes)
        self._emit_waits(eng, deps)
        if signal:
            self.seq[eng] += 1
            self.pending[eng] = False
            ev = (eng, self.seq[eng])
        else:
            self.pending[eng] = True
            ev = (eng, self.seq[eng] + 1)
        self.prog[eng].append(("op", fn, signal))
        self._mark(reads, writes, ev)
        self.n_inst += 1

    def dma(self, out_ap, in_ap, reads=(), writes=(), q="sp"):
        deps = self._deps(reads, writes)
        k = self.dma_rr
        self.dma_rr = (self.dma_rr + 1) % NDMA
        key = ("dma", k)
        if self.dma_gen[k] > 0:
            deps[key] = max(deps.get(key, 0), 16 * self.dma_gen[k])
        self._emit_waits(q, deps)
        self.dma_gen[k] += 1
        ev = (key, 16 * self.dma_gen[k])
        self.prog[q].append(("dma", out_ap, in_ap, k))
        self._mark(reads, writes, ev)
        self.n_inst += 1

    def barrier(self):
        deps = {}
        for e in ENGS:
            assert not self.pending[e]
            if self.seq[e] > 0:
                deps[e] = self.seq[e]
        for k in range(NDMA):
            if self.dma_gen[k] > 0:
                deps[("dma", k)] = 16 * self.dma_gen[k]
        for e in ENGS:
            d = dict(deps)
            d.pop(e, None)
            self._emit_waits(e, d)

    def emit(self, block, stack):
        nc = self.nc
        for e in ENGS:
            self.sems[e] = stack.enter_context(nc.semaphore("tl_" + e))
        for k in range(NDMA):
            self.dsems.append(stack.enter_context(nc.semaphore("dq_%d" % k)))

        def sem_of(key):
            if isinstance(key, tuple):
                return self.dsems[key[1]]
            return self.sems[key]

        def run(eng_name, eng):
            mysem = self.sems[eng_name]
            for item in self.prog[eng_name]:
                kind = item[0]
                if kind == "wait":
                    eng.wait_ge(sem_of(item[1]), item[2])
                elif kind == "op":
                    inst = item[1](eng)
                    if item[2]:
                        inst.then_inc(mysem, 1)
                else:
                    eng.dma_start(out=item[1], in_=item[2]).then_inc(self.dsems[item[3]], 16)

        @block.tensor
        def _(eng):
            run("pe", eng)

        @block.scalar
        def _(eng):
            run("act", eng)

        @block.vector
        def _(eng):
            run("dve", eng)

        @block.gpsimd
        def _(eng):
            run("pool", eng)

        @block.sync
        def _(eng):
            run("sp", eng)


def _dsize(dt):
    return 2 if dt == BF16 else 4


class Arena:
    def __init__(self, nc):
        self.nc = nc
        self.base = SBUF_BASE
        self.off = SBUF_BASE
        self.n = 0

    def alloc(self, name, shape, dtype):
        per = _dsize(dtype)
        for s in shape[1:]:
            per *= s
        off = (self.off + 63) // 64 * 64
        t = self.nc.alloc_sbuf_tensor_at("%s_%d" % (name, self.n), list(shape), dtype, offset=off)
        self.n += 1
        self.off = off + per
        assert self.off <= SBUF_LIMIT, ("SBUF overflow", name, self.off)
        return Buf(t, name)

    def freeze(self):
        self.base = self.off

    def reset(self):
        self.off = self.base


class Defer:
    def __init__(self):
        self.q = []

    def push(self, fn):
        self.q.append(fn)

    def flush(self, keep=0):
        while len(self.q) > keep:
            self.q.pop(0)()


class Rot:
    def __init__(self, items):
        self.items = items
        self.i = 0

    def next(self):
        b = self.items[self.i % len(self.items)]
        self.i += 1
        return b


class K:
    pass


def _ops(k):
    S = k.S

    def MM(out, lhsT, rhs, start, stop, R, W, sig=True):
        S.op("pe", lambda e: e.matmul(out, lhsT=lhsT, rhs=rhs, start=start, stop=stop), R, W, sig)

    def TR(out, in_, ident, R, W, sig=True):
        S.op("pe", lambda e: e.transpose(out, in_, ident), R, W, sig)

    def ACT(out, in_, func, R, W, bias=0.0, scale=1.0):
        S.op("act", lambda e: e.activation(out=out, in_=in_, func=func, bias=bias, scale=scale), R, W)

    def TT(eng, out, in0, in1, op, R, W):
        S.op(eng, lambda e: e.tensor_tensor(out=out, in0=in0, in1=in1, op=op), R, W)

    def TS(eng, out, in0, s1, s2, op0, op1, R, W):
        if s2 is None:
            S.op(eng, lambda e: e.tensor_scalar(out=out, in0=in0, scalar1=s1, scalar2=None, op0=op0), R, W)
        else:
            S.op(eng, lambda e: e.tensor_scalar(out=out, in0=in0, scalar1=s1, scalar2=s2, op0=op0, op1=op1), R, W)

    def STT(eng, out, in0, sc, in1, op0, op1, R, W):
        S.op(eng, lambda e: e.scalar_tensor_tensor(out=out, in0=in0, scalar=sc, in1=in1, op0=op0, op1=op1), R, W)

    def CP(eng, out, in_, R, W):
        if eng == "act":
            S.op("act", lambda e: e.activation(out=out, in_=in_, func=AF.Identity), R, W)
        else:
            S.op(eng, lambda e: e.tensor_copy(out=out, in_=in_), R, W)

    def MS(eng, ap, val, W):
        S.op(eng, lambda e: e.memset(ap, val), (), W)

    k.MM, k.TR, k.ACT, k.TT, k.TS, k.STT, k.CP, k.MS = MM, TR, ACT, TT, TS, STT, CP, MS


def build(T, nl=NL, taps=(), phases=None, feeds=()):
    assert T % 512 == 0
    nc = bass.Bass("TRN2", target_bir_lowering=False)
    k = K()
    k.nc, k.T, k.nl = nc, T, nl
    k.S = Sched(nc)
    _ops(k)
    k.A = Arena(nc)
    NT = T // 128
    NG = T // 512
    NB = T // 256
    k.NT, k.NG, k.NB = NT, NG, NB

    k.input_names = []
    pk = None if phases is None else set(p.split("_")[0] for p in phases)
    wneed = {"w_in": "p1", "w_branch_a": "p4a", "w_branch_b": "p4a", "w_out": "p4a", "w_up": "p4b", "w_down": "p4c"}

    def din(name, shape, dt=F32):
        if pk is not None and name in wneed and wneed[name] not in pk:
            return None
        k.input_names.append(name)
        return nc.dram_tensor(name, list(shape), dt, kind="ExternalInput").ap()

    def dscr(name, shape, dt):
        kind = "ExternalOutput" if name in taps else ("ExternalInput" if name in feeds else "Internal")
        if name in feeds:
            k.input_names.append(name)
        return nc.dram_tensor(name, list(shape), dt, kind=kind).ap()

    k.xT_in = din("xT", [D, T])
    k.x_in = din("x", [T, D])
    k.w_in = din("w_in", [NL, D, NIN])
    k.w_a = din("w_branch_a", [NL, D, D])
    k.w_b = din("w_branch_b", [NL, D, D])
    k.w_out = din("w_out", [NL, D, D])
    k.w_up = din("w_up", [NL, D, 2 * DFF])
    k.w_down = din("w_down", [NL, DFF, D])
    k.c_bfm = din("c_bfm", [128, NL * 64])
    k.c_bif = din("c_bif", [8, NL])
    k.c_btm = din("c_btm", [128, NL * 3072])
    k.c_cqk = din("c_cqk", [128, NL * 16 * 4])
    k.c_gain = din("c_gain", [128, NL * 1024])
    k.c_ln = din("c_ln", [128, NL * 4 * 1024])
    k.c_cff = din("c_cff", [128, NL * 44 * 3])
    k.c_rb31 = din("c_rb31", [128, 8])
    k.c_bnear = din("c_bnear", [128, 8 * 2 * 128])
    k.c_cmask = din("c_cmask", [128, 128])
    k.c_maskneg = din("c_maskneg", [128, 32 * 32])
    k.c_E = din("c_E", [32, 32 * 128], BF16)
    k.c_identb = din("c_identb", [128, 128], BF16)
    k.c_identf = din("c_identf", [128, 128])
    k.c_tri = din("c_tri", [128, 128])
    k.c_oh4 = din("c_oh4", [4, 4 * 128])
    k.y_out = nc.dram_tensor("y", [T, D], F32, kind="ExternalOutput").ap()

    k.xT_d = dscr("s_xT", [D, T], BF16)
    k.x_d = dscr("s_x", [T, D], F32)
    k.x1_d = dscr("s_x1", [T, D], F32)
    k.x1T_d = dscr("s_x1T", [D, T], BF16)
    k.qmT_d = dscr("s_qmT", [D, T], BF16)
    k.kmT_d = dscr("s_kmT", [D, T], BF16)
    k.km_d = dscr("s_km", [T, D], BF16)
    k.vm_d = dscr("s_vm", [T, D], BF16)
    k.om_d = dscr("s_om", [T, D], BF16)
    k.gates_d = dscr("s_gates", [8, T], F32)
    k.qbT_d = dscr("s_qbT", [D, T], BF16)
    k.kbT_d = dscr("s_kbT", [D, T], BF16)
    k.vb_d = dscr("s_vb", [T, D], BF16)
    k.gaT_d = dscr("s_gaT", [D, T], BF16)
    k.gbT_d = dscr("s_gbT", [D, T], BF16)
    k.yaT_d = dscr("s_yaT", [D, T], BF16)
    k.ybT_d = dscr("s_ybT", [D, T], BF16)
    k.hT_d = dscr("s_hT", [DFF, T], BF16)

    k.pf = [Buf(nc.alloc_psum_tensor("pf%d" % i, [128, 512], F32), "pf%d" % i) for i in range(6)]
    k.pb = [Buf(nc.alloc_psum_tensor("pb%d" % i, [128, 1024], BF16), "pb%d" % i) for i in range(2)]

    A = k.A
    S = k.S

    def const(name, src, shape, dt=F32):
        b = A.alloc(name, shape, dt)
        S.dma(b[:], src, writes=[b])
        return b

    k.bfm = const("bfm", k.c_bfm, [128, NL * 64])
    k.bif = const("bif", k.c_bif, [8, NL])
    k.cqk = const("cqk", k.c_cqk, [128, NL * 64])
    k.cff = const("cff", k.c_cff, [128, NL * 132])
    k.rb31 = const("rb31", k.c_rb31, [128, 8])
    k.identb = const("identb", k.c_identb, [128, 128], BF16)
    k.identf = const("identf", k.c_identf, [128, 128])
    k.tri = const("tri", k.c_tri, [128, 128])
    k.oh4 = const("oh4", k.c_oh4, [4, 512])
    A.freeze()

    ph = phases if phases is not None else ["p0"] + [x for l in range(nl) for x in
                                                    ["p1_%d" % l, "p2a_%d" % l, "p2b_%d" % l, "p3_%d" % l,
                                                     "p4a_%d" % l, "p4b_%d" % l, "p4c_%d" % l]]
    for name in ph:
        S.barrier()
        A.reset()
        if name == "p0":
            phase0(k)
        else:
            kind, l = name.split("_")
            l = int(l)
            {"p1": phase1, "p2a": phase2a, "p2b": phase2b, "p3": phase3, "p4a": phase4a, "p4b": phase4b, "p4c": phase4c}[kind](k, l)
    S.barrier()
    with ExitStack() as st:
        block = st.enter_context(nc.Block())
        S.emit(block, st)
    k.nc = nc
    return nc, k


def phase0(k):
    S, A, T = k.S, k.A, k.T
    xb = Rot([A.alloc("xb", [128, 8, 512], BF16) for _ in range(3)])
    xf = Rot([A.alloc("xf", [128, 4, 1024], F32) for _ in range(2)])
    for g in range(k.NG):
        b = xb.next()
        S.dma(b[:], k.xT_in[:, g * 512:(g + 1) * 512].rearrange("(k p) t -> p k t", p=128), writes=[b], q="pool")
        S.dma(k.xT_d[:, g * 512:(g + 1) * 512].rearrange("(k p) t -> p k t", p=128), b[:], reads=[b])
        f = xf.next()
        S.dma(f[:], k.x_in[g * 512:(g + 1) * 512, :].rearrange("(j p) d -> p j d", p=128), writes=[f])
        S.dma(k.x_d[g * 512:(g + 1) * 512, :].rearrange("(j p) d -> p j d", p=128), f[:], reads=[f], q="pool")


FM_COL0 = {"qm": 0, "km": 1024, "qb": 4104, "kb": 5128, "ga": 7176, "gb": 8200}
FM_IDX0 = {"qm": 0, "km": 8, "qb": 16, "kb": 24, "ga": 32, "gb": 40}
TM_COL0 = {"vm": 2048, "om": 3072, "vb": 6152}
TM_IDX0 = {"vm": 0, "om": 1024, "vb": 2048}


def phase1(k, l):
    S, A, T, NG = k.S, k.A, k.T, k.NG
    MM, TR, ACT, TT, TS, STT, CP, MS = k.MM, k.TR, k.ACT, k.TT, k.TS, k.STT, k.CP, k.MS
    wt = Rot([A.alloc("w", [128, 8, 1024], BF16) for _ in range(2)])
    wif = A.alloc("wif", [128, 8, 8], BF16)
    btm = A.alloc("btm", [128, 3072], F32)
    xt = Rot([A.alloc("xt", [128, 8, 512], BF16) for _ in range(3)])
    cb = Rot([A.alloc("cb", [128, 515], F32) for _ in range(3)])
    acc = Rot([A.alloc("acc", [128, 512], F32) for _ in range(3)])
    ob = Rot([A.alloc("ob", [128, 512], BF16) for _ in range(6)])
    hal = [A.alloc("hal%d" % c, [128, 3], F32) for c in range(8)]
    tkb = Rot([A.alloc("tkb", [128, 4, 1024], BF16) for _ in range(2)])
    tmo = Rot([A.alloc("tmo", [128, 1024], BF16) for _ in range(3)])
    tmf = Rot([A.alloc("tmf", [128, 512], F32) for _ in range(2)])
    ifb = Rot([A.alloc("ifb", [8, 512], F32) for _ in range(2)])
    pf = Rot(k.pf[0:5])
    pbr = Rot(k.pb)

    S.dma(btm[:], k.c_btm[:, l * 3072:(l + 1) * 3072], writes=[btm])
    S.dma(wif[:], k.w_in[l, :, 4096:4104].rearrange("(k p) n -> p k n", p=128), writes=[wif], q="pool")

    blocks = [("conv", "qm", k.qmT_d), ("conv", "km", k.kmT_d), ("fm", "qb", k.qbT_d), ("fm", "kb", k.kbT_d),
              ("fm", "ga", k.gaT_d), ("fm", "gb", k.gbT_d), ("tm", "vm", k.vm_d), ("tm", "om", k.om_d),
              ("tm", "vb", k.vb_d)]

    def load_w(name):
        c0 = FM_COL0[name] if name in FM_COL0 else TM_COL0[name]
        w = wt.next()
        S.dma(w[:], k.w_in[l, :, c0:c0 + 1024].rearrange("(k p) n -> p k n", p=128), writes=[w], q="pool")
        return w

    def load_x(g):
        x = xt.next()
        S.dma(x[:], k.xT_d[:, g * 512:(g + 1) * 512].rearrange("(k p) t -> p k t", p=128), writes=[x])
        return x

    wnext = load_w(blocks[0][1])
    dq = Defer()
    for bi, (kind, name, dest) in enumerate(blocks):
        dq.flush()
        w = wnext
        if bi + 1 < len(blocks):
            wnext = load_w(blocks[bi + 1][1])
        xnext = load_x(0)
        for g in range(NG):
            x = xnext
            if g + 1 < NG:
                xnext = load_x(g + 1)
            tsl = slice(g * 512, (g + 1) * 512)
            if kind in ("conv", "fm"):
                if name == "km":
                    tk = tkb.next()
                for cc in range(8):
                    ps = pf.next()
                    for kk in range(8):
                        MM(ps[:, 0:512], w[:, kk, cc * 128:(cc + 1) * 128], x[:, kk, :], kk == 0, kk == 7,
                           [w, x], [ps], sig=(kk == 7))
                    dq.flush(keep=2)
                    bcol = l * 64 + FM_IDX0[name] + cc
                    bias = k.bfm[:, bcol:bcol + 1]
                    o = ob.next()
                    if kind == "fm":
                        func = AF.Sigmoid if name in ("ga", "gb") else AF.Identity
                        ACT(o[:], ps[:, 0:512], func, [ps, k.bfm], [o], bias=bias)
                    else:
                        c = cb.next()
                        a = acc.next()
                        h = hal[cc]
                        ACT(c[:, 3:515], ps[:, 0:512], AF.Identity, [ps, k.bfm], [c], bias=bias)
                        if g == 0:
                            MS("pool", c[:, 0:3], 0.0, [c])
                        else:
                            CP("pool", c[:, 0:3], h[:], [h], [c])
                        ci = l * 64 + (FM_IDX0[name] + cc) * 4
                        TS("dve", a[:], c[:, 3:515], k.cqk[:, ci + 3:ci + 4], None, ALU.mult, None, [c, k.cqk], [a])
                        for j in (2, 1, 0):
                            STT("dve", a[:], c[:, j:j + 512], k.cqk[:, ci + j:ci + j + 1], a[:], ALU.mult, ALU.add,
                                [c, a, k.cqk], [a])
                        CP("pool", h[:], c[:, 512:515], [c], [h])
                        ACT(o[:], a[:], AF.Silu, [a], [o])
                        if name == "km":
                            def fin(o=o, tk=tk, cc=cc):
                                pb = pbr.next()
                                for j in range(4):
                                    TR(pb[:, j * 128:(j + 1) * 128], o[:, j * 128:(j + 1) * 128], k.identb[:],
                                       [o, k.identb], [pb], sig=(j == 3))
                                CP("act" if cc % 2 else "dve", tk[:, :, cc * 128:(cc + 1) * 128],
                                   pb[:, 0:512].rearrange("p (j f) -> p j f", f=128), [pb], [tk])
                            dq.push(fin)
                    r0 = (FM_IDX0[name] % 8 + cc) * 128
                    S.dma(dest[r0:r0 + 128, tsl], o[:], reads=[o], q="sp")
                if name == "km":
                    def store(tk=tk, tsl=tsl):
                        S.dma(k.km_d[tsl, :].rearrange("(j p) f -> p j f", p=128), tk[:], reads=[tk], q="sp")
                    dq.push(store)
                if name == "qm":
                    ps = pf.next()
                    for kk in range(8):
                        MM(ps[0:8, 0:512], wif[:, kk, :], x[:, kk, :], kk == 0, kk == 7, [wif, x], [ps], sig=(kk == 7))
                    fb = ifb.next()
                    ACT(fb[:], ps[0:8, 0:512], AF.Identity, [ps, k.bif], [fb], bias=k.bif[:, l:l + 1])
                    S.dma(k.gates_d[:, tsl], fb[:], reads=[fb], q="sp")
            else:
                for j in range(4):
                    o = tmo.next()
                    for hf in range(2):
                        ps = pf.next()
                        for kk in range(8):
                            MM(ps[:, 0:512], x[:, kk, j * 128:(j + 1) * 128], w[:, kk, hf * 512:(hf + 1) * 512],
                               kk == 0, kk == 7, [w, x], [ps], sig=(kk == 7))
                        b0 = TM_IDX0[name] + hf * 512
                        if name == "om":
                            f = tmf.next()
                            TT("dve", f[:], ps[:, 0:512], btm[:, b0:b0 + 512], ALU.add, [ps, btm], [f])
                            ACT(o[:, hf * 512:(hf + 1) * 512], f[:], AF.Sigmoid, [f], [o])
                        else:
                            TT("dve", o[:, hf * 512:(hf + 1) * 512], ps[:, 0:512], btm[:, b0:b0 + 512], ALU.add,
                               [ps, btm], [o])
                    S.dma(dest[g * 512 + j * 128:g * 512 + (j + 1) * 128, :], o[:], reads=[o], q="sp")


def phase2a(k, l):
    S, A, T, NT = k.S, k.A, k.T, k.NT
    MM, TR, ACT, TT, TS, STT, CP, MS = k.MM, k.TR, k.ACT, k.TT, k.TS, k.STT, k.CP, k.MS
    NC = NT
    k.gtm = A.alloc("gtm", [128, NC * 4], F32)
    k.ttm = A.alloc("ttm", [128, NC * 4], F32)
    k.ebc = A.alloc("ebc", [128, 4 * NC], F32)
    ig = A.alloc("ig", [4, T], F32)
    fg = A.alloc("fg", [4, T], F32)
    ones = A.alloc("ones", [4, T], F32)
    Bn = A.alloc("Bn", [4, T], F32)
    sp = fg
    ap_ = ig
    sm = {n: A.alloc(n, [4, NC], F32) for n in
          ["Bend", "Bn0", "amax", "dB", "dBp", "Mc", "Mcp", "e", "off1", "off2", "t1"]}
    S.dma(ig[:], k.gates_d[0:4, :], writes=[ig])
    S.dma(fg[:], k.gates_d[4:8, :], writes=[fg])
    MS("dve", ones[:], 1.0, [ones])
    ACT(sp[:], fg[:], AF.Exp, [fg], [sp], scale=-1.0)
    ACT(sp[:], sp[:], AF.Ln, [sp], [sp], bias=1.0)
    assert sp is fg and ap_ is ig
    S.op("dve", lambda e: e.tensor_tensor_scan(out=Bn[:], data0=ones[:], data1=sp[:], initial=0.0,
                                               op0=ALU.mult, op1=ALU.add), [ones, sp], [Bn])
    TT("dve", ap_[:], ig[:], Bn[:], ALU.add, [ig, Bn], [ap_])
    Bend, Bn0, amax, dB, dBp, Mc, Mcp, e_, off1, off2, t1 = [sm[n] for n in
                                                             ["Bend", "Bn0", "amax", "dB", "dBp", "Mc", "Mcp", "e",
                                                              "off1", "off2", "t1"]]
    CP("dve", Bend[:], Bn[:, 127::128], [Bn], [Bend])
    MS("dve", Bn0[:], 0.0, [Bn0])
    if NC > 1:
        CP("dve", Bn0[:, 1:NC], Bend[:, 0:NC - 1], [Bend], [Bn0])
    S.op("dve", lambda e: e.tensor_reduce(out=amax[:], in_=ap_[:].rearrange("p (c s) -> p c s", s=128),
                                          axis=AX.X, op=ALU.max), [ap_], [amax])
    TT("dve", amax[:], amax[:], Bn0[:], ALU.subtract, [amax, Bn0], [amax])
    TT("dve", dB[:], Bn0[:], Bend[:], ALU.subtract, [Bn0, Bend], [dB])
    MS("dve", dBp[:], 0.0, [dBp])
    if NC > 1:
        CP("dve", dBp[:, 1:NC], dB[:, 0:NC - 1], [dB], [dBp])
    S.op("dve", lambda e: e.tensor_tensor_scan(out=Mc[:], data0=dBp[:], data1=amax[:], initial=0.0,
                                               op0=ALU.add, op1=ALU.max), [dBp, amax], [Mc])
    MS("dve", Mcp[:], 0.0, [Mcp])
    if NC > 1:
        CP("dve", Mcp[:, 1:NC], Mc[:, 0:NC - 1], [Mc], [Mcp])
    TT("dve", t1[:], dBp[:], Mcp[:], ALU.add, [dBp, Mcp], [t1])
    TT("dve", t1[:], t1[:], Mc[:], ALU.subtract, [t1, Mc], [t1])
    ACT(e_[:], t1[:], AF.Exp, [t1], [e_])
    TT("dve", off2[:], Bn0[:], Mc[:], ALU.add, [Bn0, Mc], [off2])
    TS("dve", off1[:], off2[:], math.log(16.0), None, ALU.add, None, [off2], [off1])
    v3 = "p (c s) -> p c s"
    TT("dve", ap_[:].rearrange(v3, s=128), ap_[:].rearrange(v3, s=128),
       off1[:].rearrange("p (c o) -> p c o", o=1).broadcast_to([4, NC, 128]), ALU.subtract, [ap_, off1], [ap_])
    TT("dve", Bn[:].rearrange(v3, s=128), Bn[:].rearrange(v3, s=128),
       off2[:].rearrange("p (c o) -> p c o", o=1).broadcast_to([4, NC, 128]), ALU.subtract, [Bn, off2], [Bn])
    ACT(ap_[:], ap_[:], AF.Exp, [ap_], [ap_])
    ACT(Bn[:], Bn[:], AF.Exp, [Bn], [Bn])
    pg, pt, pe = k.pf[0], k.pf[1], k.pf[2]
    for c in range(NC):
        TR(pg[:, c * 4:(c + 1) * 4], ap_[:, c * 128:(c + 1) * 128], k.identf[0:4, 0:4], [ap_, k.identf], [pg],
           sig=(c == NC - 1))
    for c in range(NC):
        TR(pt[:, c * 4:(c + 1) * 4], Bn[:, c * 128:(c + 1) * 128], k.identf[0:4, 0:4], [Bn, k.identf], [pt],
           sig=(c == NC - 1))
    for h in range(4):
        MM(pe[:, h * NC:(h + 1) * NC], k.oh4[:, h * 128:(h + 1) * 128], e_[:], True, True, [k.oh4, e_], [pe],
           sig=(h == 3))
    CP("dve", k.gtm[:], pg[:, 0:NC * 4], [pg], [k.gtm])
    CP("dve", k.ttm[:], pt[:, 0:NC * 4], [pt], [k.ttm])
    CP("dve", k.ebc[:], pe[:, 0:NC * 4], [pe], [k.ebc])


def phase2b(k, l):
    S, A, T, NT = k.S, k.A, k.T, k.NT
    MM, TR, ACT, TT, TS, STT, CP, MS = k.MM, k.TR, k.ACT, k.TT, k.TS, k.STT, k.CP, k.MS
    NC = NT
    gtm = A.alloc("gtm", [128, NC * 4], F32)
    ttm = A.alloc("ttm", [128, NC * 4], F32)
    ebc = A.alloc("ebc", [128, 4 * NC], F32)
    gtm.w, ttm.w, ebc.w = k.gtm.w, k.ttm.w, k.ebc.w
    gain = A.alloc("gain", [128, 1024], F32)
    S.dma(gain[:], k.c_gain[:, l * 1024:(l + 1) * 1024], writes=[gain])
    qT4 = Rot([A.alloc("qT4", [128, 8, 512], BF16) for _ in range(2)])
    kT4 = Rot([A.alloc("kT4", [128, 8, 512], BF16) for _ in range(2)])
    k4 = Rot([A.alloc("k4", [128, 4, 1024], BF16) for _ in range(2)])
    v4 = Rot([A.alloc("v4", [128, 4, 1024], BF16) for _ in range(2)])
    o4 = Rot([A.alloc("o4", [128, 4, 1024], BF16) for _ in range(2)])
    yT4 = Rot([A.alloc("yT4", [128, 8, 512], BF16) for _ in range(2)])
    C = [A.alloc("C%d" % h, [128, 2, 257], F32) for h in range(4)]
    Cs = [Rot([A.alloc("Cs%d" % h, [128, 2, 257], BF16) for _ in range(2)]) for h in range(4)]
    sTm = Rot([A.alloc("sTm", [128, 128], BF16) for _ in range(3)])
    vg = Rot([A.alloc("vg", [128, 257], BF16) for _ in range(3)])
    hh = Rot([A.alloc("hh", [128, 256], F32) for _ in range(3)])
    hn = Rot([A.alloc("hn", [128, 256], F32) for _ in range(3)])
    ya = Rot([A.alloc("ya", [128, 256], BF16) for _ in range(4)])
    st6 = Rot([A.alloc("st6", [128, 6], F32) for _ in range(3)])
    mv = Rot([A.alloc("mv", [128, 4], F32) for _ in range(3)])
    sets = [(k.pf[0], k.pf[1], k.pf[2]), (k.pf[3], k.pf[4], k.pf[5])]
    sT_tok = [Buf(k.pf[0].t, "sT0"), Buf(k.pf[3].t, "sT1")]
    out_tok = [Buf(k.pf[0].t, "out0"), Buf(k.pf[3].t, "out1")]
    pbr = Rot(k.pb)
    for h in range(4):
        MS("pool", C[h][:], 0.0, [C[h]])

    def load(g):
        tsl = slice(g * 512, (g + 1) * 512)
        a, b, c_, d, e = qT4.next(), kT4.next(), k4.next(), v4.next(), o4.next()
        S.dma(a[:], k.qmT_d[:, tsl].rearrange("(k p) t -> p k t", p=128), writes=[a])
        S.dma(b[:], k.kmT_d[:, tsl].rearrange("(k p) t -> p k t", p=128), writes=[b])
        S.dma(c_[:], k.km_d[tsl, :].rearrange("(j p) f -> p j f", p=128), writes=[c_])
        S.dma(d[:], k.vm_d[tsl, :].rearrange("(j p) f -> p j f", p=128), writes=[d])
        S.dma(e[:], k.om_d[tsl, :].rearrange("(j p) f -> p j f", p=128), writes=[e])
        return a, b, c_, d, e

    nxt = load(0)
    it = 0
    dq = Defer()
    for g in range(k.NG):
        q_, kT_, k_, v_, o_ = nxt
        if g + 1 < k.NG:
            nxt = load(g + 1)
        yT = yT4.next()
        for j in range(4):
            c = g * 4 + j
            tj = slice(j * 128, (j + 1) * 128)
            for h in range(4):
                p = it % 2
                it += 1
                bk_a, bk_u0, bk_u1 = sets[p]
                sTt, outt = sT_tok[p], out_tok[p]
                gcol = gtm[:, c * 4 + h:c * 4 + h + 1]
                tcol = ttm[:, c * 4 + h:c * 4 + h + 1]
                ecol = ebc[:, h * NC + c:h * NC + c + 1]
                for jj in range(2):
                    MM(bk_a[:, 0:128], kT_[:, 2 * h + jj, tj], q_[:, 2 * h + jj, tj], jj == 0, jj == 1,
                       [kT_, q_], [sTt], sig=(jj == 1))
                dq.flush(keep=1)
                sm_ = sTm.next()
                TT("dve", sm_[:], bk_a[:, 0:128], k.tri[:], ALU.mult, [sTt, k.tri], [sm_])
                vg_ = vg.next()
                ACT(vg_[:, 0:256], v_[:, j, h * 256:(h + 1) * 256], AF.Identity, [v_, gtm], [vg_], scale=gcol)
                ACT(vg_[:, 256:257], gcol, AF.Identity, [gtm], [vg_])
                Ch = C[h]
                cs = Cs[h].next()
                ACT(cs[:], Ch[:], AF.Identity, [Ch, ebc], [cs], scale=ecol)
                for jj in range(2):
                    MM(bk_a[:, 128:385], q_[:, 2 * h + jj, tj], cs[:, jj, :], jj == 0, False, [q_, cs], [outt], sig=False)
                MM(bk_a[:, 128:385], sm_[:], vg_[:], False, True, [sm_, vg_], [outt])
                for jj, bk in ((0, bk_u0), (1, bk_u1)):
                    MM(bk[:, 0:257], k_[:, j, (2 * h + jj) * 128:(2 * h + jj + 1) * 128], vg_[:], True, True,
                       [k_, vg_], [bk])
                    STT("dve", Ch[:, jj, :], Ch[:, jj, :], ecol, bk[:, 0:257], ALU.mult, ALU.add, [Ch, bk, ebc], [Ch])
                m_ = mv.next()
                ACT(m_[:, 0:1], bk_a[:, 384:385], AF.Abs, [outt], [m_])
                TT("dve", m_[:, 0:1], m_[:, 0:1], tcol, ALU.max, [m_, ttm], [m_])
                S.op("dve", lambda e, m_=m_: e.reciprocal(out=m_[:, 1:2], in_=m_[:, 0:1]), [m_], [m_])
                hh_ = hh.next()
                ACT(hh_[:], bk_a[:, 128:384], AF.Identity, [outt, m_], [hh_], scale=m_[:, 1:2])
                s6 = st6.next()
                S.op("dve", lambda e, s6=s6, hh_=hh_: e.bn_stats(out=s6[:], in_=hh_[:]), [hh_], [s6])
                S.op("dve", lambda e, s6=s6, m_=m_: e.bn_aggr(out=m_[:, 2:4], in_=s6[:]), [s6], [m_])
                TS("dve", m_[:, 3:4], m_[:, 3:4], EPS, None, ALU.add, None, [m_], [m_])
                ACT(m_[:, 3:4], m_[:, 3:4], AF.Sqrt, [m_], [m_])
                S.op("dve", lambda e, m_=m_: e.reciprocal(out=m_[:, 3:4], in_=m_[:, 3:4]), [m_], [m_])
                STT("dve", m_[:, 2:3], m_[:, 2:3], -1.0, m_[:, 3:4], ALU.mult, ALU.mult, [m_], [m_])
                hn_ = hn.next()
                ACT(hn_[:], hh_[:], AF.Identity, [hh_, m_], [hn_], bias=m_[:, 2:3], scale=m_[:, 3:4])
                TT("dve", hn_[:], hn_[:], gain[:, h * 256:(h + 1) * 256], ALU.mult, [hn_, gain], [hn_])
                ya_ = ya.next()
                TT("dve", ya_[:], hn_[:], o_[:, j, h * 256:(h + 1) * 256], ALU.mult, [hn_, o_], [ya_])
                def fin(ya_=ya_, yT=yT, h=h, tj=tj):
                    pb = pbr.next()
                    for jj in range(2):
                        TR(pb[:, jj * 128:(jj + 1) * 128], ya_[:, jj * 128:(jj + 1) * 128], k.identb[:],
                           [ya_, k.identb], [pb], sig=(jj == 1))
                    CP("act", yT[:, 2 * h:2 * h + 2, tj], pb[:, 0:256].rearrange("p (a t) -> p a t", t=128), [pb],
                       [yT])
                dq.push(fin)

        def store(yT=yT, g=g):
            S.dma(k.yaT_d[:, g * 512:(g + 1) * 512].rearrange("(k p) t -> p k t", p=128), yT[:], reads=[yT], q="pool")
        dq.push(store)
    dq.flush()


def phase3(k, l):
    S, A, T, NT, NG, NB = k.S, k.A, k.T, k.NT, k.NG, k.NB
    MM, TR, ACT, TT, TS, STT, CP, MS = k.MM, k.TR, k.ACT, k.TT, k.TS, k.STT, k.CP, k.MS
    bnear = A.alloc("bnear", [128, 8 * 2 * 128], F32)
    cmask = A.alloc("cmask", [128, 128], F32)
    maskneg = A.alloc("maskneg", [128, 32 * 32], F32)
    E = A.alloc("E", [32, 32 * 128], BF16)
    S.dma(bnear[:], k.c_bnear, writes=[bnear])
    S.dma(cmask[:], k.c_cmask, writes=[cmask])
    S.dma(maskneg[:], k.c_maskneg, writes=[maskneg])
    S.dma(E[:], k.c_E, writes=[E])
    for h in range(8):
        o0 = (h * 2) * 128
        TT("dve", bnear[:, o0:o0 + 128], bnear[:, o0:o0 + 128], cmask[:], ALU.add, [bnear, cmask], [bnear])
    qT = Rot([A.alloc("qTh", [128, T], BF16) for _ in range(2)])
    kT = Rot([A.alloc("kTh", [128, T], BF16) for _ in range(2)])
    vh = Rot([A.alloc("vh", [128, NT, 129], BF16) for _ in range(2)])
    addT = Rot([A.alloc("addT", [32, T], BF16) for _ in range(2)])
    kmf = Rot([A.alloc("kmf", [128, 32], F32) for _ in range(2)])
    kmb = Rot([A.alloc("kmb", [128, 32], BF16) for _ in range(2)])
    gm = Rot([A.alloc("gm", [128, 32], F32) for _ in range(3)])
    t8 = Rot([A.alloc("t8", [128, 8], F32) for _ in range(3)])
    aq = Rot([A.alloc("aq", [128, 32], F32) for _ in range(3)])
    aqb = Rot([A.alloc("aqb", [128, 32], BF16) for _ in range(4)])
    PT = Rot([A.alloc("PT", [128, 512], BF16) for _ in range(4)])
    tmpn = Rot([A.alloc("tmpn", [128, 128], F32) for _ in range(4)])
    rden = Rot([A.alloc("rden", [128, 512], F32) for _ in range(2)])
    yb = Rot([A.alloc("yb", [128, 512], BF16) for _ in range(2)])
    onesb = A.alloc("onesb", [128, 128], BF16)
    MS("pool", onesb[:], 1.0, [onesb])
    pS = Rot([k.pf[0], k.pf[1], k.pf[2]])
    pO, pD, pG = k.pf[3], k.pf[4], k.pf[5]
    pbr = Rot(k.pb)

    def load(h):
        a, b, c_ = qT.next(), kT.next(), vh.next()
        S.dma(a[:], k.qbT_d[h * 128:(h + 1) * 128, :], writes=[a])
        S.dma(b[:], k.kbT_d[h * 128:(h + 1) * 128, :], writes=[b])
        S.dma(c_[:, :, 0:128], k.vb_d[:, h * 128:(h + 1) * 128].rearrange("(n p) d -> p n d", p=128), writes=[c_])
        MS("pool", c_[:, :, 128:129], 1.0, [c_])
        return a, b, c_

    def selection_gen(q_h, kT_h, ad_h):
        kf, kb_ = kmf.next(), kmb.next()
        MS("dve", kf[:], 0.0, [kf])
        S.op("dve", lambda e: e.tensor_reduce(
            out=kf[:, 0:NB], in_=kT_h[:].rearrange("p (n s) -> p n s", s=256), axis=AX.X, op=ALU.add), [kT_h], [kf])
        TS("dve", kb_[:], kf[:], 1.0 / 256.0, None, ALU.mult, None, [kf], [kb_])
        yield
        pend = [None]
        for i0 in range(0, NT, 16):
            n_i = min(16, NT - i0)
            for ii in range(n_i):
                i = i0 + ii
                MM(pG[:, ii * 32:(ii + 1) * 32], q_h[:, i * 128:(i + 1) * 128], kb_[:], True, True, [q_h, kb_], [pG],
                   sig=(ii == n_i - 1))
            yield
            for ii in range(n_i):
                i = i0 + ii
                own = i // 2
                g_, t_, a_, ab_ = gm.next(), t8.next(), aq.next(), aqb.next()
                TT("dve", g_[:], pG[:, ii * 32:(ii + 1) * 32], maskneg[:, own * 32:(own + 1) * 32], ALU.add,
                   [pG, maskneg], [g_])
                S.op("dve", lambda e, t_=t_, g_=g_: e.max(out=t_[:], in_=g_[:]), [g_], [t_])
                TS("dve", a_[:], g_[:], t_[:, 3:4], BIG, ALU.is_ge, ALU.mult, [g_, t_], [a_])
                TS("dve", ab_[:], a_[:], -BIG, None, ALU.add, None, [a_], [ab_])
                if pend[0] is not None:
                    pend[0]()

                def fin(ab_=ab_, i=i):
                    pb = pbr.next()
                    TR(pb[0:32, 0:128], ab_[:], k.identb[:], [ab_, k.identb], [pb])
                    CP("act", ad_h[:, i * 128:(i + 1) * 128], pb[0:32, 0:128], [pb], [ad_h])
                pend[0] = fin
                yield
        if pend[0] is not None:
            pend[0]()
        yield

    nxt = load(0)
    ad_nxt = addT.next()
    for _ in selection_gen(nxt[0], nxt[1], ad_nxt):
        pass
    for h in range(8):
        q_, kT_, v_ = nxt
        ad = ad_nxt
        gen = None
        if h + 1 < 8:
            nxt = load(h + 1)
            ad_nxt = addT.next()
            gen = selection_gen(nxt[0], nxt[1], ad_nxt)
        items = [(G, kt) for G in range(NG) for kt in range(4 * G + 4)]

        def stageA(i):
            G, kt = items[i]
            n = kt // 2
            j0 = max(kt, 4 * G)
            q0, q1 = j0 * 128, (4 * G + 4) * 128
            nq = q1 - q0
            ps = pS.next()
            need_sel = (kt < 4 * G + 2) and (G >= 2)
            MM(ps[:, 0:nq], kT_[:, kt * 128:(kt + 1) * 128], q_[:, q0:q1], True, not need_sel, [kT_, q_], [ps],
               sig=not need_sel)
            if need_sel:
                MM(ps[:, 0:nq], E[:, n * 128:(n + 1) * 128], ad[:, q0:q1], False, True, [E, ad], [ps])
            return dict(G=G, kt=kt, j0=j0, q0=q0, nq=nq, c0=q0 - G * 512, ps=ps)

        def stageB(st_):
            G, kt, j0, nq, ps = st_["G"], st_["kt"], st_["j0"], st_["nq"], st_["ps"]
            pt = PT.next()
            col = 0
            for j in range(j0, 4 * G + 4):
                typ = 0 if j == kt else (1 if j == kt + 1 else 2)
                if typ == 2:
                    break
                tn = tmpn.next()
                bo = (h * 2 + typ) * 128
                STT("dve", tn[:], ps[:, col:col + 128], SCALE_B, bnear[:, bo:bo + 128], ALU.mult, ALU.add,
                    [ps, bnear], [tn])
                ACT(pt[:, col:col + 128], tn[:], AF.Exp, [tn], [pt])
                col += 128
            if col < nq:
                ACT(pt[:, col:nq], ps[:, col:nq], AF.Exp, [ps, k.rb31], [pt], bias=k.rb31[:, h:h + 1], scale=SCALE_B)
            st_["pt"] = pt

        def stageC(st_):
            G, kt, nq, c0, pt = st_["G"], st_["kt"], st_["nq"], st_["c0"], st_["pt"]
            nkt = 4 * G + 4
            MM(pO[:, c0:c0 + nq], v_[:, kt, 0:128], pt[:, 0:nq], kt == 0, kt == nkt - 1, [v_, pt], [pO],
               sig=False)
            MM(pD[:, c0:c0 + nq], onesb[:], pt[:, 0:nq], kt == 0, kt == nkt - 1, [onesb, pt], [pD],
               sig=True)
            if kt == nkt - 1:
                rd, y_ = rden.next(), yb.next()
                S.op("dve", lambda e, rd=rd: e.reciprocal(out=rd[:], in_=pD[:, 0:512]), [pD], [rd])
                TT("dve", y_[:], pO[:, 0:512], rd[:], ALU.mult, [pO, rd], [y_])
                S.dma(k.ybT_d[h * 128:(h + 1) * 128, G * 512:(G + 1) * 512], y_[:], reads=[y_], q="pool")

        LOOK = 2
        sts = {}
        for i in range(min(LOOK, len(items))):
            sts[i] = stageA(i)
        for i in range(len(items)):
            stageB(sts[i])
            if i + LOOK < len(items):
                sts[i + LOOK] = stageA(i + LOOK)
            stageC(sts.pop(i))
            if gen is not None and i % 7 == 3:
                if next(gen, "done") == "done":
                    gen = None
        if gen is not None:
            for _ in gen:
                pass


def layer_norm_tile(k, z, lnp, gi, outf, st, mv):
    S, TS, TT, STT, ACT = k.S, k.TS, k.TT, k.STT, k.ACT
    for i in range(2):
        S.op("dve", lambda e, i=i: e.bn_stats(out=st[:, i * 6:(i + 1) * 6], in_=z[:, i * 512:(i + 1) * 512]), [z], [st])
    S.op("dve", lambda e: e.bn_aggr(out=mv[:, 0:2], in_=st[:, 0:12].rearrange("p (a b) -> p a b", b=6)), [st], [mv])
    TS("dve", mv[:, 1:2], mv[:, 1:2], EPS, None, ALU.add, None, [mv], [mv])
    ACT(mv[:, 1:2], mv[:, 1:2], AF.Sqrt, [mv], [mv])
    S.op("dve", lambda e: e.reciprocal(out=mv[:, 1:2], in_=mv[:, 1:2]), [mv], [mv])
    STT("dve", mv[:, 0:1], mv[:, 0:1], -1.0, mv[:, 1:2], ALU.mult, ALU.mult, [mv], [mv])
    ACT(z[:], z[:], AF.Identity, [z, mv], [z], bias=mv[:, 0:1], scale=mv[:, 1:2])
    TT("dve", z[:], z[:], lnp[:, gi * 1024:(gi + 1) * 1024], ALU.mult, [z, lnp], [z])
    TT("dve", outf[:], z[:], lnp[:, (gi + 1) * 1024:(gi + 2) * 1024], ALU.add, [z, lnp], [outf])


def phase4a(k, l):
    S, A, T, NG = k.S, k.A, k.T, k.NG
    MM, TR, ACT, TT, TS, STT, CP, MS = k.MM, k.TR, k.ACT, k.TT, k.TS, k.STT, k.CP, k.MS
    wa = A.alloc("wa", [128, 8, 1024], BF16)
    wb = A.alloc("wb", [128, 8, 1024], BF16)
    wo = A.alloc("wo", [128, 8, 1024], BF16)
    lnp = A.alloc("lnp", [128, 2048], F32)
    S.dma(wa[:], k.w_a[l].rearrange("(k p) n -> p k n", p=128), writes=[wa], q="pool")
    S.dma(wb[:], k.w_b[l].rearrange("(k p) n -> p k n", p=128), writes=[wb], q="pool")
    S.dma(wo[:], k.w_out[l].rearrange("(k p) n -> p k n", p=128), writes=[wo], q="pool")
    S.dma(lnp[:], k.c_ln[:, (l * 4) * 1024:(l * 4 + 2) * 1024], writes=[lnp])
    ins = [Rot([A.alloc(n, [128, 8, 512], BF16) for _ in range(2)]) for n in ("yaT", "ybT", "gaT", "gbT")]
    xin = Rot([A.alloc("xin", [128, 1024], F32) for _ in range(4)])
    mT = Rot([A.alloc("mT", [128, 8, 512], BF16) for _ in range(1)])
    t1 = Rot([A.alloc("t1", [128, 512], F32) for _ in range(2)])
    t2 = Rot([A.alloc("t2", [128, 512], F32) for _ in range(2)])
    z = Rot([A.alloc("z", [128, 1024], F32) for _ in range(2)])
    x1 = Rot([A.alloc("x1", [128, 1024], F32) for _ in range(2)])
    x1b = Rot([A.alloc("x1b", [128, 1024], BF16) for _ in range(3)])
    xT = Rot([A.alloc("x1T", [128, 8, 512], BF16) for _ in range(2)])
    st = Rot([A.alloc("st", [128, 12], F32) for _ in range(2)])
    mv = Rot([A.alloc("mv", [128, 2], F32) for _ in range(2)])
    pf = Rot(k.pf)
    pbr = Rot(k.pb)
    xsrc = k.x_d

    def load(g):
        tsl = slice(g * 512, (g + 1) * 512)
        bs = [r.next() for r in ins]
        for b, d in zip(bs, (k.yaT_d, k.ybT_d, k.gaT_d, k.gbT_d)):
            S.dma(b[:], d[:, tsl].rearrange("(k p) t -> p k t", p=128), writes=[b])
        return bs

    nxt = load(0)
    dq = Defer()
    for g in range(NG):
        ya, yb, ga, gb = nxt
        if g + 1 < NG:
            nxt = load(g + 1)
        xis = []
        for j in range(4):
            xi = xin.next()
            r0 = g * 512 + j * 128
            S.dma(xi[:], xsrc[r0:r0 + 128, :], writes=[xi])
            xis.append(xi)
        m = mT.next()
        for oc in range(8):
            pa, pb_ = pf.next(), pf.next()
            for kk in range(8):
                MM(pa[:, 0:512], wa[:, kk, oc * 128:(oc + 1) * 128], ya[:, kk, :], kk == 0, kk == 7, [wa, ya], [pa],
                   sig=(kk == 7))
            for kk in range(8):
                MM(pb_[:, 0:512], wb[:, kk, oc * 128:(oc + 1) * 128], yb[:, kk, :], kk == 0, kk == 7, [wb, yb], [pb_],
                   sig=(kk == 7))
            a_, b_ = t1.next(), t2.next()
            TT("dve", a_[:], pa[:, 0:512], ga[:, oc, :], ALU.mult, [pa, ga], [a_])
            TT("dve", b_[:], pb_[:, 0:512], gb[:, oc, :], ALU.mult, [pb_, gb], [b_])
            TT("dve", m[:, oc, :], a_[:], b_[:], ALU.add, [a_, b_], [m])
        xt_ = xT.next()
        for j in range(4):
            z_ = z.next()
            for hf in range(2):
                ps = pf.next()
                for kk in range(8):
                    MM(ps[:, 0:512], m[:, kk, j * 128:(j + 1) * 128], wo[:, kk, hf * 512:(hf + 1) * 512], kk == 0,
                       kk == 7, [m, wo], [ps], sig=(kk == 7))
                if hf == 1:
                    dq.flush(keep=0)
                STT("dve", z_[:, hf * 512:(hf + 1) * 512], xis[j][:, hf * 512:(hf + 1) * 512], ALPHA, ps[:, 0:512],
                    ALU.mult, ALU.add, [xis[j], ps], [z_])
            x1_ = x1.next()
            layer_norm_tile(k, z_, lnp, 0, x1_, st.next(), mv.next())
            r0 = g * 512 + j * 128
            S.dma(k.x1_d[r0:r0 + 128, :], x1_[:], reads=[x1_], q="pool")
            xb_ = x1b.next()
            CP("act", xb_[:], x1_[:], [x1_], [xb_])

            def fin(xb_=xb_, xt_=xt_, j=j):
                pb = pbr.next()
                for c in range(8):
                    TR(pb[:, c * 128:(c + 1) * 128], xb_[:, c * 128:(c + 1) * 128], k.identb[:], [xb_, k.identb],
                       [pb], sig=(c == 7))
                CP("act", xt_[:, :, j * 128:(j + 1) * 128], pb[:, 0:1024].rearrange("p (c t) -> p c t", t=128), [pb],
                   [xt_])
            dq.push(fin)

        def store(xt_=xt_, g=g):
            S.dma(k.x1T_d[:, g * 512:(g + 1) * 512].rearrange("(k p) t -> p k t", p=128), xt_[:], reads=[xt_],
                  q="pool")
        dq.push(store)
    dq.flush()


def phase4b(k, l):
    S, A, T, NG = k.S, k.A, k.T, k.NG
    MM, TR, ACT, TT, TS, STT, CP, MS = k.MM, k.TR, k.ACT, k.TT, k.TS, k.STT, k.CP, k.MS
    wu = A.alloc("wu", [128, 8, 2 * DFF], BF16)
    for c4 in range(4):
        c0 = c4 * 1408
        S.dma(wu[:, :, c0:c0 + 1408], k.w_up[l, :, c0:c0 + 1408].rearrange("(k p) n -> p k n", p=128), writes=[wu],
              q="pool")
    hal = A.alloc("hal", [128, 44, 2], F32)
    xTi = Rot([A.alloc("xTi", [128, 8, 512], BF16) for _ in range(2)])
    cb = Rot([A.alloc("cb", [128, 514], F32) for _ in range(4)])
    acc = Rot([A.alloc("acc", [128, 512], F32) for _ in range(4)])
    sg = Rot([A.alloc("sg", [128, 512], F32) for _ in range(2)])
    ho = Rot([A.alloc("ho", [128, 512], BF16) for _ in range(4)])
    pf = Rot(k.pf)

    def load(g):
        a = xTi.next()
        S.dma(a[:], k.x1T_d[:, g * 512:(g + 1) * 512].rearrange("(k p) t -> p k t", p=128), writes=[a])
        return a

    def conv2(pg, pu, fc, g):
        res = []
        for ps, fcol in ((pg, fc), (pu, 22 + fc)):
            c, a = cb.next(), acc.next()
            CP("act", c[:, 2:514], ps[:, 0:512], [ps], [c])
            if g == 0:
                MS("pool", c[:, 0:2], 0.0, [c])
            else:
                CP("pool", c[:, 0:2], hal[:, fcol, :], [hal], [c])
            ci = l * 132 + fcol * 3
            ACT(a[:], ps[:, 0:512], AF.Identity, [ps, k.cff], [a], scale=k.cff[:, ci + 2:ci + 3])
            res.append((c, a, ci, fcol))
        for j in (1, 0):
            for c, a, ci, fcol in res:
                STT("dve", a[:], c[:, j:j + 512], k.cff[:, ci + j:ci + j + 1], a[:], ALU.mult, ALU.add,
                    [c, a, k.cff], [a])
        for c, a, ci, fcol in res:
            CP("pool", hal[:, fcol, :], c[:, 512:514], [c], [hal])
        return res[0][1], res[1][1]

    nxt = load(0)
    for g in range(NG):
        xt_in = nxt
        if g + 1 < NG:
            nxt = load(g + 1)
        for fc in range(22):
            pg, pu = pf.next(), pf.next()
            for kk in range(8):
                MM(pg[:, 0:512], wu[:, kk, fc * 128:(fc + 1) * 128], xt_in[:, kk, :], kk == 0, kk == 7, [wu, xt_in],
                   [pg], sig=(kk == 7))
            for kk in range(8):
                MM(pu[:, 0:512], wu[:, kk, DFF + fc * 128:DFF + (fc + 1) * 128], xt_in[:, kk, :], kk == 0, kk == 7,
                   [wu, xt_in], [pu], sig=(kk == 7))
            ag, au = conv2(pg, pu, fc, g)
            s_ = sg.next()
            ACT(s_[:], ag[:], AF.Silu, [ag], [s_])
            h_ = ho.next()
            TT("dve", h_[:], s_[:], au[:], ALU.mult, [s_, au], [h_])
            S.dma(k.hT_d[fc * 128:(fc + 1) * 128, g * 512:(g + 1) * 512], h_[:], reads=[h_], q="sp")


def phase4c(k, l):
    S, A, T, NG = k.S, k.A, k.T, k.NG
    MM, TR, ACT, TT, TS, STT, CP, MS = k.MM, k.TR, k.ACT, k.TT, k.TS, k.STT, k.CP, k.MS
    last = (l == k.nl - 1)
    wd = A.alloc("wd", [128, 22, 1024], BF16)
    lnp = A.alloc("lnp", [128, 2048], F32)
    S.dma(wd[:], k.w_down[l].rearrange("(k p) n -> p k n", p=128), writes=[wd], q="pool")
    S.dma(lnp[:], k.c_ln[:, (l * 4 + 2) * 1024:(l * 4 + 4) * 1024], writes=[lnp])
    hTi = Rot([A.alloc("hTi", [128, 22, 512], BF16) for _ in range(2)])
    xin = Rot([A.alloc("xin", [128, 4, 1024], F32) for _ in range(2)])
    z = Rot([A.alloc("z", [128, 1024], F32) for _ in range(2)])
    x2 = Rot([A.alloc("x2", [128, 1024], F32) for _ in range(2)])
    x2b = Rot([A.alloc("x2b", [128, 1024], BF16) for _ in range(3)])
    xT = Rot([A.alloc("x2T", [128, 8, 512], BF16) for _ in range(2)])
    st = Rot([A.alloc("st", [128, 12], F32) for _ in range(2)])
    mv = Rot([A.alloc("mv", [128, 2], F32) for _ in range(2)])
    pf = Rot(k.pf)
    pbr = Rot(k.pb)

    def load(g):
        tsl = slice(g * 512, (g + 1) * 512)
        a, b = hTi.next(), xin.next()
        S.dma(a[:], k.hT_d[:, tsl].rearrange("(k p) t -> p k t", p=128), writes=[a])
        S.dma(b[:], k.x1_d[tsl, :].rearrange("(j p) d -> p j d", p=128), writes=[b])
        return a, b

    nxt = load(0)
    dq = Defer()
    for g in range(NG):
        hT_, xi = nxt
        if g + 1 < NG:
            nxt = load(g + 1)
        xt_ = xT.next()
        for j in range(4):
            z_ = z.next()
            for hf in range(2):
                ps = pf.next()
                for kk in range(22):
                    MM(ps[:, 0:512], hT_[:, kk, j * 128:(j + 1) * 128], wd[:, kk, hf * 512:(hf + 1) * 512], kk == 0,
                       kk == 21, [hT_, wd], [ps], sig=(kk == 21))
                if hf == 1:
                    dq.flush(keep=0)
                STT("dve", z_[:, hf * 512:(hf + 1) * 512], xi[:, j, hf * 512:(hf + 1) * 512], ALPHA, ps[:, 0:512],
                    ALU.mult, ALU.add, [xi, ps], [z_])
            x2_ = x2.next()
            layer_norm_tile(k, z_, lnp, 0, x2_, st.next(), mv.next())
            r0 = g * 512 + j * 128
            if last:
                S.dma(k.y_out[r0:r0 + 128, :], x2_[:], reads=[x2_], q="pool")
            else:
                S.dma(k.x_d[r0:r0 + 128, :], x2_[:], reads=[x2_], q="pool")
                xb_ = x2b.next()
                CP("act", xb_[:], x2_[:], [x2_], [xb_])

                def fin(xb_=xb_, xt_=xt_, j=j):
                    pb = pbr.next()
                    for c in range(8):
                        TR(pb[:, c * 128:(c + 1) * 128], xb_[:, c * 128:(c + 1) * 128], k.identb[:], [xb_, k.identb],
                           [pb], sig=(c == 7))
                    CP("act", xt_[:, :, j * 128:(j + 1) * 128], pb[:, 0:1024].rearrange("p (c t) -> p c t", t=128),
                       [pb], [xt_])
                dq.push(fin)
        if not last:
            def store(xt_=xt_, g=g):
                S.dma(k.xT_d[:, g * 512:(g + 1) * 512].rearrange("(k p) t -> p k t", p=128), xt_[:], reads=[xt_],
                      q="pool")
            dq.push(store)
    dq.flush()


def t5_bucket_np(rel):
    n = np.maximum(rel, 0)
    max_exact = NBK // 2
    nf = np.maximum(n, max_exact).astype(np.float32)
    large = max_exact + (np.log(nf / np.float32(max_exact)) / np.float32(math.log(128 / max_exact))
                         * np.float32(NBK - max_exact)).astype(np.int32)
    large = np.minimum(large, NBK - 1)
    return np.where(n < max_exact, n, large)


def host_consts(inp):
    f32 = np.float32
    b_in = np.asarray(inp["b_in"], f32)
    c = {}
    bfm = np.zeros((128, NL * 64), f32)
    for l in range(NL):
        for name, c0 in FM_COL0.items():
            for cc in range(8):
                bfm[:, l * 64 + FM_IDX0[name] + cc] = b_in[l, c0 + cc * 128:c0 + (cc + 1) * 128]
    c["c_bfm"] = bfm
    c["c_bif"] = np.ascontiguousarray(b_in[:, 4096:4104].T)
    btm = np.zeros((128, NL * 3072), f32)
    for l in range(NL):
        for name, c0 in TM_COL0.items():
            btm[:, l * 3072 + TM_IDX0[name]:l * 3072 + TM_IDX0[name] + 1024] = b_in[l, c0:c0 + 1024][None, :]
    c["c_btm"] = btm
    cq = np.asarray(inp["conv_qk"], f32)
    c["c_cqk"] = np.ascontiguousarray(cq.reshape(NL, 4, 16, 128).transpose(3, 0, 2, 1).reshape(128, NL * 64))
    c["c_gain"] = np.ascontiguousarray(np.broadcast_to(np.asarray(inp["mlstm_norm"], f32).reshape(1, NL * 1024),
                                                       (128, NL * 1024)))
    ln = np.stack([np.asarray(inp[n], f32) for n in ("ln1_g", "ln1_b", "ln2_g", "ln2_b")], axis=1)
    c["c_ln"] = np.ascontiguousarray(np.broadcast_to(ln.reshape(1, NL * 4 * 1024), (128, NL * 4 * 1024)))
    cf = np.asarray(inp["conv_ffn"], f32)
    c["c_cff"] = np.ascontiguousarray(cf.reshape(NL, 3, 44, 128).transpose(3, 0, 2, 1).reshape(128, NL * 132))
    rb = np.asarray(inp["rel_bias"], f32)
    c["c_rb31"] = np.ascontiguousarray(np.broadcast_to(rb[31][None, :], (128, 8)))
    key = np.arange(128)[:, None]
    q = np.arange(128)[None, :]
    bn = np.zeros((128, 8, 2, 128), f32)
    for typ, off in ((0, 0), (1, 128)):
        bk = t5_bucket_np(q + off - key)
        bn[:, :, typ, :] = rb[bk].transpose(0, 2, 1)
    c["c_bnear"] = bn.reshape(128, 8 * 2 * 128)
    c["c_cmask"] = np.where(q >= key, 0.0, -BIG).astype(f32)
    mn = np.zeros((32, 32), f32)
    for own in range(32):
        mn[own, own] = 1e30
        mn[own, own + 1:] = -1e30
    c["c_maskneg"] = np.ascontiguousarray(np.broadcast_to(mn.reshape(1, 1024), (128, 1024)))
    E = np.zeros((32, 32, 128), f32)
    for n in range(32):
        E[n, n, :] = 1.0
    c["c_E"] = E.reshape(32, 32 * 128).astype(ml_dtypes.bfloat16)
    c["c_identb"] = np.eye(128, dtype=f32).astype(ml_dtypes.bfloat16)
    c["c_identf"] = np.eye(128, dtype=f32)
    c["c_tri"] = (q >= key).astype(f32)
    oh = np.zeros((4, 4, 128), f32)
    for h in range(4):
        oh[h, h, :] = 1.0
    c["c_oh4"] = oh.reshape(4, 512)
    return c


_WNAMES = ("w_in", "w_branch_a", "w_branch_b", "w_out", "w_up", "w_down")


def make_in_maps(inp, n_cores=8, names=None):
    x = np.asarray(inp["x"], np.float32)
    B = x.shape[0]
    c = host_consts(inp)
    shared = {n: np.ascontiguousarray(np.asarray(inp[n], np.float32)) for n in _WNAMES}
    shared.update(c)
    maps = []
    for core in range(n_cores):
        b = core % B
        m = dict(shared)
        m["x"] = np.ascontiguousarray(x[b])
        m["xT"] = np.ascontiguousarray(x[b].T)
        if names is not None:
            m = {n: v for n, v in m.items() if n in names}
        maps.append(m)
    return maps


def kernel(**inputs):
    x = np.asarray(inputs["x"], np.float32)
    B, T, _ = x.shape
    nc, _ = build(T)
    maps = make_in_maps(inputs)
    res = run_bass_kernel_spmd(nc, maps, core_ids=list(range(8)))
    out = np.stack([np.asarray(res.results[b]["y"], np.float32) for b in range(B)], axis=0)
    return out
```

```python
import math
from contextlib import ExitStack

import numpy as np
import ml_dtypes
import concourse.bass as bass
import concourse.mybir as mybir
from concourse.bass_utils import run_bass_kernel_spmd

F32 = mybir.dt.float32
BF16 = mybir.dt.bfloat16
AF = mybir.ActivationFunctionType
ALU = mybir.AluOpType
AX = mybir.AxisListType

D = 1024
NIN = 9224
DFF = 2816
NL = 2
ALPHA = (2 * NL) ** 0.25
EPS = 1e-5
NBK = 32
BIG = 30000.0
SCALE_B = 128 ** -0.5

ENGS = ["pe", "act", "dve", "pool", "sp"]
NDMA = 24
SAME_ENGINE_SYNC = True
SBUF_BASE = 20 * 1024
SBUF_LIMIT = 222 * 1024


class Buf:
    __slots__ = ("t", "name", "w", "r")

    def __init__(self, t, name):
        self.t = t
        self.name = name
        self.w = None
        self.r = {}

    def __getitem__(self, idx):
        return self.t[idx]


class Sched:
    def __init__(self, nc):
        self.nc = nc
        self.prog = {e: [] for e in ENGS}
        self.seq = {e: 0 for e in ENGS}
        self.pending = {e: False for e in ENGS}
        self.waited = {e: {} for e in ENGS}
        self.dma_gen = [0] * NDMA
        self.dma_rr = 0
        self.sems = {}
        self.dsems = []
        self.n_inst = 0

    def _deps(self, reads, writes):
        deps = {}
        for b in list(reads) + list(writes):
            if b.w is not None:
                k, v = b.w
                if deps.get(k, 0) < v:
                    deps[k] = v
        for b in writes:
            for k, v in b.r.items():
                if deps.get(k, 0) < v:
                    deps[k] = v
        return deps

    def _emit_waits(self, eng, deps):
        wd = self.waited[eng]
        for k, v in deps.items():
            if k == eng and (eng == "pe" or not SAME_ENGINE_SYNC):
                continue
            if wd.get(k, 0) >= v:
                continue
            wd[k] = v
            self.prog[eng].append(("wait", k, v))

    def _mark(self, reads, writes, ev):
        k, v = ev
        for b in writes:
            b.w = ev
            b.r = {}
        for b in reads:
            if b.r.get(k, 0) < v:
                b.r[k] = v

    def op(self, eng, fn, reads=(), writes=(), signal=True):
        deps = self._deps(reads, writes)
        self._emit_waits(eng, deps)
        if signal:
            self.seq[eng] += 1
            self.pending[eng] = False
            ev = (eng, self.seq[eng])
        else:
            self.pending[eng] = True
            ev = (eng, self.seq[eng] + 1)
        self.prog[eng].append(("op", fn, signal))
        self._mark(reads, writes, ev)
        self.n_inst += 1

    def dma(self, out_ap, in_ap, reads=(), writes=(), q="sp"):
        deps = self._deps(reads, writes)
        k = self.dma_rr
        self.dma_rr = (self.dma_rr + 1) % NDMA
        key = ("dma", k)
        if self.dma_gen[k] > 0:
            deps[key] = max(deps.get(key, 0), 16 * self.dma_gen[k])
        self._emit_waits(q, deps)
        self.dma_gen[k] += 1
        ev = (key, 16 * self.dma_gen[k])
        self.prog[q].append(("dma", out_ap, in_ap, k))
        self._mark(reads, writes, ev)
        self.n_inst += 1

    def barrier(self):
        deps = {}
        for e in ENGS:
            assert not self.pending[e]
            if self.seq[e] > 0:
                deps[e] = self.seq[e]
        for k in range(NDMA):
            if self.dma_gen[k] > 0:
                deps[("dma", k)] = 16 * self.dma_gen[k]
        for e in ENGS:
            d = dict(deps)
            d.pop(e, None)
            self._emit_waits(e, d)

    def emit(self, block, stack):
        nc = self.nc
        for e in ENGS:
            self.sems[e] = stack.enter_context(nc.semaphore("tl_" + e))
        for k in range(NDMA):
            self.dsems.append(stack.enter_context(nc.semaphore("dq_%d" % k)))

        def sem_of(key):
            if isinstance(key, tuple):
                return self.dsems[key[1]]
            return self.sems[key]

        def run(eng_name, eng):
            mysem = self.sems[eng_name]
            for item in self.prog[eng_name]:
                kind = item[0]
                if kind == "wait":
                    eng.wait_ge(sem_of(item[1]), item[2])
                elif kind == "op":
                    inst = item[1](eng)
                    if item[2]:
                        inst.then_inc(mysem, 1)
                else:
                    eng.dma_start(out=item[1], in_=item[2]).then_inc(self.dsems[item[3]], 16)

        @block.tensor
        def _(eng):
            run("pe", eng)

        @block.scalar
        def _(eng):
            run("act", eng)

        @block.vector
        def _(eng):
            run("dve", eng)

        @block.gpsimd
        def _(eng):
            run("pool", eng)

        @block.sync
        def _(eng):
            run("sp", eng)


def _dsize(dt):
    return 2 if dt == BF16 else 4


class Arena:
    def __init__(self, nc):
        self.nc = nc
        self.base = SBUF_BASE
        self.off = SBUF_BASE
        self.n = 0

    def alloc(self, name, shape, dtype):
        per = _dsize(dtype)
        for s in shape[1:]:
            per *= s
        off = (self.off + 63) // 64 * 64
        t = self.nc.alloc_sbuf_tensor_at("%s_%d" % (name, self.n), list(shape), dtype, offset=off)
        self.n += 1
        self.off = off + per
        assert self.off <= SBUF_LIMIT, ("SBUF overflow", name, self.off)
        return Buf(t, name)

    def freeze(self):
        self.base = self.off

    def reset(self):
        self.off = self.base


class Defer:
    def __init__(self):
        self.q = []

    def push(self, fn):
        self.q.append(fn)

    def flush(self, keep=0):
        while len(self.q) > keep:
            self.q.pop(0)()


class Rot:
    def __init__(self, items):
        self.items = items
        self.i = 0

    def next(self):
        b = self.items[self.i % len(self.items)]
        self.i += 1
        return b


class K:
    pass


def _ops(k):
    S = k.S

    def MM(out, lhsT, rhs, start, stop, R, W, sig=True):
        S.op("pe", lambda e: e.matmul(out, lhsT=lhsT, rhs=rhs, start=start, stop=stop), R, W, sig)

    def TR(out, in_, ident, R, W, sig=True):
        S.op("pe", lambda e: e.transpose(out, in_, ident), R, W, sig)

    def ACT(out, in_, func, R, W, bias=0.0, scale=1.0):
        S.op("act", lambda e: e.activation(out=out, in_=in_, func=func, bias=bias, scale=scale), R, W)

    def TT(eng, out, in0, in1, op, R, W):
        S.op(eng, lambda e: e.tensor_tensor(out=out, in0=in0, in1=in1, op=op), R, W)

    def TS(eng, out, in0, s1, s2, op0, op1, R, W):
        if s2 is None:
            S.op(eng, lambda e: e.tensor_scalar(out=out, in0=in0, scalar1=s1, scalar2=None, op0=op0), R, W)
        else:
            S.op(eng, lambda e: e.tensor_scalar(out=out, in0=in0, scalar1=s1, scalar2=s2, op0=op0, op1=op1), R, W)

    def STT(eng, out, in0, sc, in1, op0, op1, R, W):
        S.op(eng, lambda e: e.scalar_tensor_tensor(out=out, in0=in0, scalar=sc, in1=in1, op0=op0, op1=op1), R, W)

    def CP(eng, out, in_, R, W):
        if eng == "act":
            S.op("act", lambda e: e.activation(out=out, in_=in_, func=AF.Identity), R, W)
        else:
            S.op(eng, lambda e: e.tensor_copy(out=out, in_=in_), R, W)

    def MS(eng, ap, val, W):
        S.op(eng, lambda e: e.memset(ap, val), (), W)

    k.MM, k.TR, k.ACT, k.TT, k.TS, k.STT, k.CP, k.MS = MM, TR, ACT, TT, TS, STT, CP, MS


def build(T, nl=NL, taps=(), phases=None, feeds=()):
    assert T % 512 == 0
    nc = bass.Bass("TRN2", target_bir_lowering=False)
    k = K()
    k.nc, k.T, k.nl = nc, T, nl
    k.S = Sched(nc)
    _ops(k)
    k.A = Arena(nc)
    NT = T // 128
    NG = T // 512
    NB = T // 256
    k.NT, k.NG, k.NB = NT, NG, NB

    k.input_names = []
    pk = None if phases is None else set(p.split("_")[0] for p in phases)
    wneed = {"w_in": "p1", "w_branch_a": "p4a", "w_branch_b": "p4a", "w_out": "p4a", "w_up": "p4b", "w_down": "p4c"}

    def din(name, shape, dt=F32):
        if pk is not None and name in wneed and wneed[name] not in pk:
            return None
        k.input_names.append(name)
        return nc.dram_tensor(name, list(shape), dt, kind="ExternalInput").ap()

    def dscr(name, shape, dt):
        kind = "ExternalOutput" if name in taps else ("ExternalInput" if name in feeds else "Internal")
        if name in feeds:
            k.input_names.append(name)
        return nc.dram_tensor(name, list(shape), dt, kind=kind).ap()

    k.xT_in = din("xT", [D, T])
    k.x_in = din("x", [T, D])
    k.w_in = din("w_in", [NL, D, NIN])
    k.w_a = din("w_branch_a", [NL, D, D])
    k.w_b = din("w_branch_b", [NL, D, D])
    k.w_out = din("w_out", [NL, D, D])
    k.w_up = din("w_up", [NL, D, 2 * DFF])
    k.w_down = din("w_down", [NL, DFF, D])
    k.c_bfm = din("c_bfm", [128, NL * 64])
    k.c_bif = din("c_bif", [8, NL])
    k.c_btm = din("c_btm", [128, NL * 3072])
    k.c_cqk = din("c_cqk", [128, NL * 16 * 4])
    k.c_gain = din("c_gain", [128, NL * 1024])
    k.c_ln = din("c_ln", [128, NL * 4 * 1024])
    k.c_cff = din("c_cff", [128, NL * 44 * 3])
    k.c_rb31 = din("c_rb31", [128, 8])
    k.c_bnear = din("c_bnear", [128, 8 * 2 * 128])
    k.c_cmask = din("c_cmask", [128, 128])
    k.c_maskneg = din("c_maskneg", [128, 32 * 32])
    k.c_E = din("c_E", [32, 32 * 128], BF16)
    k.c_identb = din("c_identb", [128, 128], BF16)
    k.c_identf = din("c_identf", [128, 128])
    k.c_tri = din("c_tri", [128, 128])
    k.c_oh4 = din("c_oh4", [4, 4 * 128])
    k.y_out = nc.dram_tensor("y", [T, D], F32, kind="ExternalOutput").ap()

    k.xT_d = dscr("s_xT", [D, T], BF16)
    k.x_d = dscr("s_x", [T, D], F32)
    k.x1_d = dscr("s_x1", [T, D], F32)
    k.x1T_d = dscr("s_x1T", [D, T], BF16)
    k.qmT_d = dscr("s_qmT", [D, T], BF16)
    k.kmT_d = dscr("s_kmT", [D, T], BF16)
    k.km_d = dscr("s_km", [T, D], BF16)
    k.vm_d = dscr("s_vm", [T, D], BF16)
    k.om_d = dscr("s_om", [T, D], BF16)
    k.gates_d = dscr("s_gates", [8, T], F32)
    k.qbT_d = dscr("s_qbT", [D, T], BF16)
    k.kbT_d = dscr("s_kbT", [D, T], BF16)
    k.vb_d = dscr("s_vb", [T, D], BF16)
    k.gaT_d = dscr("s_gaT", [D, T], BF16)
    k.gbT_d = dscr("s_gbT", [D, T], BF16)
    k.yaT_d = dscr("s_yaT", [D, T], BF16)
    k.ybT_d = dscr("s_ybT", [D, T], BF16)
    k.hT_d = dscr("s_hT", [DFF, T], BF16)

    k.pf = [Buf(nc.alloc_psum_tensor("pf%d" % i, [128, 512], F32), "pf%d" % i) for i in range(6)]
    k.pb = [Buf(nc.alloc_psum_tensor("pb%d" % i, [128, 1024], BF16), "pb%d" % i) for i in range(2)]

    A = k.A
    S = k.S

    def const(name, src, shape, dt=F32):
        b = A.alloc(name, shape, dt)
        S.dma(b[:], src, writes=[b])
        return b

    k.bfm = const("bfm", k.c_bfm, [128, NL * 64])
    k.bif = const("bif", k.c_bif, [8, NL])
    k.cqk = const("cqk", k.c_cqk, [128, NL * 64])
    k.cff = const("cff", k.c_cff, [128, NL * 132])
    k.rb31 = const("rb31", k.c_rb31, [128, 8])
    k.identb = const("identb", k.c_identb, [128, 128], BF16)
    k.identf = const("identf", k.c_identf, [128, 128])
    k.tri = const("tri", k.c_tri, [128, 128])
    k.oh4 = const("oh4", k.c_oh4, [4, 512])
    A.freeze()

    ph = phases if phases is not None else ["p0"] + [x for l in range(nl) for x in
                                                    ["p1_%d" % l, "p2a_%d" % l, "p2b_%d" % l, "p3_%d" % l,
                                                     "p4a_%d" % l, "p4b_%d" % l, "p4c_%d" % l]]
    for name in ph:
        S.barrier()
        A.reset()
        if name == "p0":
            phase0(k)
        else:
            kind, l = name.split("_")
            l = int(l)
            {"p1": phase1, "p2a": phase2a, "p2b": phase2b, "p3": phase3, "p4a": phase4a, "p4b": phase4b, "p4c": phase4c}[kind](k, l)
    S.barrier()
    with ExitStack() as st:
        block = st.enter_context(nc.Block())
        S.emit(block, st)
    k.nc = nc
    return nc, k


def phase0(k):
    S, A, T = k.S, k.A, k.T
    xb = Rot([A.alloc("xb", [128, 8, 512], BF16) for _ in range(3)])
    xf = Rot([A.alloc("xf", [128, 4, 1024], F32) for _ in range(2)])
    for g in range(k.NG):
        b = xb.next()
        S.dma(b[:], k.xT_in[:, g * 512:(g + 1) * 512].rearrange("(k p) t -> p k t", p=128), writes=[b], q="pool")
        S.dma(k.xT_d[:, g * 512:(g + 1) * 512].rearrange("(k p) t -> p k t", p=128), b[:], reads=[b])
        f = xf.next()
        S.dma(f[:], k.x_in[g * 512:(g + 1) * 512, :].rearrange("(j p) d -> p j d", p=128), writes=[f])
        S.dma(k.x_d[g * 512:(g + 1) * 512, :].rearrange("(j p) d -> p j d", p=128), f[:], reads=[f], q="pool")


FM_COL0 = {"qm": 0, "km": 1024, "qb": 4104, "kb": 5128, "ga": 7176, "gb": 8200}
FM_IDX0 = {"qm": 0, "km": 8, "qb": 16, "kb": 24, "ga": 32, "gb": 40}
TM_COL0 = {"vm": 2048, "om": 3072, "vb": 6152}
TM_IDX0 = {"vm": 0, "om": 1024, "vb": 2048}


def phase1(k, l):
    S, A, T, NG = k.S, k.A, k.T, k.NG
    MM, TR, ACT, TT, TS, STT, CP, MS = k.MM, k.TR, k.ACT, k.TT, k.TS, k.STT, k.CP, k.MS
    wt = Rot([A.alloc("w", [128, 8, 1024], BF16) for _ in range(2)])
    wif = A.alloc("wif", [128, 8, 8], BF16)
    btm = A.alloc("btm", [128, 3072], F32)
    xt = Rot([A.alloc("xt", [128, 8, 512], BF16) for _ in range(3)])
    cb = Rot([A.alloc("cb", [128, 515], F32) for _ in range(3)])
    acc = Rot([A.alloc("acc", [128, 512], F32) for _ in range(4)])
    ob = Rot([A.alloc("ob", [128, 512], BF16) for _ in range(6)])
    hal = [A.alloc("hal%d" % c, [128, 3], F32) for c in range(8)]
    tkb = Rot([A.alloc("tkb", [128, 4, 1024], BF16) for _ in range(2)])
    tmo = Rot([A.alloc("tmo", [128, 1024], BF16) for _ in range(3)])
    tmf = Rot([A.alloc("tmf", [128, 512], F32) for _ in range(2)])
    ifb = Rot([A.alloc("ifb", [8, 512], F32) for _ in range(2)])
    pf = Rot(k.pf[0:5])
    pbr = Rot(k.pb)

    S.dma(btm[:], k.c_btm[:, l * 3072:(l + 1) * 3072], writes=[btm])
    S.dma(wif[:], k.w_in[l, :, 4096:4104].rearrange("(k p) n -> p k n", p=128), writes=[wif], q="pool")

    blocks = [("conv", "qm", k.qmT_d), ("conv", "km", k.kmT_d), ("fm", "qb", k.qbT_d), ("fm", "kb", k.kbT_d),
              ("fm", "ga", k.gaT_d), ("fm", "gb", k.gbT_d), ("tm", "vm", k.vm_d), ("tm", "om", k.om_d),
              ("tm", "vb", k.vb_d)]

    def load_w(name):
        c0 = FM_COL0[name] if name in FM_COL0 else TM_COL0[name]
        w = wt.next()
        S.dma(w[:], k.w_in[l, :, c0:c0 + 1024].rearrange("(k p) n -> p k n", p=128), writes=[w], q="pool")
        return w

    def load_x(g):
        x = xt.next()
        S.dma(x[:], k.xT_d[:, g * 512:(g + 1) * 512].rearrange("(k p) t -> p k t", p=128), writes=[x])
        return x

    wnext = load_w(blocks[0][1])
    dq = Defer()
    tq = Defer()
    for bi, (kind, name, dest) in enumerate(blocks):
        dq.flush()
        w = wnext
        if bi + 1 < len(blocks):
            wnext = load_w(blocks[bi + 1][1])
        xnext = load_x(0)
        for g in range(NG):
            x = xnext
            if g + 1 < NG:
                xnext = load_x(g + 1)
            tsl = slice(g * 512, (g + 1) * 512)
            if kind in ("conv", "fm"):
                if name == "km":
                    tk = tkb.next()
                for cc in range(8):
                    ps = pf.next()
                    for kk in range(8):
                        MM(ps[:, 0:512], w[:, kk, cc * 128:(cc + 1) * 128], x[:, kk, :], kk == 0, kk == 7,
                           [w, x], [ps], sig=(kk == 7))
                    dq.flush(keep=2)
                    bcol = l * 64 + FM_IDX0[name] + cc
                    bias = k.bfm[:, bcol:bcol + 1]
                    o = ob.next()
                    if kind == "fm":
                        func = AF.Sigmoid if name in ("ga", "gb") else AF.Identity
                        ACT(o[:], ps[:, 0:512], func, [ps, k.bfm], [o], bias=bias)
                    else:
                        c = cb.next()
                        a = acc.next()
                        h = hal[cc]
                        ACT(c[:, 3:515], ps[:, 0:512], AF.Identity, [ps, k.bfm], [c], bias=bias)
                        if g == 0:
                            MS("pool", c[:, 0:3], 0.0, [c])
                        else:
                            CP("pool", c[:, 0:3], h[:], [h], [c])
                        ci = l * 64 + (FM_IDX0[name] + cc) * 4
                        TS("dve", a[:], c[:, 3:515], k.cqk[:, ci + 3:ci + 4], None, ALU.mult, None, [c, k.cqk], [a])
                        for j in (2, 1, 0):
                            STT("dve", a[:], c[:, j:j + 512], k.cqk[:, ci + j:ci + j + 1], a[:], ALU.mult, ALU.add,
                                [c, a, k.cqk], [a])
                        CP("pool", h[:], c[:, 512:515], [c], [h])

                        def tail(o=o, a=a, cc=cc, tk=(tk if name == "km" else None), tsl=tsl, name=name, dest=dest):
                            ACT(o[:], a[:], AF.Silu, [a], [o])
                            r0_ = (FM_IDX0[name] % 8 + cc) * 128
                            S.dma(dest[r0_:r0_ + 128, tsl], o[:], reads=[o], q="sp")
                            if tk is not None:
                                def fin(o=o, tk=tk, cc=cc):
                                    pb = pbr.next()
                                    for j in range(4):
                                        TR(pb[:, j * 128:(j + 1) * 128], o[:, j * 128:(j + 1) * 128], k.identb[:],
                                           [o, k.identb], [pb], sig=(j == 3))
                                    CP("act" if cc % 2 else "dve", tk[:, :, cc * 128:(cc + 1) * 128],
                                       pb[:, 0:512].rearrange("p (j f) -> p j f", f=128), [pb], [tk])
                                dq.push(fin)
                        tq.push(tail)
                        tq.flush(keep=1)
                        continue
                    r0 = (FM_IDX0[name] % 8 + cc) * 128
                    S.dma(dest[r0:r0 + 128, tsl], o[:], reads=[o], q="sp")
                if kind == "conv":
                    tq.flush()
                if name == "km":
                    def store(tk=tk, tsl=tsl):
                        S.dma(k.km_d[tsl, :].rearrange("(j p) f -> p j f", p=128), tk[:], reads=[tk], q="sp")
                    dq.push(store)
                if name == "qm":
                    ps = pf.next()
                    for kk in range(8):
                        MM(ps[0:8, 0:512], wif[:, kk, :], x[:, kk, :], kk == 0, kk == 7, [wif, x], [ps], sig=(kk == 7))
                    fb = ifb.next()
                    ACT(fb[:], ps[0:8, 0:512], AF.Identity, [ps, k.bif], [fb], bias=k.bif[:, l:l + 1])
                    S.dma(k.gates_d[:, tsl], fb[:], reads=[fb], q="sp")
            else:
                for j in range(4):
                    o = tmo.next()
                    for hf in range(2):
                        ps = pf.next()
                        for kk in range(8):
                            MM(ps[:, 0:512], x[:, kk, j * 128:(j + 1) * 128], w[:, kk, hf * 512:(hf + 1) * 512],
                               kk == 0, kk == 7, [w, x], [ps], sig=(kk == 7))
                        b0 = TM_IDX0[name] + hf * 512
                        if name == "om":
                            f = tmf.next()
                            TT("dve", f[:], ps[:, 0:512], btm[:, b0:b0 + 512], ALU.add, [ps, btm], [f])
                            ACT(o[:, hf * 512:(hf + 1) * 512], f[:], AF.Sigmoid, [f], [o])
                        else:
                            TT("dve", o[:, hf * 512:(hf + 1) * 512], ps[:, 0:512], btm[:, b0:b0 + 512], ALU.add,
                               [ps, btm], [o])
                    S.dma(dest[g * 512 + j * 128:g * 512 + (j + 1) * 128, :], o[:], reads=[o], q="sp")


def phase2a(k, l):
    S, A, T, NT = k.S, k.A, k.T, k.NT
    MM, TR, ACT, TT, TS, STT, CP, MS = k.MM, k.TR, k.ACT, k.TT, k.TS, k.STT, k.CP, k.MS
    NC = NT
    k.gtm = A.alloc("gtm", [128, NC * 4], F32)
    k.ttm = A.alloc("ttm", [128, NC * 4], F32)
    k.ebc = A.alloc("ebc", [128, 4 * NC], F32)
    ig = A.alloc("ig", [4, T], F32)
    fg = A.alloc("fg", [4, T], F32)
    ones = A.alloc("ones", [4, T], F32)
    Bn = A.alloc("Bn", [4, T], F32)
    sp = fg
    ap_ = ig
    sm = {n: A.alloc(n, [4, NC], F32) for n in
          ["Bend", "Bn0", "amax", "dB", "dBp", "Mc", "Mcp", "e", "off1", "off2", "t1"]}
    S.dma(ig[:], k.gates_d[0:4, :], writes=[ig])
    S.dma(fg[:], k.gates_d[4:8, :], writes=[fg])
    MS("dve", ones[:], 1.0, [ones])
    ACT(sp[:], fg[:], AF.Exp, [fg], [sp], scale=-1.0)
    ACT(sp[:], sp[:], AF.Ln, [sp], [sp], bias=1.0)
    assert sp is fg and ap_ is ig
    S.op("dve", lambda e: e.tensor_tensor_scan(out=Bn[:], data0=ones[:], data1=sp[:], initial=0.0,
                                               op0=ALU.mult, op1=ALU.add), [ones, sp], [Bn])
    TT("dve", ap_[:], ig[:], Bn[:], ALU.add, [ig, Bn], [ap_])
    Bend, Bn0, amax, dB, dBp, Mc, Mcp, e_, off1, off2, t1 = [sm[n] for n in
                                                             ["Bend", "Bn0", "amax", "dB", "dBp", "Mc", "Mcp", "e",
                                                              "off1", "off2", "t1"]]
    CP("dve", Bend[:], Bn[:, 127::128], [Bn], [Bend])
    MS("dve", Bn0[:], 0.0, [Bn0])
    if NC > 1:
        CP("dve", Bn0[:, 1:NC], Bend[:, 0:NC - 1], [Bend], [Bn0])
    S.op("dve", lambda e: e.tensor_reduce(out=amax[:], in_=ap_[:].rearrange("p (c s) -> p c s", s=128),
                                          axis=AX.X, op=ALU.max), [ap_], [amax])
    TT("dve", amax[:], amax[:], Bn0[:], ALU.subtract, [amax, Bn0], [amax])
    TT("dve", dB[:], Bn0[:], Bend[:], ALU.subtract, [Bn0, Bend], [dB])
    MS("dve", dBp[:], 0.0, [dBp])
    if NC > 1:
        CP("dve", dBp[:, 1:NC], dB[:, 0:NC - 1], [dB], [dBp])
    S.op("dve", lambda e: e.tensor_tensor_scan(out=Mc[:], data0=dBp[:], data1=amax[:], initial=0.0,
                                               op0=ALU.add, op1=ALU.max), [dBp, amax], [Mc])
    MS("dve", Mcp[:], 0.0, [Mcp])
    if NC > 1:
        CP("dve", Mcp[:, 1:NC], Mc[:, 0:NC - 1], [Mc], [Mcp])
    TT("dve", t1[:], dBp[:], Mcp[:], ALU.add, [dBp, Mcp], [t1])
    TT("dve", t1[:], t1[:], Mc[:], ALU.subtract, [t1, Mc], [t1])
    ACT(e_[:], t1[:], AF.Exp, [t1], [e_])
    TT("dve", off2[:], Bn0[:], Mc[:], ALU.add, [Bn0, Mc], [off2])
    TS("dve", off1[:], off2[:], math.log(16.0), None, ALU.add, None, [off2], [off1])
    v3 = "p (c s) -> p c s"
    TT("dve", ap_[:].rearrange(v3, s=128), ap_[:].rearrange(v3, s=128),
       off1[:].rearrange("p (c o) -> p c o", o=1).broadcast_to([4, NC, 128]), ALU.subtract, [ap_, off1], [ap_])
    TT("dve", Bn[:].rearrange(v3, s=128), Bn[:].rearrange(v3, s=128),
       off2[:].rearrange("p (c o) -> p c o", o=1).broadcast_to([4, NC, 128]), ALU.subtract, [Bn, off2], [Bn])
    ACT(ap_[:], ap_[:], AF.Exp, [ap_], [ap_])
    ACT(Bn[:], Bn[:], AF.Exp, [Bn], [Bn])
    pg, pt, pe = k.pf[0], k.pf[1], k.pf[2]
    for c in range(NC):
        TR(pg[:, c * 4:(c + 1) * 4], ap_[:, c * 128:(c + 1) * 128], k.identf[0:4, 0:4], [ap_, k.identf], [pg],
           sig=(c == NC - 1))
    for c in range(NC):
        TR(pt[:, c * 4:(c + 1) * 4], Bn[:, c * 128:(c + 1) * 128], k.identf[0:4, 0:4], [Bn, k.identf], [pt],
           sig=(c == NC - 1))
    for h in range(4):
        MM(pe[:, h * NC:(h + 1) * NC], k.oh4[:, h * 128:(h + 1) * 128], e_[:], True, True, [k.oh4, e_], [pe],
           sig=(h == 3))
    CP("dve", k.gtm[:], pg[:, 0:NC * 4], [pg], [k.gtm])
    CP("dve", k.ttm[:], pt[:, 0:NC * 4], [pt], [k.ttm])
    CP("dve", k.ebc[:], pe[:, 0:NC * 4], [pe], [k.ebc])


def phase2b(k, l):
    S, A, T, NT = k.S, k.A, k.T, k.NT
    MM, TR, ACT, TT, TS, STT, CP, MS = k.MM, k.TR, k.ACT, k.TT, k.TS, k.STT, k.CP, k.MS
    NC = NT
    gtm = A.alloc("gtm", [128, NC * 4], F32)
    ttm = A.alloc("ttm", [128, NC * 4], F32)
    ebc = A.alloc("ebc", [128, 4 * NC], F32)
    gtm.w, ttm.w, ebc.w = k.gtm.w, k.ttm.w, k.ebc.w
    gain = A.alloc("gain", [128, 1024], F32)
    S.dma(gain[:], k.c_gain[:, l * 1024:(l + 1) * 1024], writes=[gain])
    qT4 = Rot([A.alloc("qT4", [128, 8, 512], BF16) for _ in range(2)])
    kT4 = Rot([A.alloc("kT4", [128, 8, 512], BF16) for _ in range(2)])
    k4 = Rot([A.alloc("k4", [128, 4, 1024], BF16) for _ in range(2)])
    v4 = Rot([A.alloc("v4", [128, 4, 1024], BF16) for _ in range(2)])
    o4 = Rot([A.alloc("o4", [128, 4, 1024], BF16) for _ in range(2)])
    yT4 = Rot([A.alloc("yT4", [128, 8, 512], BF16) for _ in range(2)])
    C = [A.alloc("C%d" % h, [128, 2, 257], F32) for h in range(4)]
    Cs = [Rot([A.alloc("Cs%d" % h, [128, 2, 257], BF16) for _ in range(2)]) for h in range(4)]
    sTm = Rot([A.alloc("sTm", [128, 128], BF16) for _ in range(3)])
    vg = Rot([A.alloc("vg", [128, 257], BF16) for _ in range(3)])
    hh = Rot([A.alloc("hh", [128, 256], F32) for _ in range(3)])
    hn = Rot([A.alloc("hn", [128, 256], F32) for _ in range(3)])
    ya = Rot([A.alloc("ya", [128, 256], BF16) for _ in range(4)])
    st6 = Rot([A.alloc("st6", [128, 6], F32) for _ in range(3)])
    mv = Rot([A.alloc("mv", [128, 4], F32) for _ in range(3)])
    sets = [(k.pf[0], k.pf[1], k.pf[2]), (k.pf[3], k.pf[4], k.pf[5])]
    sT_tok = [Buf(k.pf[0].t, "sT0"), Buf(k.pf[3].t, "sT1")]
    out_tok = [Buf(k.pf[0].t, "out0"), Buf(k.pf[3].t, "out1")]
    pbr = Rot(k.pb)
    for h in range(4):
        MS("pool", C[h][:], 0.0, [C[h]])

    def load(g):
        tsl = slice(g * 512, (g + 1) * 512)
        a, b, c_, d, e = qT4.next(), kT4.next(), k4.next(), v4.next(), o4.next()
        S.dma(a[:], k.qmT_d[:, tsl].rearrange("(k p) t -> p k t", p=128), writes=[a])
        S.dma(b[:], k.kmT_d[:, tsl].rearrange("(k p) t -> p k t", p=128), writes=[b])
        S.dma(c_[:], k.km_d[tsl, :].rearrange("(j p) f -> p j f", p=128), writes=[c_])
        S.dma(d[:], k.vm_d[tsl, :].rearrange("(j p) f -> p j f", p=128), writes=[d])
        S.dma(e[:], k.om_d[tsl, :].rearrange("(j p) f -> p j f", p=128), writes=[e])
        return a, b, c_, d, e

    nxt = load(0)
    it = 0
    dq = Defer()
    for g in range(k.NG):
        q_, kT_, k_, v_, o_ = nxt
        if g + 1 < k.NG:
            nxt = load(g + 1)
        yT = yT4.next()
        for j in range(4):
            c = g * 4 + j
            tj = slice(j * 128, (j + 1) * 128)
            for h in range(4):
                p = it % 2
                it += 1
                bk_a, bk_u0, bk_u1 = sets[p]
                sTt, outt = sT_tok[p], out_tok[p]
                gcol = gtm[:, c * 4 + h:c * 4 + h + 1]
                tcol = ttm[:, c * 4 + h:c * 4 + h + 1]
                ecol = ebc[:, h * NC + c:h * NC + c + 1]
                for jj in range(2):
                    MM(bk_a[:, 0:128], kT_[:, 2 * h + jj, tj], q_[:, 2 * h + jj, tj], jj == 0, jj == 1,
                       [kT_, q_], [sTt], sig=(jj == 1))
                dq.flush(keep=1)
                sm_ = sTm.next()
                TT("dve", sm_[:], bk_a[:, 0:128], k.tri[:], ALU.mult, [sTt, k.tri], [sm_])
                vg_ = vg.next()
                ACT(vg_[:, 0:256], v_[:, j, h * 256:(h + 1) * 256], AF.Identity, [v_, gtm], [vg_], scale=gcol)
                ACT(vg_[:, 256:257], gcol, AF.Identity, [gtm], [vg_])
                Ch = C[h]
                cs = Cs[h].next()
                ACT(cs[:], Ch[:], AF.Identity, [Ch, ebc], [cs], scale=ecol)
                for jj in range(2):
                    MM(bk_a[:, 128:385], q_[:, 2 * h + jj, tj], cs[:, jj, :], jj == 0, False, [q_, cs], [outt], sig=False)
                MM(bk_a[:, 128:385], sm_[:], vg_[:], False, True, [sm_, vg_], [outt])
                for jj, bk in ((0, bk_u0), (1, bk_u1)):
                    MM(bk[:, 0:257], k_[:, j, (2 * h + jj) * 128:(2 * h + jj + 1) * 128], vg_[:], True, True,
                       [k_, vg_], [bk])
                    STT("dve", Ch[:, jj, :], Ch[:, jj, :], ecol, bk[:, 0:257], ALU.mult, ALU.add, [Ch, bk, ebc], [Ch])
                m_ = mv.next()
                ACT(m_[:, 0:1], bk_a[:, 384:385], AF.Abs, [outt], [m_])
                TT("dve", m_[:, 0:1], m_[:, 0:1], tcol, ALU.max, [m_, ttm], [m_])
                S.op("dve", lambda e, m_=m_: e.reciprocal(out=m_[:, 1:2], in_=m_[:, 0:1]), [m_], [m_])
                hh_ = hh.next()
                ACT(hh_[:], bk_a[:, 128:384], AF.Identity, [outt, m_], [hh_], scale=m_[:, 1:2])
                s6 = st6.next()
                S.op("dve", lambda e, s6=s6, hh_=hh_: e.bn_stats(out=s6[:], in_=hh_[:]), [hh_], [s6])
                S.op("dve", lambda e, s6=s6, m_=m_: e.bn_aggr(out=m_[:, 2:4], in_=s6[:]), [s6], [m_])
                TS("dve", m_[:, 3:4], m_[:, 3:4], EPS, None, ALU.add, None, [m_], [m_])
                ACT(m_[:, 3:4], m_[:, 3:4], AF.Sqrt, [m_], [m_])
                S.op("dve", lambda e, m_=m_: e.reciprocal(out=m_[:, 3:4], in_=m_[:, 3:4]), [m_], [m_])
                STT("dve", m_[:, 2:3], m_[:, 2:3], -1.0, m_[:, 3:4], ALU.mult, ALU.mult, [m_], [m_])
                hn_ = hn.next()
                ACT(hn_[:], hh_[:], AF.Identity, [hh_, m_], [hn_], bias=m_[:, 2:3], scale=m_[:, 3:4])
                TT("dve", hn_[:], hn_[:], gain[:, h * 256:(h + 1) * 256], ALU.mult, [hn_, gain], [hn_])
                ya_ = ya.next()
                TT("dve", ya_[:], hn_[:], o_[:, j, h * 256:(h + 1) * 256], ALU.mult, [hn_, o_], [ya_])
                def fin(ya_=ya_, yT=yT, h=h, tj=tj):
                    pb = pbr.next()
                    for jj in range(2):
                        TR(pb[:, jj * 128:(jj + 1) * 128], ya_[:, jj * 128:(jj + 1) * 128], k.identb[:],
                           [ya_, k.identb], [pb], sig=(jj == 1))
                    CP("act", yT[:, 2 * h:2 * h + 2, tj], pb[:, 0:256].rearrange("p (a t) -> p a t", t=128), [pb],
                       [yT])
                dq.push(fin)

        def store(yT=yT, g=g):
            S.dma(k.yaT_d[:, g * 512:(g + 1) * 512].rearrange("(k p) t -> p k t", p=128), yT[:], reads=[yT], q="pool")
        dq.push(store)
    dq.flush()


def phase3(k, l):
    S, A, T, NT, NG, NB = k.S, k.A, k.T, k.NT, k.NG, k.NB
    MM, TR, ACT, TT, TS, STT, CP, MS = k.MM, k.TR, k.ACT, k.TT, k.TS, k.STT, k.CP, k.MS
    bnear = A.alloc("bnear", [128, 8 * 2 * 128], F32)
    cmask = A.alloc("cmask", [128, 128], F32)
    maskneg = A.alloc("maskneg", [128, 32 * 32], F32)
    E = A.alloc("E", [32, 32 * 128], BF16)
    S.dma(bnear[:], k.c_bnear, writes=[bnear])
    S.dma(cmask[:], k.c_cmask, writes=[cmask])
    S.dma(maskneg[:], k.c_maskneg, writes=[maskneg])
    S.dma(E[:], k.c_E, writes=[E])
    for h in range(8):
        o0 = (h * 2) * 128
        TT("dve", bnear[:, o0:o0 + 128], bnear[:, o0:o0 + 128], cmask[:], ALU.add, [bnear, cmask], [bnear])
    qT = Rot([A.alloc("qTh", [128, T], BF16) for _ in range(2)])
    kT = Rot([A.alloc("kTh", [128, T], BF16) for _ in range(2)])
    vh = Rot([A.alloc("vh", [128, NT, 129], BF16) for _ in range(2)])
    addT = Rot([A.alloc("addT", [32, T], BF16) for _ in range(2)])
    kmf = Rot([A.alloc("kmf", [128, 32], F32) for _ in range(2)])
    kmb = Rot([A.alloc("kmb", [128, 32], BF16) for _ in range(2)])
    gm = Rot([A.alloc("gm", [128, 32], F32) for _ in range(3)])
    t8 = Rot([A.alloc("t8", [128, 8], F32) for _ in range(3)])
    aq = Rot([A.alloc("aq", [128, 32], F32) for _ in range(3)])
    aqb = Rot([A.alloc("aqb", [128, 32], BF16) for _ in range(4)])
    PT = Rot([A.alloc("PT", [128, 512], BF16) for _ in range(4)])
    tmpn = Rot([A.alloc("tmpn", [128, 128], F32) for _ in range(4)])
    rden = Rot([A.alloc("rden", [128, 512], F32) for _ in range(2)])
    yb = Rot([A.alloc("yb", [128, 512], BF16) for _ in range(2)])
    onesb = A.alloc("onesb", [128, 128], BF16)
    MS("pool", onesb[:], 1.0, [onesb])
    pS = Rot([k.pf[0], k.pf[1], k.pf[2]])
    pO, pD, pG = k.pf[3], k.pf[4], k.pf[5]
    pbr = Rot(k.pb)

    def load(h):
        a, b, c_ = qT.next(), kT.next(), vh.next()
        S.dma(a[:], k.qbT_d[h * 128:(h + 1) * 128, :], writes=[a])
        S.dma(b[:], k.kbT_d[h * 128:(h + 1) * 128, :], writes=[b])
        S.dma(c_[:, :, 0:128], k.vb_d[:, h * 128:(h + 1) * 128].rearrange("(n p) d -> p n d", p=128), writes=[c_])
        MS("pool", c_[:, :, 128:129], 1.0, [c_])
        return a, b, c_

    def selection_gen(q_h, kT_h, ad_h):
        kf, kb_ = kmf.next(), kmb.next()
        MS("dve", kf[:], 0.0, [kf])
        S.op("dve", lambda e: e.tensor_reduce(
            out=kf[:, 0:NB], in_=kT_h[:].rearrange("p (n s) -> p n s", s=256), axis=AX.X, op=ALU.add), [kT_h], [kf])
        TS("dve", kb_[:], kf[:], 1.0 / 256.0, None, ALU.mult, None, [kf], [kb_])
        yield
        pend = [None]
        for i0 in range(0, NT, 16):
            n_i = min(16, NT - i0)
            for ii in range(n_i):
                i = i0 + ii
                MM(pG[:, ii * 32:(ii + 1) * 32], q_h[:, i * 128:(i + 1) * 128], kb_[:], True, True, [q_h, kb_], [pG],
                   sig=(ii == n_i - 1))
            yield
            for ii in range(n_i):
                i = i0 + ii
                own = i // 2
                g_, t_, a_, ab_ = gm.next(), t8.next(), aq.next(), aqb.next()
                TT("dve", g_[:], pG[:, ii * 32:(ii + 1) * 32], maskneg[:, own * 32:(own + 1) * 32], ALU.add,
                   [pG, maskneg], [g_])
                S.op("dve", lambda e, t_=t_, g_=g_: e.max(out=t_[:], in_=g_[:]), [g_], [t_])
                TS("dve", a_[:], g_[:], t_[:, 3:4], BIG, ALU.is_ge, ALU.mult, [g_, t_], [a_])
                TS("dve", ab_[:], a_[:], -BIG, None, ALU.add, None, [a_], [ab_])
                if pend[0] is not None:
                    pend[0]()

                def fin(ab_=ab_, i=i):
                    pb = pbr.next()
                    TR(pb[0:32, 0:128], ab_[:], k.identb[:], [ab_, k.identb], [pb])
                    CP("act", ad_h[:, i * 128:(i + 1) * 128], pb[0:32, 0:128], [pb], [ad_h])
                pend[0] = fin
                yield
        if pend[0] is not None:
            pend[0]()
        yield

    nxt = load(0)
    ad_nxt = addT.next()
    for _ in selection_gen(nxt[0], nxt[1], ad_nxt):
        pass
    for h in range(8):
        q_, kT_, v_ = nxt
        ad = ad_nxt
        gen = None
        if h + 1 < 8:
            nxt = load(h + 1)
            ad_nxt = addT.next()
            gen = selection_gen(nxt[0], nxt[1], ad_nxt)
        items = [(G, kt) for G in range(NG) for kt in range(4 * G + 4)]

        def stageA(i):
            G, kt = items[i]
            n = kt // 2
            j0 = max(kt, 4 * G)
            q0, q1 = j0 * 128, (4 * G + 4) * 128
            nq = q1 - q0
            ps = pS.next()
            need_sel = (kt < 4 * G + 2) and (G >= 2)
            MM(ps[:, 0:nq], kT_[:, kt * 128:(kt + 1) * 128], q_[:, q0:q1], True, not need_sel, [kT_, q_], [ps],
               sig=not need_sel)
            if need_sel:
                MM(ps[:, 0:nq], E[:, n * 128:(n + 1) * 128], ad[:, q0:q1], False, True, [E, ad], [ps])
            return dict(G=G, kt=kt, j0=j0, q0=q0, nq=nq, c0=q0 - G * 512, ps=ps)

        def stageB(st_):
            G, kt, j0, nq, ps = st_["G"], st_["kt"], st_["j0"], st_["nq"], st_["ps"]
            pt = PT.next()
            col = 0
            for j in range(j0, 4 * G + 4):
                typ = 0 if j == kt else (1 if j == kt + 1 else 2)
                if typ == 2:
                    break
                tn = tmpn.next()
                bo = (h * 2 + typ) * 128
                STT("dve", tn[:], ps[:, col:col + 128], SCALE_B, bnear[:, bo:bo + 128], ALU.mult, ALU.add,
                    [ps, bnear], [tn])
                ACT(pt[:, col:col + 128], tn[:], AF.Exp, [tn], [pt])
                col += 128
            if col < nq:
                ACT(pt[:, col:nq], ps[:, col:nq], AF.Exp, [ps, k.rb31], [pt], bias=k.rb31[:, h:h + 1], scale=SCALE_B)
            st_["pt"] = pt

        def stageC(st_):
            G, kt, nq, c0, pt = st_["G"], st_["kt"], st_["nq"], st_["c0"], st_["pt"]
            nkt = 4 * G + 4
            MM(pO[:, c0:c0 + nq], v_[:, kt, 0:128], pt[:, 0:nq], kt == 0, kt == nkt - 1, [v_, pt], [pO],
               sig=False)
            MM(pD[:, c0:c0 + nq], onesb[:], pt[:, 0:nq], kt == 0, kt == nkt - 1, [onesb, pt], [pD],
               sig=True)
            if kt == nkt - 1:
                rd, y_ = rden.next(), yb.next()
                S.op("dve", lambda e, rd=rd: e.reciprocal(out=rd[:], in_=pD[:, 0:512]), [pD], [rd])
                TT("dve", y_[:], pO[:, 0:512], rd[:], ALU.mult, [pO, rd], [y_])
                S.dma(k.ybT_d[h * 128:(h + 1) * 128, G * 512:(G + 1) * 512], y_[:], reads=[y_], q="pool")

        LOOK = 2
        sts = {}
        for i in range(min(LOOK, len(items))):
            sts[i] = stageA(i)
        for i in range(len(items)):
            stageB(sts[i])
            if i + LOOK < len(items):
                sts[i + LOOK] = stageA(i + LOOK)
            stageC(sts.pop(i))
            if gen is not None and i % 7 == 3:
                if next(gen, "done") == "done":
                    gen = None
        if gen is not None:
            for _ in gen:
                pass


def layer_norm_tile(k, z, lnp, gi, outf, st, mv):
    S, TS, TT, STT, ACT = k.S, k.TS, k.TT, k.STT, k.ACT
    for i in range(2):
        S.op("dve", lambda e, i=i: e.bn_stats(out=st[:, i * 6:(i + 1) * 6], in_=z[:, i * 512:(i + 1) * 512]), [z], [st])
    S.op("dve", lambda e: e.bn_aggr(out=mv[:, 0:2], in_=st[:, 0:12].rearrange("p (a b) -> p a b", b=6)), [st], [mv])
    TS("dve", mv[:, 1:2], mv[:, 1:2], EPS, None, ALU.add, None, [mv], [mv])
    ACT(mv[:, 1:2], mv[:, 1:2], AF.Sqrt, [mv], [mv])
    S.op("dve", lambda e: e.reciprocal(out=mv[:, 1:2], in_=mv[:, 1:2]), [mv], [mv])
    STT("dve", mv[:, 0:1], mv[:, 0:1], -1.0, mv[:, 1:2], ALU.mult, ALU.mult, [mv], [mv])
    ACT(z[:], z[:], AF.Identity, [z, mv], [z], bias=mv[:, 0:1], scale=mv[:, 1:2])
    TT("dve", z[:], z[:], lnp[:, gi * 1024:(gi + 1) * 1024], ALU.mult, [z, lnp], [z])
    TT("dve", outf[:], z[:], lnp[:, (gi + 1) * 1024:(gi + 2) * 1024], ALU.add, [z, lnp], [outf])


def phase4a(k, l):
    S, A, T, NG = k.S, k.A, k.T, k.NG
    MM, TR, ACT, TT, TS, STT, CP, MS = k.MM, k.TR, k.ACT, k.TT, k.TS, k.STT, k.CP, k.MS
    wa = A.alloc("wa", [128, 8, 1024], BF16)
    wb = A.alloc("wb", [128, 8, 1024], BF16)
    wo = A.alloc("wo", [128, 8, 1024], BF16)
    lnp = A.alloc("lnp", [128, 2048], F32)
    S.dma(wa[:], k.w_a[l].rearrange("(k p) n -> p k n", p=128), writes=[wa], q="pool")
    S.dma(wb[:], k.w_b[l].rearrange("(k p) n -> p k n", p=128), writes=[wb], q="pool")
    S.dma(wo[:], k.w_out[l].rearrange("(k p) n -> p k n", p=128), writes=[wo], q="pool")
    S.dma(lnp[:], k.c_ln[:, (l * 4) * 1024:(l * 4 + 2) * 1024], writes=[lnp])
    ins = [Rot([A.alloc(n, [128, 8, 512], BF16) for _ in range(2)]) for n in ("yaT", "ybT", "gaT", "gbT")]
    xin = Rot([A.alloc("xin", [128, 1024], F32) for _ in range(4)])
    mT = Rot([A.alloc("mT", [128, 8, 512], BF16) for _ in range(1)])
    t1 = Rot([A.alloc("t1", [128, 512], F32) for _ in range(2)])
    t2 = Rot([A.alloc("t2", [128, 512], F32) for _ in range(2)])
    z = Rot([A.alloc("z", [128, 1024], F32) for _ in range(2)])
    x1 = Rot([A.alloc("x1", [128, 1024], F32) for _ in range(2)])
    x1b = Rot([A.alloc("x1b", [128, 1024], BF16) for _ in range(3)])
    xT = Rot([A.alloc("x1T", [128, 8, 512], BF16) for _ in range(2)])
    st = Rot([A.alloc("st", [128, 12], F32) for _ in range(2)])
    mv = Rot([A.alloc("mv", [128, 2], F32) for _ in range(2)])
    pf = Rot(k.pf)
    pbr = Rot(k.pb)
    xsrc = k.x_d

    def load(g):
        tsl = slice(g * 512, (g + 1) * 512)
        bs = [r.next() for r in ins]
        for b, d in zip(bs, (k.yaT_d, k.ybT_d, k.gaT_d, k.gbT_d)):
            S.dma(b[:], d[:, tsl].rearrange("(k p) t -> p k t", p=128), writes=[b])
        return bs

    nxt = load(0)
    dq = Defer()
    for g in range(NG):
        ya, yb, ga, gb = nxt
        if g + 1 < NG:
            nxt = load(g + 1)
        xis = []
        for j in range(4):
            xi = xin.next()
            r0 = g * 512 + j * 128
            S.dma(xi[:], xsrc[r0:r0 + 128, :], writes=[xi])
            xis.append(xi)
        m = mT.next()
        for oc in range(8):
            pa, pb_ = pf.next(), pf.next()
            for kk in range(8):
                MM(pa[:, 0:512], wa[:, kk, oc * 128:(oc + 1) * 128], ya[:, kk, :], kk == 0, kk == 7, [wa, ya], [pa],
                   sig=(kk == 7))
            for kk in range(8):
                MM(pb_[:, 0:512], wb[:, kk, oc * 128:(oc + 1) * 128], yb[:, kk, :], kk == 0, kk == 7, [wb, yb], [pb_],
                   sig=(kk == 7))
            a_, b_ = t1.next(), t2.next()
            TT("dve", a_[:], pa[:, 0:512], ga[:, oc, :], ALU.mult, [pa, ga], [a_])
            TT("dve", b_[:], pb_[:, 0:512], gb[:, oc, :], ALU.mult, [pb_, gb], [b_])
            TT("dve", m[:, oc, :], a_[:], b_[:], ALU.add, [a_, b_], [m])
        xt_ = xT.next()
        for j in range(4):
            z_ = z.next()
            for hf in range(2):
                ps = pf.next()
                for kk in range(8):
                    MM(ps[:, 0:512], m[:, kk, j * 128:(j + 1) * 128], wo[:, kk, hf * 512:(hf + 1) * 512], kk == 0,
                       kk == 7, [m, wo], [ps], sig=(kk == 7))
                if hf == 1:
                    dq.flush(keep=0)
                STT("dve", z_[:, hf * 512:(hf + 1) * 512], xis[j][:, hf * 512:(hf + 1) * 512], ALPHA, ps[:, 0:512],
                    ALU.mult, ALU.add, [xis[j], ps], [z_])
            x1_ = x1.next()
            layer_norm_tile(k, z_, lnp, 0, x1_, st.next(), mv.next())
            r0 = g * 512 + j * 128
            S.dma(k.x1_d[r0:r0 + 128, :], x1_[:], reads=[x1_], q="pool")
            xb_ = x1b.next()
            CP("act", xb_[:], x1_[:], [x1_], [xb_])

            def fin(xb_=xb_, xt_=xt_, j=j):
                pb = pbr.next()
                for c in range(8):
                    TR(pb[:, c * 128:(c + 1) * 128], xb_[:, c * 128:(c + 1) * 128], k.identb[:], [xb_, k.identb],
                       [pb], sig=(c == 7))
                CP("act", xt_[:, :, j * 128:(j + 1) * 128], pb[:, 0:1024].rearrange("p (c t) -> p c t", t=128), [pb],
                   [xt_])
            dq.push(fin)

        def store(xt_=xt_, g=g):
            S.dma(k.x1T_d[:, g * 512:(g + 1) * 512].rearrange("(k p) t -> p k t", p=128), xt_[:], reads=[xt_],
                  q="pool")
        dq.push(store)
    dq.flush()


def phase4b(k, l):
    S, A, T, NG = k.S, k.A, k.T, k.NG
    MM, TR, ACT, TT, TS, STT, CP, MS = k.MM, k.TR, k.ACT, k.TT, k.TS, k.STT, k.CP, k.MS
    wu = A.alloc("wu", [128, 8, 2 * DFF], BF16)
    for c4 in range(4):
        c0 = c4 * 1408
        S.dma(wu[:, :, c0:c0 + 1408], k.w_up[l, :, c0:c0 + 1408].rearrange("(k p) n -> p k n", p=128), writes=[wu],
              q="pool")
    hal = A.alloc("hal", [128, 44, 2], F32)
    xTi = Rot([A.alloc("xTi", [128, 8, 512], BF16) for _ in range(2)])
    cb = Rot([A.alloc("cb", [128, 514], F32) for _ in range(4)])
    acc = Rot([A.alloc("acc", [128, 512], F32) for _ in range(4)])
    sg = Rot([A.alloc("sg", [128, 512], F32) for _ in range(2)])
    ho = Rot([A.alloc("ho", [128, 512], BF16) for _ in range(4)])
    pf = Rot(k.pf)

    def load(g):
        a = xTi.next()
        S.dma(a[:], k.x1T_d[:, g * 512:(g + 1) * 512].rearrange("(k p) t -> p k t", p=128), writes=[a])
        return a

    def conv(ps, fcol, g, eng):
        c, a = cb.next(), acc.next()
        CP("act", c[:, 2:514], ps[:, 0:512], [ps], [c])
        if g == 0:
            MS("pool", c[:, 0:2], 0.0, [c])
        else:
            CP("pool", c[:, 0:2], hal[:, fcol, :], [hal], [c])
        ci = l * 132 + fcol * 3
        ACT(a[:], ps[:, 0:512], AF.Identity, [ps, k.cff], [a], scale=k.cff[:, ci + 2:ci + 3])
        for j in (1, 0):
            STT(eng, a[:], c[:, j:j + 512], k.cff[:, ci + j:ci + j + 1], a[:], ALU.mult, ALU.add, [c, a, k.cff], [a])
        CP("pool", hal[:, fcol, :], c[:, 512:514], [c], [hal])
        return a

    nxt = load(0)
    for g in range(NG):
        xt_in = nxt
        if g + 1 < NG:
            nxt = load(g + 1)
        for fc in range(22):
            pg, pu = pf.next(), pf.next()
            for kk in range(8):
                MM(pg[:, 0:512], wu[:, kk, fc * 128:(fc + 1) * 128], xt_in[:, kk, :], kk == 0, kk == 7, [wu, xt_in],
                   [pg], sig=(kk == 7))
            for kk in range(8):
                MM(pu[:, 0:512], wu[:, kk, DFF + fc * 128:DFF + (fc + 1) * 128], xt_in[:, kk, :], kk == 0, kk == 7,
                   [wu, xt_in], [pu], sig=(kk == 7))
            ag = conv(pg, fc, g, "dve")
            au = conv(pu, 22 + fc, g, "dve")
            s_ = sg.next()
            ACT(s_[:], ag[:], AF.Silu, [ag], [s_])
            h_ = ho.next()
            TT("dve", h_[:], s_[:], au[:], ALU.mult, [s_, au], [h_])
            S.dma(k.hT_d[fc * 128:(fc + 1) * 128, g * 512:(g + 1) * 512], h_[:], reads=[h_], q="sp")


def phase4c(k, l):
    S, A, T, NG = k.S, k.A, k.T, k.NG
    MM, TR, ACT, TT, TS, STT, CP, MS = k.MM, k.TR, k.ACT, k.TT, k.TS, k.STT, k.CP, k.MS
    last = (l == k.nl - 1)
    wd = A.alloc("wd", [128, 22, 1024], BF16)
    lnp = A.alloc("lnp", [128, 2048], F32)
    S.dma(wd[:], k.w_down[l].rearrange("(k p) n -> p k n", p=128), writes=[wd], q="pool")
    S.dma(lnp[:], k.c_ln[:, (l * 4 + 2) * 1024:(l * 4 + 4) * 1024], writes=[lnp])
    hTi = Rot([A.alloc("hTi", [128, 22, 512], BF16) for _ in range(2)])
    xin = Rot([A.alloc("xin", [128, 4, 1024], F32) for _ in range(2)])
    z = Rot([A.alloc("z", [128, 1024], F32) for _ in range(2)])
    x2 = Rot([A.alloc("x2", [128, 1024], F32) for _ in range(2)])
    x2b = Rot([A.alloc("x2b", [128, 1024], BF16) for _ in range(3)])
    xT = Rot([A.alloc("x2T", [128, 8, 512], BF16) for _ in range(2)])
    st = Rot([A.alloc("st", [128, 12], F32) for _ in range(2)])
    mv = Rot([A.alloc("mv", [128, 2], F32) for _ in range(2)])
    pf = Rot(k.pf)
    pbr = Rot(k.pb)

    def load(g):
        tsl = slice(g * 512, (g + 1) * 512)
        a, b = hTi.next(), xin.next()
        S.dma(a[:], k.hT_d[:, tsl].rearrange("(k p) t -> p k t", p=128), writes=[a])
        S.dma(b[:], k.x1_d[tsl, :].rearrange("(j p) d -> p j d", p=128), writes=[b])
        return a, b

    nxt = load(0)
    dq = Defer()
    for g in range(NG):
        hT_, xi = nxt
        if g + 1 < NG:
            nxt = load(g + 1)
        xt_ = xT.next()
        for j in range(4):
            z_ = z.next()
            for hf in range(2):
                ps = pf.next()
                for kk in range(22):
                    MM(ps[:, 0:512], hT_[:, kk, j * 128:(j + 1) * 128], wd[:, kk, hf * 512:(hf + 1) * 512], kk == 0,
                       kk == 21, [hT_, wd], [ps], sig=(kk == 21))
                if hf == 1:
                    dq.flush(keep=0)
                STT("dve", z_[:, hf * 512:(hf + 1) * 512], xi[:, j, hf * 512:(hf + 1) * 512], ALPHA, ps[:, 0:512],
                    ALU.mult, ALU.add, [xi, ps], [z_])
            x2_ = x2.next()
            layer_norm_tile(k, z_, lnp, 0, x2_, st.next(), mv.next())
            r0 = g * 512 + j * 128
            if last:
                S.dma(k.y_out[r0:r0 + 128, :], x2_[:], reads=[x2_], q="pool")
            else:
                S.dma(k.x_d[r0:r0 + 128, :], x2_[:], reads=[x2_], q="pool")
                xb_ = x2b.next()
                CP("act", xb_[:], x2_[:], [x2_], [xb_])

                def fin(xb_=xb_, xt_=xt_, j=j):
                    pb = pbr.next()
                    for c in range(8):
                        TR(pb[:, c * 128:(c + 1) * 128], xb_[:, c * 128:(c + 1) * 128], k.identb[:], [xb_, k.identb],
                           [pb], sig=(c == 7))
                    CP("act", xt_[:, :, j * 128:(j + 1) * 128], pb[:, 0:1024].rearrange("p (c t) -> p c t", t=128),
                       [pb], [xt_])
                dq.push(fin)
        if not last:
            def store(xt_=xt_, g=g):
                S.dma(k.xT_d[:, g * 512:(g + 1) * 512].rearrange("(k p) t -> p k t", p=128), xt_[:], reads=[xt_],
                      q="pool")
            dq.push(store)
    dq.flush()


def t5_bucket_np(rel):
    n = np.maximum(rel, 0)
    max_exact = NBK // 2
    nf = np.maximum(n, max_exact).astype(np.float32)
    large = max_exact + (np.log(nf / np.float32(max_exact)) / np.float32(math.log(128 / max_exact))
                         * np.float32(NBK - max_exact)).astype(np.int32)
    large = np.minimum(large, NBK - 1)
    return np.where(n < max_exact, n, large)


def host_consts(inp):
    f32 = np.float32
    b_in = np.asarray(inp["b_in"], f32)
    c = {}
    bfm = np.zeros((128, NL * 64), f32)
    for l in range(NL):
        for name, c0 in FM_COL0.items():
            for cc in range(8):
                bfm[:, l * 64 + FM_IDX0[name] + cc] = b_in[l, c0 + cc * 128:c0 + (cc + 1) * 128]
    c["c_bfm"] = bfm
    c["c_bif"] = np.ascontiguousarray(b_in[:, 4096:4104].T)
    btm = np.zeros((128, NL * 3072), f32)
    for l in range(NL):
        for name, c0 in TM_COL0.items():
            btm[:, l * 3072 + TM_IDX0[name]:l * 3072 + TM_IDX0[name] + 1024] = b_in[l, c0:c0 + 1024][None, :]
    c["c_btm"] = btm
    cq = np.asarray(inp["conv_qk"], f32)
    c["c_cqk"] = np.ascontiguousarray(cq.reshape(NL, 4, 16, 128).transpose(3, 0, 2, 1).reshape(128, NL * 64))
    c["c_gain"] = np.ascontiguousarray(np.broadcast_to(np.asarray(inp["mlstm_norm"], f32).reshape(1, NL * 1024),
                                                       (128, NL * 1024)))
    ln = np.stack([np.asarray(inp[n], f32) for n in ("ln1_g", "ln1_b", "ln2_g", "ln2_b")], axis=1)
    c["c_ln"] = np.ascontiguousarray(np.broadcast_to(ln.reshape(1, NL * 4 * 1024), (128, NL * 4 * 1024)))
    cf = np.asarray(inp["conv_ffn"], f32)
    c["c_cff"] = np.ascontiguousarray(cf.reshape(NL, 3, 44, 128).transpose(3, 0, 2, 1).reshape(128, NL * 132))
    rb = np.asarray(inp["rel_bias"], f32)
    c["c_rb31"] = np.ascontiguousarray(np.broadcast_to(rb[31][None, :], (128, 8)))
    key = np.arange(128)[:, None]
    q = np.arange(128)[None, :]
    bn = np.zeros((128, 8, 2, 128), f32)
    for typ, off in ((0, 0), (1, 128)):
        bk = t5_bucket_np(q + off - key)
        bn[:, :, typ, :] = rb[bk].transpose(0, 2, 1)
    c["c_bnear"] = bn.reshape(128, 8 * 2 * 128)
    c["c_cmask"] = np.where(q >= key, 0.0, -BIG).astype(f32)
    mn = np.zeros((32, 32), f32)
    for own in range(32):
        mn[own, own] = 1e30
        mn[own, own + 1:] = -1e30
    c["c_maskneg"] = np.ascontiguousarray(np.broadcast_to(mn.reshape(1, 1024), (128, 1024)))
    E = np.zeros((32, 32, 128), f32)
    for n in range(32):
        E[n, n, :] = 1.0
    c["c_E"] = E.reshape(32, 32 * 128).astype(ml_dtypes.bfloat16)
    c["c_identb"] = np.eye(128, dtype=f32).astype(ml_dtypes.bfloat16)
    c["c_identf"] = np.eye(128, dtype=f32)
    c["c_tri"] = (q >= key).astype(f32)
    oh = np.zeros((4, 4, 128), f32)
    for h in range(4):
        oh[h, h, :] = 1.0
    c["c_oh4"] = oh.reshape(4, 512)
    return c


_WNAMES = ("w_in", "w_branch_a", "w_branch_b", "w_out", "w_up", "w_down")


def make_in_maps(inp, n_cores=8, names=None):
    x = np.asarray(inp["x"], np.float32)
    B = x.shape[0]
    c = host_consts(inp)
    shared = {n: np.ascontiguousarray(np.asarray(inp[n], np.float32)) for n in _WNAMES}
    shared.update(c)
    maps = []
    for core in range(n_cores):
        b = core % B
        m = dict(shared)
        m["x"] = np.ascontiguousarray(x[b])
        m["xT"] = np.ascontiguousarray(x[b].T)
        if names is not None:
            m = {n: v for n, v in m.items() if n in names}
        maps.append(m)
    return maps


def kernel(**inputs):
    x = np.asarray(inputs["x"], np.float32)
    B, T, _ = x.shape
    nc, _ = build(T)
    maps = make_in_maps(inputs)
    res = run_bass_kernel_spmd(nc, maps, core_ids=list(range(8)))
    out = np.stack([np.asarray(res.results[b]["y"], np.float32) for b in range(B)], axis=0)
    return out
```

```python
import math
from contextlib import ExitStack

import numpy as np
import ml_dtypes
import concourse.bass as bass
import concourse.mybir as mybir
from concourse.bass_utils import run_bass_kernel_spmd

F32 = mybir.dt.float32
BF16 = mybir.dt.bfloat16
AF = mybir.ActivationFunctionType
ALU = mybir.AluOpType
AX = mybir.AxisListType

D = 1024
NIN = 9224
DFF = 2816
NL = 2
ALPHA = (2 * NL) ** 0.25
EPS = 1e-5
NBK = 32
BIG = 30000.0
SCALE_B = 128 ** -0.5

ENGS = ["pe", "act", "dve", "pool", "sp"]
NDMA = 24
SAME_ENGINE_SYNC = True
SBUF_BASE = 20 * 1024
SBUF_LIMIT = 222 * 1024


class Buf:
    __slots__ = ("t", "name", "w", "r")

    def __init__(self, t, name):
        self.t = t
        self.name = name
        self.w = None
        self.r = {}

    def __getitem__(self, idx):
        return self.t[idx]


class Sched:
    def __init__(self, nc):
        self.nc = nc
        self.prog = {e: [] for e in ENGS}
        self.seq = {e: 0 for e in ENGS}
        self.pending = {e: False for e in ENGS}
        self.waited = {e: {} for e in ENGS}
        self.dma_gen = [0] * NDMA
        self.dma_rr = 0
        self.sems = {}
        self.dsems = []
        self.n_inst = 0

    def _deps(self, reads, writes):
        deps = {}
        for b in list(reads) + list(writes):
            if b.w is not None:
                k, v = b.w
                if deps.get(k, 0) < v:
                    deps[k] = v
        for b in writes:
            for k, v in b.r.items():
                if deps.get(k, 0) < v:
                    deps[k] = v
        return deps

    def _emit_waits(self, eng, deps):
        wd = self.waited[eng]
        for k, v in deps.items():
            if k == eng and (eng == "pe" or not SAME_ENGINE_SYNC):
                continue
            if wd.get(k, 0) >= v:
                continue
            wd[k] = v
            self.prog[eng].append(("wait", k, v))

    def _mark(self, reads, writes, ev):
        k, v = ev
        for b in writes:
            b.w = ev
            b.r = {}
        for b in reads:
            if b.r.get(k, 0) < v:
                b.r[k] = v

    def op(self, eng, fn, reads=(), writes=(), signal=True):
        deps = self._deps(reads, writes)
        self._emit_waits(eng, deps)
        if signal:
            self.seq[eng] += 1
            self.pending[eng] = False
            ev = (eng, self.seq[eng])
        else:
            self.pending[eng] = True
            ev = (eng, self.seq[eng] + 1)
        self.prog[eng].append(("op", fn, signal))
        self._mark(reads, writes, ev)
        self.n_inst += 1

    def dma(self, out_ap, in_ap, reads=(), writes=(), q="sp"):
        deps = self._deps(reads, writes)
        k = self.dma_rr
        self.dma_rr = (self.dma_rr + 1) % NDMA
        key = ("dma", k)
        if self.dma_gen[k] > 0:
            deps[key] = max(deps.get(key, 0), 16 * self.dma_gen[k])
        self._emit_waits(q, deps)
        self.dma_gen[k] += 1
        ev = (key, 16 * self.dma_gen[k])
        self.prog[q].append(("dma", out_ap, in_ap, k))
        self._mark(reads, writes, ev)
        self.n_inst += 1

    def barrier(self):
        deps = {}
        for e in ENGS:
            assert not self.pending[e]
            if self.seq[e] > 0:
                deps[e] = self.seq[e]
        for k in range(NDMA):
            if self.dma_gen[k] > 0:
                deps[("dma", k)] = 16 * self.dma_gen[k]
        for e in ENGS:
            d = dict(deps)
            d.pop(e, None)
            self._emit_waits(e, d)

    def emit(self, block, stack):
        nc = self.nc
        for e in ENGS:
            self.sems[e] = stack.enter_context(nc.semaphore("tl_" + e))
        for k in range(NDMA):
            self.dsems.append(stack.enter_context(nc.semaphore("dq_%d" % k)))

        def sem_of(key):
            if isinstance(key, tuple):
                return self.dsems[key[1]]
            return self.sems[key]

        def run(eng_name, eng):
            mysem = self.sems[eng_name]
            for item in self.prog[eng_name]:
                kind = item[0]
                if kind == "wait":
                    eng.wait_ge(sem_of(item[1]), item[2])
                elif kind == "op":
                    inst = item[1](eng)
                    if item[2]:
                        inst.then_inc(mysem, 1)
                else:
                    eng.dma_start(out=item[1], in_=item[2]).then_inc(self.dsems[item[3]], 16)

        @block.tensor
        def _(eng):
            run("pe", eng)

        @block.scalar
        def _(eng):
            run("act", eng)

        @block.vector
        def _(eng):
            run("dve", eng)

        @block.gpsimd
        def _(eng):
            run("pool", eng)

        @block.sync
        def _(eng):
            run("sp", eng)


def _dsize(dt):
    return 2 if dt == BF16 else 4


class Arena:
    def __init__(self, nc):
        self.nc = nc
        self.base = SBUF_BASE
        self.off = SBUF_BASE
        self.n = 0

    def alloc(self, name, shape, dtype):
        per = _dsize(dtype)
        for s in shape[1:]:
            per *= s
        off = (self.off + 63) // 64 * 64
        t = self.nc.alloc_sbuf_tensor_at("%s_%d" % (name, self.n), list(shape), dtype, offset=off)
        self.n += 1
        self.off = off + per
        assert self.off <= SBUF_LIMIT, ("SBUF overflow", name, self.off)
        return Buf(t, name)

    def freeze(self):
        self.base = self.off

    def reset(self):
        self.off = self.base


class Defer:
    def __init__(self):
        self.q = []

    def push(self, fn):
        self.q.append(fn)

    def flush(self, keep=0):
        while len(self.q) > keep:
            self.q.pop(0)()


class Rot:
    def __init__(self, items):
        self.items = items
        self.i = 0

    def next(self):
        b = self.items[self.i % len(self.items)]
        self.i += 1
        return b


class K:
    pass


def _ops(k):
    S = k.S

    def MM(out, lhsT, rhs, start, stop, R, W, sig=True):
        S.op("pe", lambda e: e.matmul(out, lhsT=lhsT, rhs=rhs, start=start, stop=stop), R, W, sig)

    def TR(out, in_, ident, R, W, sig=True):
        S.op("pe", lambda e: e.transpose(out, in_, ident), R, W, sig)

    def ACT(out, in_, func, R, W, bias=0.0, scale=1.0):
        S.op("act", lambda e: e.activation(out=out, in_=in_, func=func, bias=bias, scale=scale), R, W)

    def TT(eng, out, in0, in1, op, R, W):
        S.op(eng, lambda e: e.tensor_tensor(out=out, in0=in0, in1=in1, op=op), R, W)

    def TS(eng, out, in0, s1, s2, op0, op1, R, W):
        if s2 is None:
            S.op(eng, lambda e: e.tensor_scalar(out=out, in0=in0, scalar1=s1, scalar2=None, op0=op0), R, W)
        else:
            S.op(eng, lambda e: e.tensor_scalar(out=out, in0=in0, scalar1=s1, scalar2=s2, op0=op0, op1=op1), R, W)

    def STT(eng, out, in0, sc, in1, op0, op1, R, W):
        S.op(eng, lambda e: e.scalar_tensor_tensor(out=out, in0=in0, scalar=sc, in1=in1, op0=op0, op1=op1), R, W)

    def CP(eng, out, in_, R, W):
        if eng == "act":
            S.op("act", lambda e: e.activation(out=out, in_=in_, func=AF.Identity), R, W)
        else:
            S.op(eng, lambda e: e.tensor_copy(out=out, in_=in_), R, W)

    def MS(eng, ap, val, W):
        S.op(eng, lambda e: e.memset(ap, val), (), W)

    k.MM, k.TR, k.ACT, k.TT, k.TS, k.STT, k.CP, k.MS = MM, TR, ACT, TT, TS, STT, CP, MS


def build(T, nl=NL, taps=(), phases=None, feeds=()):
    assert T % 512 == 0
    nc = bass.Bass("TRN2", target_bir_lowering=False)
    k = K()
    k.nc, k.T, k.nl = nc, T, nl
    k.S = Sched(nc)
    _ops(k)
    k.A = Arena(nc)
    NT = T // 128
    NG = T // 512
    NB = T // 256
    k.NT, k.NG, k.NB = NT, NG, NB

    k.input_names = []
    pk = None if phases is None else set(p.split("_")[0] for p in phases)
    wneed = {"w_in": "p1", "w_branch_a": "p4a", "w_branch_b": "p4a", "w_out": "p4a", "w_up": "p4b", "w_down": "p4c"}

    def din(name, shape, dt=F32):
        if pk is not None and name in wneed and wneed[name] not in pk:
            return None
        k.input_names.append(name)
        return nc.dram_tensor(name, list(shape), dt, kind="ExternalInput").ap()

    def dscr(name, shape, dt):
        kind = "ExternalOutput" if name in taps else ("ExternalInput" if name in feeds else "Internal")
        if name in feeds:
            k.input_names.append(name)
        return nc.dram_tensor(name, list(shape), dt, kind=kind).ap()

    k.xT_in = din("xT", [D, T])
    k.x_in = din("x", [T, D])
    k.w_in = din("w_in", [NL, D, NIN])
    k.w_a = din("w_branch_a", [NL, D, D])
    k.w_b = din("w_branch_b", [NL, D, D])
    k.w_out = din("w_out", [NL, D, D])
    k.w_up = din("w_up", [NL, D, 2 * DFF])
    k.w_down = din("w_down", [NL, DFF, D])
    k.c_bfm = din("c_bfm", [128, NL * 64])
    k.c_bif = din("c_bif", [8, NL])
    k.c_btm = din("c_btm", [128, NL * 3072])
    k.c_cqk = din("c_cqk", [128, NL * 16 * 4])
    k.c_gain = din("c_gain", [128, NL * 1024])
    k.c_ln = din("c_ln", [128, NL * 4 * 1024])
    k.c_cff = din("c_cff", [128, NL * 44 * 3])
    k.c_rb31 = din("c_rb31", [128, 8])
    k.c_bnear = din("c_bnear", [128, 8 * 2 * 128])
    k.c_cmask = din("c_cmask", [128, 128])
    k.c_maskneg = din("c_maskneg", [128, 32 * 32])
    k.c_E = din("c_E", [32, 32 * 128], BF16)
    k.c_identb = din("c_identb", [128, 128], BF16)
    k.c_identf = din("c_identf", [128, 128])
    k.c_tri = din("c_tri", [128, 128])
    k.c_oh4 = din("c_oh4", [4, 4 * 128])
    k.y_out = nc.dram_tensor("y", [T, D], F32, kind="ExternalOutput").ap()

    k.xT_d = dscr("s_xT", [D, T], BF16)
    k.x_d = dscr("s_x", [T, D], F32)
    k.x1_d = dscr("s_x1", [T, D], F32)
    k.x1T_d = dscr("s_x1T", [D, T], BF16)
    k.qmT_d = dscr("s_qmT", [D, T], BF16)
    k.kmT_d = dscr("s_kmT", [D, T], BF16)
    k.km_d = dscr("s_km", [T, D], BF16)
    k.vm_d = dscr("s_vm", [T, D], BF16)
    k.om_d = dscr("s_om", [T, D], BF16)
    k.gates_d = dscr("s_gates", [8, T], F32)
    k.qbT_d = dscr("s_qbT", [D, T], BF16)
    k.kbT_d = dscr("s_kbT", [D, T], BF16)
    k.vb_d = dscr("s_vb", [T, D], BF16)
    k.gaT_d = dscr("s_gaT", [D, T], BF16)
    k.gbT_d = dscr("s_gbT", [D, T], BF16)
    k.yaT_d = dscr("s_yaT", [D, T], BF16)
    k.ybT_d = dscr("s_ybT", [D, T], BF16)
    k.hT_d = dscr("s_hT", [DFF, T], BF16)

    k.pf = [Buf(nc.alloc_psum_tensor("pf%d" % i, [128, 512], F32), "pf%d" % i) for i in range(6)]
    k.pb = [Buf(nc.alloc_psum_tensor("pb%d" % i, [128, 1024], BF16), "pb%d" % i) for i in range(2)]

    A = k.A
    S = k.S

    def const(name, src, shape, dt=F32):
        b = A.alloc(name, shape, dt)
        S.dma(b[:], src, writes=[b])
        return b

    k.bfm = const("bfm", k.c_bfm, [128, NL * 64])
    k.bif = const("bif", k.c_bif, [8, NL])
    k.cqk = const("cqk", k.c_cqk, [128, NL * 64])
    k.cff = const("cff", k.c_cff, [128, NL * 132])
    k.rb31 = const("rb31", k.c_rb31, [128, 8])
    k.identb = const("identb", k.c_identb, [128, 128], BF16)
    k.identf = const("identf", k.c_identf, [128, 128])
    k.tri = const("tri", k.c_tri, [128, 128])
    k.oh4 = const("oh4", k.c_oh4, [4, 512])
    A.freeze()

    ph = phases if phases is not None else ["p0"] + [x for l in range(nl) for x in
                                                    ["p1_%d" % l, "p2a_%d" % l, "p2b_%d" % l, "p3_%d" % l,
                                                     "p4a_%d" % l, "p4b_%d" % l, "p4c_%d" % l]]
    for name in ph:
        S.barrier()
        A.reset()
        if name == "p0":
            phase0(k)
        else:
            kind, l = name.split("_")
            l = int(l)
            {"p1": phase1, "p2a": phase2a, "p2b": phase2b, "p3": phase3, "p4a": phase4a, "p4b": phase4b, "p4c": phase4c}[kind](k, l)
    S.barrier()
    with ExitStack() as st:
        block = st.enter_context(nc.Block())
        S.emit(block, st)
    k.nc = nc
    return nc, k


def phase0(k):
    S, A, T = k.S, k.A, k.T
    xb = Rot([A.alloc("xb", [128, 8, 512], BF16) for _ in range(3)])
    xf = Rot([A.alloc("xf", [128, 4, 1024], F32) for _ in range(2)])
    for g in range(k.NG):
        b = xb.next()
        S.dma(b[:], k.xT_in[:, g * 512:(g + 1) * 512].rearrange("(k p) t -> p k t", p=128), writes=[b], q="pool")
        S.dma(k.xT_d[:, g * 512:(g + 1) * 512].rearrange("(k p) t -> p k t", p=128), b[:], reads=[b])
        f = xf.next()
        S.dma(f[:], k.x_in[g * 512:(g + 1) * 512, :].rearrange("(j p) d -> p j d", p=128), writes=[f])
        S.dma(k.x_d[g * 512:(g + 1) * 512, :].rearrange("(j p) d -> p j d", p=128), f[:], reads=[f], q="pool")


FM_COL0 = {"qm": 0, "km": 1024, "qb": 4104, "kb": 5128, "ga": 7176, "gb": 8200}
FM_IDX0 = {"qm": 0, "km": 8, "qb": 16, "kb": 24, "ga": 32, "gb": 40}
TM_COL0 = {"vm": 2048, "om": 3072, "vb": 6152}
TM_IDX0 = {"vm": 0, "om": 1024, "vb": 2048}


def phase1(k, l):
    S, A, T, NG = k.S, k.A, k.T, k.NG
    MM, TR, ACT, TT, TS, STT, CP, MS = k.MM, k.TR, k.ACT, k.TT, k.TS, k.STT, k.CP, k.MS
    wt = Rot([A.alloc("w", [128, 8, 1024], BF16) for _ in range(2)])
    wif = A.alloc("wif", [128, 8, 8], BF16)
    btm = A.alloc("btm", [128, 3072], F32)
    xt = Rot([A.alloc("xt", [128, 8, 512], BF16) for _ in range(3)])
    cb = Rot([A.alloc("cb", [128, 515], F32) for _ in range(3)])
    acc = Rot([A.alloc("acc", [128, 512], F32) for _ in range(4)])
    ob = Rot([A.alloc("ob", [128, 512], BF16) for _ in range(6)])
    hal = [A.alloc("hal%d" % c, [128, 3], F32) for c in range(8)]
    tkb = Rot([A.alloc("tkb", [128, 4, 1024], BF16) for _ in range(2)])
    tmo = Rot([A.alloc("tmo", [128, 1024], BF16) for _ in range(3)])
    tmf = Rot([A.alloc("tmf", [128, 512], F32) for _ in range(2)])
    ifb = Rot([A.alloc("ifb", [8, 512], F32) for _ in range(2)])
    pf = Rot(k.pf[0:5])
    pbr = Rot(k.pb)

    S.dma(btm[:], k.c_btm[:, l * 3072:(l + 1) * 3072], writes=[btm])
    S.dma(wif[:], k.w_in[l, :, 4096:4104].rearrange("(k p) n -> p k n", p=128), writes=[wif], q="pool")

    blocks = [("conv", "qm", k.qmT_d), ("conv", "km", k.kmT_d), ("fm", "qb", k.qbT_d), ("fm", "kb", k.kbT_d),
              ("fm", "ga", k.gaT_d), ("fm", "gb", k.gbT_d), ("tm", "vm", k.vm_d), ("tm", "om", k.om_d),
              ("tm", "vb", k.vb_d)]

    def load_w(name):
        c0 = FM_COL0[name] if name in FM_COL0 else TM_COL0[name]
        w = wt.next()
        S.dma(w[:], k.w_in[l, :, c0:c0 + 1024].rearrange("(k p) n -> p k n", p=128), writes=[w], q="pool")
        return w

    def load_x(g):
        x = xt.next()
        S.dma(x[:], k.xT_d[:, g * 512:(g + 1) * 512].rearrange("(k p) t -> p k t", p=128), writes=[x])
        return x

    wnext = load_w(blocks[0][1])
    dq = Defer()
    tq = Defer()
    for bi, (kind, name, dest) in enumerate(blocks):
        dq.flush()
        w = wnext
        if bi + 1 < len(blocks):
            wnext = load_w(blocks[bi + 1][1])
        xnext = load_x(0)
        for g in range(NG):
            x = xnext
            if g + 1 < NG:
                xnext = load_x(g + 1)
            tsl = slice(g * 512, (g + 1) * 512)
            if kind in ("conv", "fm"):
                if name == "km":
                    tk = tkb.next()
                for cc in range(8):
                    ps = pf.next()
                    for kk in range(8):
                        MM(ps[:, 0:512], w[:, kk, cc * 128:(cc + 1) * 128], x[:, kk, :], kk == 0, kk == 7,
                           [w, x], [ps], sig=(kk == 7))
                    dq.flush(keep=2)
                    bcol = l * 64 + FM_IDX0[name] + cc
                    bias = k.bfm[:, bcol:bcol + 1]
                    o = ob.next()
                    if kind == "fm":
                        func = AF.Sigmoid if name in ("ga", "gb") else AF.Identity
                        ACT(o[:], ps[:, 0:512], func, [ps, k.bfm], [o], bias=bias)
                    else:
                        c = cb.next()
                        a = acc.next()
                        h = hal[cc]
                        ACT(c[:, 3:515], ps[:, 0:512], AF.Identity, [ps, k.bfm], [c], bias=bias)
                        if g == 0:
                            MS("pool", c[:, 0:3], 0.0, [c])
                        else:
                            CP("pool", c[:, 0:3], h[:], [h], [c])
                        ci = l * 64 + (FM_IDX0[name] + cc) * 4
                        TS("dve", a[:], c[:, 3:515], k.cqk[:, ci + 3:ci + 4], None, ALU.mult, None, [c, k.cqk], [a])
                        for j in (2, 1, 0):
                            STT("dve", a[:], c[:, j:j + 512], k.cqk[:, ci + j:ci + j + 1], a[:], ALU.mult, ALU.add,
                                [c, a, k.cqk], [a])
                        CP("pool", h[:], c[:, 512:515], [c], [h])

                        def tail(o=o, a=a, cc=cc, tk=(tk if name == "km" else None), tsl=tsl, name=name, dest=dest):
                            ACT(o[:], a[:], AF.Silu, [a], [o])
                            r0_ = (FM_IDX0[name] % 8 + cc) * 128
                            S.dma(dest[r0_:r0_ + 128, tsl], o[:], reads=[o], q="sp")
                            if tk is not None:
                                def fin(o=o, tk=tk, cc=cc):
                                    pb = pbr.next()
                                    for j in range(4):
                                        TR(pb[:, j * 128:(j + 1) * 128], o[:, j * 128:(j + 1) * 128], k.identb[:],
                                           [o, k.identb], [pb], sig=(j == 3))
                                    CP("act" if cc % 2 else "dve", tk[:, :, cc * 128:(cc + 1) * 128],
                                       pb[:, 0:512].rearrange("p (j f) -> p j f", f=128), [pb], [tk])
                                dq.push(fin)
                        tq.push(tail)
                        tq.flush(keep=1)
                        continue
                    r0 = (FM_IDX0[name] % 8 + cc) * 128
                    S.dma(dest[r0:r0 + 128, tsl], o[:], reads=[o], q="sp")
                if kind == "conv":
                    tq.flush()
                if name == "km":
                    def store(tk=tk, tsl=tsl):
                        S.dma(k.km_d[tsl, :].rearrange("(j p) f -> p j f", p=128), tk[:], reads=[tk], q="sp")
                    dq.push(store)
                if name == "qm":
                    ps = pf.next()
                    for kk in range(8):
                        MM(ps[0:8, 0:512], wif[:, kk, :], x[:, kk, :], kk == 0, kk == 7, [wif, x], [ps], sig=(kk == 7))
                    fb = ifb.next()
                    ACT(fb[:], ps[0:8, 0:512], AF.Identity, [ps, k.bif], [fb], bias=k.bif[:, l:l + 1])
                    S.dma(k.gates_d[:, tsl], fb[:], reads=[fb], q="sp")
            else:
                for j in range(4):
                    o = tmo.next()
                    for hf in range(2):
                        ps = pf.next()
                        for kk in range(8):
                            MM(ps[:, 0:512], x[:, kk, j * 128:(j + 1) * 128], w[:, kk, hf * 512:(hf + 1) * 512],
                               kk == 0, kk == 7, [w, x], [ps], sig=(kk == 7))
                        b0 = TM_IDX0[name] + hf * 512
                        if name == "om":
                            f = tmf.next()
                            TT("dve", f[:], ps[:, 0:512], btm[:, b0:b0 + 512], ALU.add, [ps, btm], [f])
                            ACT(o[:, hf * 512:(hf + 1) * 512], f[:], AF.Sigmoid, [f], [o])
                        else:
                            TT("dve", o[:, hf * 512:(hf + 1) * 512], ps[:, 0:512], btm[:, b0:b0 + 512], ALU.add,
                               [ps, btm], [o])
                    S.dma(dest[g * 512 + j * 128:g * 512 + (j + 1) * 128, :], o[:], reads=[o], q="sp")


def phase2a(k, l):
    S, A, T, NT = k.S, k.A, k.T, k.NT
    MM, TR, ACT, TT, TS, STT, CP, MS = k.MM, k.TR, k.ACT, k.TT, k.TS, k.STT, k.CP, k.MS
    NC = NT
    k.gtm = A.alloc("gtm", [128, NC * 4], F32)
    k.ttm = A.alloc("ttm", [128, NC * 4], F32)
    k.ebc = A.alloc("ebc", [128, 4 * NC], F32)
    ig = A.alloc("ig", [4, T], F32)
    fg = A.alloc("fg", [4, T], F32)
    ones = A.alloc("ones", [4, T], F32)
    Bn = A.alloc("Bn", [4, T], F32)
    sp = fg
    ap_ = ig
    sm = {n: A.alloc(n, [4, NC], F32) for n in
          ["Bend", "Bn0", "amax", "dB", "dBp", "Mc", "Mcp", "e", "off1", "off2", "t1"]}
    S.dma(ig[:], k.gates_d[0:4, :], writes=[ig])
    S.dma(fg[:], k.gates_d[4:8, :], writes=[fg])
    MS("dve", ones[:], 1.0, [ones])
    ACT(sp[:], fg[:], AF.Exp, [fg], [sp], scale=-1.0)
    ACT(sp[:], sp[:], AF.Ln, [sp], [sp], bias=1.0)
    assert sp is fg and ap_ is ig
    S.op("dve", lambda e: e.tensor_tensor_scan(out=Bn[:], data0=ones[:], data1=sp[:], initial=0.0,
                                               op0=ALU.mult, op1=ALU.add), [ones, sp], [Bn])
    TT("dve", ap_[:], ig[:], Bn[:], ALU.add, [ig, Bn], [ap_])
    Bend, Bn0, amax, dB, dBp, Mc, Mcp, e_, off1, off2, t1 = [sm[n] for n in
                                                             ["Bend", "Bn0", "amax", "dB", "dBp", "Mc", "Mcp", "e",
                                                              "off1", "off2", "t1"]]
    CP("dve", Bend[:], Bn[:, 127::128], [Bn], [Bend])
    MS("dve", Bn0[:], 0.0, [Bn0])
    if NC > 1:
        CP("dve", Bn0[:, 1:NC], Bend[:, 0:NC - 1], [Bend], [Bn0])
    S.op("dve", lambda e: e.tensor_reduce(out=amax[:], in_=ap_[:].rearrange("p (c s) -> p c s", s=128),
                                          axis=AX.X, op=ALU.max), [ap_], [amax])
    TT("dve", amax[:], amax[:], Bn0[:], ALU.subtract, [amax, Bn0], [amax])
    TT("dve", dB[:], Bn0[:], Bend[:], ALU.subtract, [Bn0, Bend], [dB])
    MS("dve", dBp[:], 0.0, [dBp])
    if NC > 1:
        CP("dve", dBp[:, 1:NC], dB[:, 0:NC - 1], [dB], [dBp])
    S.op("dve", lambda e: e.tensor_tensor_scan(out=Mc[:], data0=dBp[:], data1=amax[:], initial=0.0,
                                               op0=ALU.add, op1=ALU.max), [dBp, amax], [Mc])
    MS("dve", Mcp[:], 0.0, [Mcp])
    if NC > 1:
        CP("dve", Mcp[:, 1:NC], Mc[:, 0:NC - 1], [Mc], [Mcp])
    TT("dve", t1[:], dBp[:], Mcp[:], ALU.add, [dBp, Mcp], [t1])
    TT("dve", t1[:], t1[:], Mc[:], ALU.subtract, [t1, Mc], [t1])
    ACT(e_[:], t1[:], AF.Exp, [t1], [e_])
    TT("dve", off2[:], Bn0[:], Mc[:], ALU.add, [Bn0, Mc], [off2])
    TS("dve", off1[:], off2[:], math.log(16.0), None, ALU.add, None, [off2], [off1])
    v3 = "p (c s) -> p c s"
    TT("dve", ap_[:].rearrange(v3, s=128), ap_[:].rearrange(v3, s=128),
       off1[:].rearrange("p (c o) -> p c o", o=1).broadcast_to([4, NC, 128]), ALU.subtract, [ap_, off1], [ap_])
    TT("dve", Bn[:].rearrange(v3, s=128), Bn[:].rearrange(v3, s=128),
       off2[:].rearrange("p (c o) -> p c o", o=1).broadcast_to([4, NC, 128]), ALU.subtract, [Bn, off2], [Bn])
    ACT(ap_[:], ap_[:], AF.Exp, [ap_], [ap_])
    ACT(Bn[:], Bn[:], AF.Exp, [Bn], [Bn])
    pg, pt, pe = k.pf[0], k.pf[1], k.pf[2]
    for c in range(NC):
        TR(pg[:, c * 4:(c + 1) * 4], ap_[:, c * 128:(c + 1) * 128], k.identf[0:4, 0:4], [ap_, k.identf], [pg],
           sig=(c == NC - 1))
    for c in range(NC):
        TR(pt[:, c * 4:(c + 1) * 4], Bn[:, c * 128:(c + 1) * 128], k.identf[0:4, 0:4], [Bn, k.identf], [pt],
           sig=(c == NC - 1))
    for h in range(4):
        MM(pe[:, h * NC:(h + 1) * NC], k.oh4[:, h * 128:(h + 1) * 128], e_[:], True, True, [k.oh4, e_], [pe],
           sig=(h == 3))
    CP("dve", k.gtm[:], pg[:, 0:NC * 4], [pg], [k.gtm])
    CP("dve", k.ttm[:], pt[:, 0:NC * 4], [pt], [k.ttm])
    CP("dve", k.ebc[:], pe[:, 0:NC * 4], [pe], [k.ebc])


def phase2b(k, l):
    S, A, T, NT = k.S, k.A, k.T, k.NT
    MM, TR, ACT, TT, TS, STT, CP, MS = k.MM, k.TR, k.ACT, k.TT, k.TS, k.STT, k.CP, k.MS
    NC = NT
    gtm = A.alloc("gtm", [128, NC * 4], F32)
    ttm = A.alloc("ttm", [128, NC * 4], F32)
    ebc = A.alloc("ebc", [128, 4 * NC], F32)
    gtm.w, ttm.w, ebc.w = k.gtm.w, k.ttm.w, k.ebc.w
    gain = A.alloc("gain", [128, 1024], F32)
    S.dma(gain[:], k.c_gain[:, l * 1024:(l + 1) * 1024], writes=[gain])
    qT4 = Rot([A.alloc("qT4", [128, 8, 512], BF16) for _ in range(2)])
    kT4 = Rot([A.alloc("kT4", [128, 8, 512], BF16) for _ in range(2)])
    k4 = Rot([A.alloc("k4", [128, 4, 1024], BF16) for _ in range(2)])
    v4 = Rot([A.alloc("v4", [128, 4, 1024], BF16) for _ in range(2)])
    o4 = Rot([A.alloc("o4", [128, 4, 1024], BF16) for _ in range(2)])
    yT4 = Rot([A.alloc("yT4", [128, 8, 512], BF16) for _ in range(2)])
    C = [A.alloc("C%d" % h, [128, 2, 257], F32) for h in range(4)]
    Cs = [Rot([A.alloc("Cs%d" % h, [128, 2, 257], BF16) for _ in range(2)]) for h in range(4)]
    sTm = Rot([A.alloc("sTm", [128, 128], BF16) for _ in range(3)])
    vg = Rot([A.alloc("vg", [128, 257], BF16) for _ in range(3)])
    hh = Rot([A.alloc("hh", [128, 256], F32) for _ in range(3)])
    hn = Rot([A.alloc("hn", [128, 256], F32) for _ in range(3)])
    ya = Rot([A.alloc("ya", [128, 256], BF16) for _ in range(4)])
    st6 = Rot([A.alloc("st6", [128, 6], F32) for _ in range(3)])
    mv = Rot([A.alloc("mv", [128, 4], F32) for _ in range(3)])
    sets = [(k.pf[0], k.pf[1], k.pf[2]), (k.pf[3], k.pf[4], k.pf[5])]
    sT_tok = [Buf(k.pf[0].t, "sT0"), Buf(k.pf[3].t, "sT1")]
    out_tok = [Buf(k.pf[0].t, "out0"), Buf(k.pf[3].t, "out1")]
    pbr = Rot(k.pb)
    for h in range(4):
        MS("pool", C[h][:], 0.0, [C[h]])

    def load(g):
        tsl = slice(g * 512, (g + 1) * 512)
        a, b, c_, d, e = qT4.next(), kT4.next(), k4.next(), v4.next(), o4.next()
        S.dma(a[:], k.qmT_d[:, tsl].rearrange("(k p) t -> p k t", p=128), writes=[a])
        S.dma(b[:], k.kmT_d[:, tsl].rearrange("(k p) t -> p k t", p=128), writes=[b])
        S.dma(c_[:], k.km_d[tsl, :].rearrange("(j p) f -> p j f", p=128), writes=[c_])
        S.dma(d[:], k.vm_d[tsl, :].rearrange("(j p) f -> p j f", p=128), writes=[d])
        S.dma(e[:], k.om_d[tsl, :].rearrange("(j p) f -> p j f", p=128), writes=[e])
        return a, b, c_, d, e

    nxt = load(0)
    it = 0
    dq = Defer()
    eq = Defer()
    for g in range(k.NG):
        q_, kT_, k_, v_, o_ = nxt
        if g + 1 < k.NG:
            nxt = load(g + 1)
        yT = yT4.next()
        for j in range(4):
            c = g * 4 + j
            tj = slice(j * 128, (j + 1) * 128)
            for h in range(4):
                p = it % 2
                it += 1
                bk_a, bk_u0, bk_u1 = sets[p]
                sTt, outt = sT_tok[p], out_tok[p]
                gcol = gtm[:, c * 4 + h:c * 4 + h + 1]
                tcol = ttm[:, c * 4 + h:c * 4 + h + 1]
                ecol = ebc[:, h * NC + c:h * NC + c + 1]
                for jj in range(2):
                    MM(bk_a[:, 0:128], kT_[:, 2 * h + jj, tj], q_[:, 2 * h + jj, tj], jj == 0, jj == 1,
                       [kT_, q_], [sTt], sig=(jj == 1))
                dq.flush(keep=1)
                sm_ = sTm.next()
                TT("dve", sm_[:], bk_a[:, 0:128], k.tri[:], ALU.mult, [sTt, k.tri], [sm_])
                vg_ = vg.next()
                ACT(vg_[:, 0:256], v_[:, j, h * 256:(h + 1) * 256], AF.Identity, [v_, gtm], [vg_], scale=gcol)
                ACT(vg_[:, 256:257], gcol, AF.Identity, [gtm], [vg_])
                Ch = C[h]
                cs = Cs[h].next()
                ACT(cs[:], Ch[:], AF.Identity, [Ch, ebc], [cs], scale=ecol)
                for jj in range(2):
                    MM(bk_a[:, 128:385], q_[:, 2 * h + jj, tj], cs[:, jj, :], jj == 0, False, [q_, cs], [outt], sig=False)
                MM(bk_a[:, 128:385], sm_[:], vg_[:], False, True, [sm_, vg_], [outt])
                for jj, bk in ((0, bk_u0), (1, bk_u1)):
                    MM(bk[:, 0:257], k_[:, j, (2 * h + jj) * 128:(2 * h + jj + 1) * 128], vg_[:], True, True,
                       [k_, vg_], [bk])
                    STT("dve", Ch[:, jj, :], Ch[:, jj, :], ecol, bk[:, 0:257], ALU.mult, ALU.add, [Ch, bk, ebc], [Ch])
                def epi(bk_a=bk_a, outt=outt, tcol=tcol, h=h, j=j, tj=tj, o_=o_, yT=yT):
                    m_ = mv.next()
                    ACT(m_[:, 0:1], bk_a[:, 384:385], AF.Abs, [outt], [m_])
                    TT("dve", m_[:, 0:1], m_[:, 0:1], tcol, ALU.max, [m_, ttm], [m_])
                    S.op("dve", lambda e, m_=m_: e.reciprocal(out=m_[:, 1:2], in_=m_[:, 0:1]), [m_], [m_])
                    hh_ = hh.next()
                    ACT(hh_[:], bk_a[:, 128:384], AF.Identity, [outt, m_], [hh_], scale=m_[:, 1:2])
                    s6 = st6.next()
                    S.op("dve", lambda e, s6=s6, hh_=hh_: e.bn_stats(out=s6[:], in_=hh_[:]), [hh_], [s6])
                    S.op("dve", lambda e, s6=s6, m_=m_: e.bn_aggr(out=m_[:, 2:4], in_=s6[:]), [s6], [m_])
                    TS("dve", m_[:, 3:4], m_[:, 3:4], EPS, None, ALU.add, None, [m_], [m_])
                    ACT(m_[:, 3:4], m_[:, 3:4], AF.Sqrt, [m_], [m_])
                    S.op("dve", lambda e, m_=m_: e.reciprocal(out=m_[:, 3:4], in_=m_[:, 3:4]), [m_], [m_])
                    STT("dve", m_[:, 2:3], m_[:, 2:3], -1.0, m_[:, 3:4], ALU.mult, ALU.mult, [m_], [m_])
                    hn_ = hn.next()
                    ACT(hn_[:], hh_[:], AF.Identity, [hh_, m_], [hn_], bias=m_[:, 2:3], scale=m_[:, 3:4])
                    TT("dve", hn_[:], hn_[:], gain[:, h * 256:(h + 1) * 256], ALU.mult, [hn_, gain], [hn_])
                    ya_ = ya.next()
                    TT("dve", ya_[:], hn_[:], o_[:, j, h * 256:(h + 1) * 256], ALU.mult, [hn_, o_], [ya_])
                    def fin(ya_=ya_, yT=yT, h=h, tj=tj):
                        pb = pbr.next()
                        for jj in range(2):
                            TR(pb[:, jj * 128:(jj + 1) * 128], ya_[:, jj * 128:(jj + 1) * 128], k.identb[:],
                               [ya_, k.identb], [pb], sig=(jj == 1))
                        CP("act", yT[:, 2 * h:2 * h + 2, tj], pb[:, 0:256].rearrange("p (a t) -> p a t", t=128), [pb],
                           [yT])
                    dq.push(fin)
                eq.push(epi)
                eq.flush(keep=1)

        eq.flush()

        def store(yT=yT, g=g):
            S.dma(k.yaT_d[:, g * 512:(g + 1) * 512].rearrange("(k p) t -> p k t", p=128), yT[:], reads=[yT], q="pool")
        dq.push(store)
    dq.flush()


def phase3(k, l):
    S, A, T, NT, NG, NB = k.S, k.A, k.T, k.NT, k.NG, k.NB
    MM, TR, ACT, TT, TS, STT, CP, MS = k.MM, k.TR, k.ACT, k.TT, k.TS, k.STT, k.CP, k.MS
    bnear = A.alloc("bnear", [128, 8 * 2 * 128], F32)
    cmask = A.alloc("cmask", [128, 128], F32)
    maskneg = A.alloc("maskneg", [128, 32 * 32], F32)
    E = A.alloc("E", [32, 32 * 128], BF16)
    S.dma(bnear[:], k.c_bnear, writes=[bnear])
    S.dma(cmask[:], k.c_cmask, writes=[cmask])
    S.dma(maskneg[:], k.c_maskneg, writes=[maskneg])
    S.dma(E[:], k.c_E, writes=[E])
    for h in range(8):
        o0 = (h * 2) * 128
        TT("dve", bnear[:, o0:o0 + 128], bnear[:, o0:o0 + 128], cmask[:], ALU.add, [bnear, cmask], [bnear])
    qT = Rot([A.alloc("qTh", [128, T], BF16) for _ in range(2)])
    kT = Rot([A.alloc("kTh", [128, T], BF16) for _ in range(2)])
    vh = Rot([A.alloc("vh", [128, NT, 129], BF16) for _ in range(2)])
    addT = Rot([A.alloc("addT", [32, T], BF16) for _ in range(2)])
    kmf = Rot([A.alloc("kmf", [128, 32], F32) for _ in range(2)])
    kmb = Rot([A.alloc("kmb", [128, 32], BF16) for _ in range(2)])
    gm = Rot([A.alloc("gm", [128, 32], F32) for _ in range(3)])
    t8 = Rot([A.alloc("t8", [128, 8], F32) for _ in range(3)])
    aq = Rot([A.alloc("aq", [128, 32], F32) for _ in range(3)])
    aqb = Rot([A.alloc("aqb", [128, 32], BF16) for _ in range(4)])
    PT = Rot([A.alloc("PT", [128, 512], BF16) for _ in range(4)])
    tmpn = Rot([A.alloc("tmpn", [128, 128], F32) for _ in range(4)])
    rden = Rot([A.alloc("rden", [128, 512], F32) for _ in range(2)])
    yb = Rot([A.alloc("yb", [128, 512], BF16) for _ in range(2)])
    onesb = A.alloc("onesb", [128, 128], BF16)
    MS("pool", onesb[:], 1.0, [onesb])
    pS = Rot([k.pf[0], k.pf[1], k.pf[2]])
    pO, pD, pG = k.pf[3], k.pf[4], k.pf[5]
    pbr = Rot(k.pb)

    def load(h):
        a, b, c_ = qT.next(), kT.next(), vh.next()
        S.dma(a[:], k.qbT_d[h * 128:(h + 1) * 128, :], writes=[a])
        S.dma(b[:], k.kbT_d[h * 128:(h + 1) * 128, :], writes=[b])
        S.dma(c_[:, :, 0:128], k.vb_d[:, h * 128:(h + 1) * 128].rearrange("(n p) d -> p n d", p=128), writes=[c_])
        MS("pool", c_[:, :, 128:129], 1.0, [c_])
        return a, b, c_

    def selection_gen(q_h, kT_h, ad_h):
        kf, kb_ = kmf.next(), kmb.next()
        MS("dve", kf[:], 0.0, [kf])
        S.op("dve", lambda e: e.tensor_reduce(
            out=kf[:, 0:NB], in_=kT_h[:].rearrange("p (n s) -> p n s", s=256), axis=AX.X, op=ALU.add), [kT_h], [kf])
        TS("dve", kb_[:], kf[:], 1.0 / 256.0, None, ALU.mult, None, [kf], [kb_])
        yield
        pend = [None]
        for i0 in range(0, NT, 16):
            n_i = min(16, NT - i0)
            for ii in range(n_i):
                i = i0 + ii
                MM(pG[:, ii * 32:(ii + 1) * 32], q_h[:, i * 128:(i + 1) * 128], kb_[:], True, True, [q_h, kb_], [pG],
                   sig=(ii == n_i - 1))
            yield
            for ii in range(n_i):
                i = i0 + ii
                own = i // 2
                g_, t_, a_, ab_ = gm.next(), t8.next(), aq.next(), aqb.next()
                TT("dve", g_[:], pG[:, ii * 32:(ii + 1) * 32], maskneg[:, own * 32:(own + 1) * 32], ALU.add,
                   [pG, maskneg], [g_])
                S.op("dve", lambda e, t_=t_, g_=g_: e.max(out=t_[:], in_=g_[:]), [g_], [t_])
                TS("dve", a_[:], g_[:], t_[:, 3:4], BIG, ALU.is_ge, ALU.mult, [g_, t_], [a_])
                TS("dve", ab_[:], a_[:], -BIG, None, ALU.add, None, [a_], [ab_])
                if pend[0] is not None:
                    pend[0]()

                def fin(ab_=ab_, i=i):
                    pb = pbr.next()
                    TR(pb[0:32, 0:128], ab_[:], k.identb[:], [ab_, k.identb], [pb])
                    CP("act", ad_h[:, i * 128:(i + 1) * 128], pb[0:32, 0:128], [pb], [ad_h])
                pend[0] = fin
                yield
        if pend[0] is not None:
            pend[0]()
        yield

    nxt = load(0)
    ad_nxt = addT.next()
    for _ in selection_gen(nxt[0], nxt[1], ad_nxt):
        pass
    for h in range(8):
        q_, kT_, v_ = nxt
        ad = ad_nxt
        gen = None
        if h + 1 < 8:
            nxt = load(h + 1)
            ad_nxt = addT.next()
            gen = selection_gen(nxt[0], nxt[1], ad_nxt)
        items = [(G, kt) for G in range(NG) for kt in range(4 * G + 4)]

        def stageA(i):
            G, kt = items[i]
            n = kt // 2
            j0 = max(kt, 4 * G)
            q0, q1 = j0 * 128, (4 * G + 4) * 128
            nq = q1 - q0
            ps = pS.next()
            need_sel = (kt < 4 * G + 2) and (G >= 2)
            MM(ps[:, 0:nq], kT_[:, kt * 128:(kt + 1) * 128], q_[:, q0:q1], True, not need_sel, [kT_, q_], [ps],
               sig=not need_sel)
            if need_sel:
                MM(ps[:, 0:nq], E[:, n * 128:(n + 1) * 128], ad[:, q0:q1], False, True, [E, ad], [ps])
            return dict(G=G, kt=kt, j0=j0, q0=q0, nq=nq, c0=q0 - G * 512, ps=ps)

        def stageB(st_):
            G, kt, j0, nq, ps = st_["G"], st_["kt"], st_["j0"], st_["nq"], st_["ps"]
            pt = PT.next()
            col = 0
            for j in range(j0, 4 * G + 4):
                typ = 0 if j == kt else (1 if j == kt + 1 else 2)
                if typ == 2:
                    break
                tn = tmpn.next()
                bo = (h * 2 + typ) * 128
                STT("dve", tn[:], ps[:, col:col + 128], SCALE_B, bnear[:, bo:bo + 128], ALU.mult, ALU.add,
                    [ps, bnear], [tn])
                ACT(pt[:, col:col + 128], tn[:], AF.Exp, [tn], [pt])
                col += 128
            if col < nq:
                ACT(pt[:, col:nq], ps[:, col:nq], AF.Exp, [ps, k.rb31], [pt], bias=k.rb31[:, h:h + 1], scale=SCALE_B)
            st_["pt"] = pt

        def stageC(st_):
            G, kt, nq, c0, pt = st_["G"], st_["kt"], st_["nq"], st_["c0"], st_["pt"]
            nkt = 4 * G + 4
            MM(pO[:, c0:c0 + nq], v_[:, kt, 0:128], pt[:, 0:nq], kt == 0, kt == nkt - 1, [v_, pt], [pO],
               sig=False)
            MM(pD[:, c0:c0 + nq], onesb[:], pt[:, 0:nq], kt == 0, kt == nkt - 1, [onesb, pt], [pD],
               sig=True)
            if kt == nkt - 1:
                rd, y_ = rden.next(), yb.next()
                S.op("dve", lambda e, rd=rd: e.reciprocal(out=rd[:], in_=pD[:, 0:512]), [pD], [rd])
                TT("dve", y_[:], pO[:, 0:512], rd[:], ALU.mult, [pO, rd], [y_])
                S.dma(k.ybT_d[h * 128:(h + 1) * 128, G * 512:(G + 1) * 512], y_[:], reads=[y_], q="pool")

        LOOK = 2
        sts = {}
        for i in range(min(LOOK, len(items))):
            sts[i] = stageA(i)
        for i in range(len(items)):
            stageB(sts[i])
            if i + LOOK < len(items):
                sts[i + LOOK] = stageA(i + LOOK)
            stageC(sts.pop(i))
            if gen is not None and i % 7 == 3:
                if next(gen, "done") == "done":
                    gen = None
        if gen is not None:
            for _ in gen:
                pass


def layer_norm_tile(k, z, lnp, gi, outf, st, mv):
    S, TS, TT, STT, ACT = k.S, k.TS, k.TT, k.STT, k.ACT
    for i in range(2):
        S.op("dve", lambda e, i=i: e.bn_stats(out=st[:, i * 6:(i + 1) * 6], in_=z[:, i * 512:(i + 1) * 512]), [z], [st])
    S.op("dve", lambda e: e.bn_aggr(out=mv[:, 0:2], in_=st[:, 0:12].rearrange("p (a b) -> p a b", b=6)), [st], [mv])
    TS("dve", mv[:, 1:2], mv[:, 1:2], EPS, None, ALU.add, None, [mv], [mv])
    ACT(mv[:, 1:2], mv[:, 1:2], AF.Sqrt, [mv], [mv])
    S.op("dve", lambda e: e.reciprocal(out=mv[:, 1:2], in_=mv[:, 1:2]), [mv], [mv])
    STT("dve", mv[:, 0:1], mv[:, 0:1], -1.0, mv[:, 1:2], ALU.mult, ALU.mult, [mv], [mv])
    ACT(z[:], z[:], AF.Identity, [z, mv], [z], bias=mv[:, 0:1], scale=mv[:, 1:2])
    TT("dve", z[:], z[:], lnp[:, gi * 1024:(gi + 1) * 1024], ALU.mult, [z, lnp], [z])
    TT("dve", outf[:], z[:], lnp[:, (gi + 1) * 1024:(gi + 2) * 1024], ALU.add, [z, lnp], [outf])


def phase4a(k, l):
    S, A, T, NG = k.S, k.A, k.T, k.NG
    MM, TR, ACT, TT, TS, STT, CP, MS = k.MM, k.TR, k.ACT, k.TT, k.TS, k.STT, k.CP, k.MS
    wa = A.alloc("wa", [128, 8, 1024], BF16)
    wb = A.alloc("wb", [128, 8, 1024], BF16)
    wo = A.alloc("wo", [128, 8, 1024], BF16)
    lnp = A.alloc("lnp", [128, 2048], F32)
    S.dma(wa[:], k.w_a[l].rearrange("(k p) n -> p k n", p=128), writes=[wa], q="pool")
    S.dma(wb[:], k.w_b[l].rearrange("(k p) n -> p k n", p=128), writes=[wb], q="pool")
    S.dma(wo[:], k.w_out[l].rearrange("(k p) n -> p k n", p=128), writes=[wo], q="pool")
    S.dma(lnp[:], k.c_ln[:, (l * 4) * 1024:(l * 4 + 2) * 1024], writes=[lnp])
    ins = [Rot([A.alloc(n, [128, 8, 512], BF16) for _ in range(2)]) for n in ("yaT", "ybT", "gaT", "gbT")]
    xin = Rot([A.alloc("xin", [128, 1024], F32) for _ in range(4)])
    mT = Rot([A.alloc("mT", [128, 8, 512], BF16) for _ in range(1)])
    t1 = Rot([A.alloc("t1", [128, 512], F32) for _ in range(2)])
    t2 = Rot([A.alloc("t2", [128, 512], F32) for _ in range(2)])
    z = Rot([A.alloc("z", [128, 1024], F32) for _ in range(2)])
    x1 = Rot([A.alloc("x1", [128, 1024], F32) for _ in range(2)])
    x1b = Rot([A.alloc("x1b", [128, 1024], BF16) for _ in range(3)])
    xT = Rot([A.alloc("x1T", [128, 8, 512], BF16) for _ in range(2)])
    st = Rot([A.alloc("st", [128, 12], F32) for _ in range(2)])
    mv = Rot([A.alloc("mv", [128, 2], F32) for _ in range(2)])
    pf = Rot(k.pf)
    pbr = Rot(k.pb)
    xsrc = k.x_d

    def load(g):
        tsl = slice(g * 512, (g + 1) * 512)
        bs = [r.next() for r in ins]
        for b, d in zip(bs, (k.yaT_d, k.ybT_d, k.gaT_d, k.gbT_d)):
            S.dma(b[:], d[:, tsl].rearrange("(k p) t -> p k t", p=128), writes=[b])
        return bs

    nxt = load(0)
    dq = Defer()
    for g in range(NG):
        ya, yb, ga, gb = nxt
        if g + 1 < NG:
            nxt = load(g + 1)
        xis = []
        for j in range(4):
            xi = xin.next()
            r0 = g * 512 + j * 128
            S.dma(xi[:], xsrc[r0:r0 + 128, :], writes=[xi])
            xis.append(xi)
        m = mT.next()
        for oc in range(8):
            pa, pb_ = pf.next(), pf.next()
            for kk in range(8):
                MM(pa[:, 0:512], wa[:, kk, oc * 128:(oc + 1) * 128], ya[:, kk, :], kk == 0, kk == 7, [wa, ya], [pa],
                   sig=(kk == 7))
            for kk in range(8):
                MM(pb_[:, 0:512], wb[:, kk, oc * 128:(oc + 1) * 128], yb[:, kk, :], kk == 0, kk == 7, [wb, yb], [pb_],
                   sig=(kk == 7))
            a_, b_ = t1.next(), t2.next()
            TT("dve", a_[:], pa[:, 0:512], ga[:, oc, :], ALU.mult, [pa, ga], [a_])
            TT("dve", b_[:], pb_[:, 0:512], gb[:, oc, :], ALU.mult, [pb_, gb], [b_])
            TT("dve", m[:, oc, :], a_[:], b_[:], ALU.add, [a_, b_], [m])
        xt_ = xT.next()
        for j in range(4):
            z_ = z.next()
            for hf in range(2):
                ps = pf.next()
                for kk in range(8):
                    MM(ps[:, 0:512], m[:, kk, j * 128:(j + 1) * 128], wo[:, kk, hf * 512:(hf + 1) * 512], kk == 0,
                       kk == 7, [m, wo], [ps], sig=(kk == 7))
                if hf == 1:
                    dq.flush(keep=0)
                STT("dve", z_[:, hf * 512:(hf + 1) * 512], xis[j][:, hf * 512:(hf + 1) * 512], ALPHA, ps[:, 0:512],
                    ALU.mult, ALU.add, [xis[j], ps], [z_])
            x1_ = x1.next()
            layer_norm_tile(k, z_, lnp, 0, x1_, st.next(), mv.next())
            r0 = g * 512 + j * 128
            S.dma(k.x1_d[r0:r0 + 128, :], x1_[:], reads=[x1_], q="pool")
            xb_ = x1b.next()
            CP("act", xb_[:], x1_[:], [x1_], [xb_])

            def fin(xb_=xb_, xt_=xt_, j=j):
                pb = pbr.next()
                for c in range(8):
                    TR(pb[:, c * 128:(c + 1) * 128], xb_[:, c * 128:(c + 1) * 128], k.identb[:], [xb_, k.identb],
                       [pb], sig=(c == 7))
                CP("act", xt_[:, :, j * 128:(j + 1) * 128], pb[:, 0:1024].rearrange("p (c t) -> p c t", t=128), [pb],
                   [xt_])
            dq.push(fin)

        def store(xt_=xt_, g=g):
            S.dma(k.x1T_d[:, g * 512:(g + 1) * 512].rearrange("(k p) t -> p k t", p=128), xt_[:], reads=[xt_],
                  q="pool")
        dq.push(store)
    dq.flush()


def phase4b(k, l):
    S, A, T, NG = k.S, k.A, k.T, k.NG
    MM, TR, ACT, TT, TS, STT, CP, MS = k.MM, k.TR, k.ACT, k.TT, k.TS, k.STT, k.CP, k.MS
    wu = A.alloc("wu", [128, 8, 2 * DFF], BF16)
    for c4 in range(4):
        c0 = c4 * 1408
        S.dma(wu[:, :, c0:c0 + 1408], k.w_up[l, :, c0:c0 + 1408].rearrange("(k p) n -> p k n", p=128), writes=[wu],
              q="pool")
    hal = A.alloc("hal", [128, 44, 2], F32)
    xTi = Rot([A.alloc("xTi", [128, 8, 512], BF16) for _ in range(2)])
    cb = Rot([A.alloc("cb", [128, 514], F32) for _ in range(4)])
    acc = Rot([A.alloc("acc", [128, 512], F32) for _ in range(4)])
    sg = Rot([A.alloc("sg", [128, 512], F32) for _ in range(2)])
    ho = Rot([A.alloc("ho", [128, 512], BF16) for _ in range(4)])
    pf = Rot(k.pf)

    def load(g):
        a = xTi.next()
        S.dma(a[:], k.x1T_d[:, g * 512:(g + 1) * 512].rearrange("(k p) t -> p k t", p=128), writes=[a])
        return a

    def conv(ps, fcol, g, eng):
        c, a = cb.next(), acc.next()
        CP("act", c[:, 2:514], ps[:, 0:512], [ps], [c])
        if g == 0:
            MS("pool", c[:, 0:2], 0.0, [c])
        else:
            CP("pool", c[:, 0:2], hal[:, fcol, :], [hal], [c])
        ci = l * 132 + fcol * 3
        ACT(a[:], ps[:, 0:512], AF.Identity, [ps, k.cff], [a], scale=k.cff[:, ci + 2:ci + 3])
        for j in (1, 0):
            STT(eng, a[:], c[:, j:j + 512], k.cff[:, ci + j:ci + j + 1], a[:], ALU.mult, ALU.add, [c, a, k.cff], [a])
        CP("pool", hal[:, fcol, :], c[:, 512:514], [c], [hal])
        return a

    nxt = load(0)
    for g in range(NG):
        xt_in = nxt
        if g + 1 < NG:
            nxt = load(g + 1)
        for fc in range(22):
            pg, pu = pf.next(), pf.next()
            for kk in range(8):
                MM(pg[:, 0:512], wu[:, kk, fc * 128:(fc + 1) * 128], xt_in[:, kk, :], kk == 0, kk == 7, [wu, xt_in],
                   [pg], sig=(kk == 7))
            for kk in range(8):
                MM(pu[:, 0:512], wu[:, kk, DFF + fc * 128:DFF + (fc + 1) * 128], xt_in[:, kk, :], kk == 0, kk == 7,
                   [wu, xt_in], [pu], sig=(kk == 7))
            ag = conv(pg, fc, g, "dve")
            au = conv(pu, 22 + fc, g, "dve")
            s_ = sg.next()
            ACT(s_[:], ag[:], AF.Silu, [ag], [s_])
            h_ = ho.next()
            TT("dve", h_[:], s_[:], au[:], ALU.mult, [s_, au], [h_])
            S.dma(k.hT_d[fc * 128:(fc + 1) * 128, g * 512:(g + 1) * 512], h_[:], reads=[h_], q="sp")


def phase4c(k, l):
    S, A, T, NG = k.S, k.A, k.T, k.NG
    MM, TR, ACT, TT, TS, STT, CP, MS = k.MM, k.TR, k.ACT, k.TT, k.TS, k.STT, k.CP, k.MS
    last = (l == k.nl - 1)
    wd = A.alloc("wd", [128, 22, 1024], BF16)
    lnp = A.alloc("lnp", [128, 2048], F32)
    S.dma(wd[:], k.w_down[l].rearrange("(k p) n -> p k n", p=128), writes=[wd], q="pool")
    S.dma(lnp[:], k.c_ln[:, (l * 4 + 2) * 1024:(l * 4 + 4) * 1024], writes=[lnp])
    hTi = Rot([A.alloc("hTi", [128, 22, 512], BF16) for _ in range(2)])
    xin = Rot([A.alloc("xin", [128, 4, 1024], F32) for _ in range(2)])
    z = Rot([A.alloc("z", [128, 1024], F32) for _ in range(2)])
    x2 = Rot([A.alloc("x2", [128, 1024], F32) for _ in range(2)])
    x2b = Rot([A.alloc("x2b", [128, 1024], BF16) for _ in range(3)])
    xT = Rot([A.alloc("x2T", [128, 8, 512], BF16) for _ in range(2)])
    st = Rot([A.alloc("st", [128, 12], F32) for _ in range(2)])
    mv = Rot([A.alloc("mv", [128, 2], F32) for _ in range(2)])
    pf = Rot(k.pf)
    pbr = Rot(k.pb)

    def load(g):
        tsl = slice(g * 512, (g + 1) * 512)
        a, b = hTi.next(), xin.next()
        S.dma(a[:], k.hT_d[:, tsl].rearrange("(k p) t -> p k t", p=128), writes=[a])
        S.dma(b[:], k.x1_d[tsl, :].rearrange("(j p) d -> p j d", p=128), writes=[b])
        return a, b

    nxt = load(0)
    dq = Defer()
    for g in range(NG):
        hT_, xi = nxt
        if g + 1 < NG:
            nxt = load(g + 1)
        xt_ = xT.next()
        for j in range(4):
            z_ = z.next()
            for hf in range(2):
                ps = pf.next()
                for kk in range(22):
                    MM(ps[:, 0:512], hT_[:, kk, j * 128:(j + 1) * 128], wd[:, kk, hf * 512:(hf + 1) * 512], kk == 0,
                       kk == 21, [hT_, wd], [ps], sig=(kk == 21))
                if hf == 1:
                    dq.flush(keep=0)
                STT("dve", z_[:, hf * 512:(hf + 1) * 512], xi[:, j, hf * 512:(hf + 1) * 512], ALPHA, ps[:, 0:512],
                    ALU.mult, ALU.add, [xi, ps], [z_])
            x2_ = x2.next()
            layer_norm_tile(k, z_, lnp, 0, x2_, st.next(), mv.next())
            r0 = g * 512 + j * 128
            if last:
                S.dma(k.y_out[r0:r0 + 128, :], x2_[:], reads=[x2_], q="pool")
            else:
                S.dma(k.x_d[r0:r0 + 128, :], x2_[:], reads=[x2_], q="pool")
                xb_ = x2b.next()
                CP("act", xb_[:], x2_[:], [x2_], [xb_])

                def fin(xb_=xb_, xt_=xt_, j=j):
                    pb = pbr.next()
                    for c in range(8):
                        TR(pb[:, c * 128:(c + 1) * 128], xb_[:, c * 128:(c + 1) * 128], k.identb[:], [xb_, k.identb],
                           [pb], sig=(c == 7))
                    CP("act", xt_[:, :, j * 128:(j + 1) * 128], pb[:, 0:1024].rearrange("p (c t) -> p c t", t=128),
                       [pb], [xt_])
                dq.push(fin)
        if not last:
            def store(xt_=xt_, g=g):
                S.dma(k.xT_d[:, g * 512:(g + 1) * 512].rearrange("(k p) t -> p k t", p=128), xt_[:], reads=[xt_],
                      q="pool")
            dq.push(store)
    dq.flush()


def t5_bucket_np(rel):
    n = np.maximum(rel, 0)
    max_exact = NBK // 2
    nf = np.maximum(n, max_exact).astype(np.float32)
    large = max_exact + (np.log(nf / np.float32(max_exact)) / np.float32(math.log(128 / max_exact))
                         * np.float32(NBK - max_exact)).astype(np.int32)
    large = np.minimum(large, NBK - 1)
    return np.where(n < max_exact, n, large)


def host_consts(inp):
    f32 = np.float32
    b_in = np.asarray(inp["b_in"], f32)
    c = {}
    bfm = np.zeros((128, NL * 64), f32)
    for l in range(NL):
        for name, c0 in FM_COL0.items():
            for cc in range(8):
                bfm[:, l * 64 + FM_IDX0[name] + cc] = b_in[l, c0 + cc * 128:c0 + (cc + 1) * 128]
    c["c_bfm"] = bfm
    c["c_bif"] = np.ascontiguousarray(b_in[:, 4096:4104].T)
    btm = np.zeros((128, NL * 3072), f32)
    for l in range(NL):
        for name, c0 in TM_COL0.items():
            btm[:, l * 3072 + TM_IDX0[name]:l * 3072 + TM_IDX0[name] + 1024] = b_in[l, c0:c0 + 1024][None, :]
    c["c_btm"] = btm
    cq = np.asarray(inp["conv_qk"], f32)
    c["c_cqk"] = np.ascontiguousarray(cq.reshape(NL, 4, 16, 128).transpose(3, 0, 2, 1).reshape(128, NL * 64))
    c["c_gain"] = np.ascontiguousarray(np.broadcast_to(np.asarray(inp["mlstm_norm"], f32).reshape(1, NL * 1024),
                                                       (128, NL * 1024)))
    ln = np.stack([np.asarray(inp[n], f32) for n in ("ln1_g", "ln1_b", "ln2_g", "ln2_b")], axis=1)
    c["c_ln"] = np.ascontiguousarray(np.broadcast_to(ln.reshape(1, NL * 4 * 1024), (128, NL * 4 * 1024)))
    cf = np.asarray(inp["conv_ffn"], f32)
    c["c_cff"] = np.ascontiguousarray(cf.reshape(NL, 3, 44, 128).transpose(3, 0, 2, 1).reshape(128, NL * 132))
    rb = np.asarray(inp["rel_bias"], f32)
    c["c_rb31"] = np.ascontiguousarray(np.broadcast_to(rb[31][None, :], (128, 8)))
    key = np.arange(128)[:, None]
    q = np.arange(128)[None, :]
    bn = np.zeros((128, 8, 2, 128), f32)
    for typ, off in ((0, 0), (1, 128)):
        bk = t5_bucket_np(q + off - key)
        bn[:, :, typ, :] = rb[bk].transpose(0, 2, 1)
    c["c_bnear"] = bn.reshape(128, 8 * 2 * 128)
    c["c_cmask"] = np.where(q >= key, 0.0, -BIG).astype(f32)
    mn = np.zeros((32, 32), f32)
    for own in range(32):
        mn[own, own] = 1e30
        mn[own, own + 1:] = -1e30
    c["c_maskneg"] = np.ascontiguousarray(np.broadcast_to(mn.reshape(1, 1024), (128, 1024)))
    E = np.zeros((32, 32, 128), f32)
    for n in range(32):
        E[n, n, :] = 1.0
    c["c_E"] = E.reshape(32, 32 * 128).astype(ml_dtypes.bfloat16)
    c["c_identb"] = np.eye(128, dtype=f32).astype(ml_dtypes.bfloat16)
    c["c_identf"] = np.eye(128, dtype=f32)
    c["c_tri"] = (q >= key).astype(f32)
    oh = np.zeros((4, 4, 128), f32)
    for h in range(4):
        oh[h, h, :] = 1.0
    c["c_oh4"] = oh.reshape(4, 512)
    return c


_WNAMES = ("w_in", "w_branch_a", "w_branch_b", "w_out", "w_up", "w_down")


def make_in_maps(inp, n_cores=8, names=None):
    x = np.asarray(inp["x"], np.float32)
    B = x.shape[0]
    c = host_consts(inp)
    shared = {n: np.ascontiguousarray(np.asarray(inp[n], np.float32)) for n in _WNAMES}
    shared.update(c)
    maps = []
    for core in range(n_cores):
        b = core % B
        m = dict(shared)
        m["x"] = np.ascontiguousarray(x[b])
        m["xT"] = np.ascontiguousarray(x[b].T)
        if names is not None:
            m = {n: v for n, v in m.items() if n in names}
        maps.append(m)
    return maps


def kernel(**inputs):
    x = np.asarray(inputs["x"], np.float32)
    B, T, _ = x.shape
    nc, _ = build(T)
    maps = make_in_maps(inputs)
    res = run_bass_kernel_spmd(nc, maps, core_ids=list(range(8)))
    out = np.stack([np.asarray(res.results[b]["y"], np.float32) for b in range(B)], axis=0)
    return out
```
